# Optimizing a Trainium2 kernel written in Bass

```python
import math
import jax
import jax.numpy as jnp
from jax import lax
import numpy as np


D_MODEL = 1024
BATCH = 1
SEQ = 16384
DEPTH = 2

GRID_W = 64
CTX_LEN = 256
ROPE_BASE = 10000.0
CONV_K = 5
CHUNK = 64
Q_BLOCK = 128
EPS = 1e-6
N_BRANCH = 4
BRANCH_W = D_MODEL // 2

DIFF_H = 4
DIFF_HD = 64
DIFF_QK = DIFF_H * 2 * DIFF_HD
DIFF_V = DIFF_H * 2 * DIFF_HD
ML_H = 4
ML_DK = 64
ML_DV = 128
ML_QK = 2 * ML_H * ML_DK
ML_V = ML_H * ML_DV
ML_GATES = 4 * ML_H
MLA_H = 4
MLA_Q_RANK = 384
MLA_KV_RANK = 256
MLA_NOPE = 64
MLA_ROPE = 32
MLA_V = 128
SSD_H = 8
SSD_P = 64
SSD_G = 2
SSD_N = 64
D_INNER = SSD_H * SSD_P
SSD_XBC = D_INNER + 2 * SSD_G * SSD_N
SSD_DT = 2 * SSD_H
D_FF = 4 * D_MODEL
DN_ALPHA = (2 * DEPTH) ** 0.25
DN_BETA = (8 * DEPTH) ** -0.25

IN_SPLITS = (DIFF_QK, DIFF_QK, DIFF_V,
             ML_QK, ML_V, ML_V, ML_GATES,
             MLA_Q_RANK, MLA_KV_RANK, MLA_ROPE,
             D_INNER, SSD_XBC, SSD_DT)
IN_COLS = sum(IN_SPLITS)

kernel_name = 'hybrid_gated_diffusion_block'


def layer_norm(x, g=None, b=None):
    xf = x.astype(jnp.float32)
    mu = jnp.mean(xf, -1, keepdims=True)
    var = jnp.mean(jnp.square(xf - mu), -1, keepdims=True)
    y = (xf - mu) * lax.rsqrt(var + EPS)
    if g is not None:
        y = y * g + b
    return y.astype(x.dtype)


def rms_norm(x, w):
    xf = x.astype(jnp.float32)
    y = xf * lax.rsqrt(jnp.mean(jnp.square(xf), -1, keepdims=True) + EPS) * w
    return y.astype(x.dtype)


def modulate(x, shift, scale):
    return layer_norm(x) * (1.0 + scale) + shift


def rope_axis(x, pos):
    d = x.shape[-1]
    half = d // 2
    inv = ROPE_BASE ** (-jnp.arange(half, dtype=jnp.float32) * 2.0 / d)
    ang = pos.astype(jnp.float32)[:, None] * inv[None, :]
    bshape = (pos.shape[0],) + (1,) * (x.ndim - 3) + (half,)
    cos = jnp.cos(ang).reshape(bshape)
    sin = jnp.sin(ang).reshape(bshape)
    xf = x.astype(jnp.float32)
    x1, x2 = xf[..., :half], xf[..., half:]
    return jnp.concatenate([x1 * cos - x2 * sin, x2 * cos + x1 * sin], -1).astype(x.dtype)


def rope_2d(x, row, col):
    half = x.shape[-1] // 2
    return jnp.concatenate([rope_axis(x[..., :half], row), rope_axis(x[..., half:], col)], -1)


def dwconv(x, w, b):
    ch = x.shape[-1]
    y = lax.conv_general_dilated(x, w[:, None, :].astype(x.dtype), window_strides=(1,),
                                 padding=[(CONV_K // 2, CONV_K // 2)],
                                 dimension_numbers=('NWC', 'WIO', 'NWC'), feature_group_count=ch)
    return y + b


def to_chunks(a):
    bsz, t = a.shape[:2]
    return jnp.moveaxis(a.reshape((bsz, t // CHUNK, CHUNK) + a.shape[2:]), 1, 0)


def from_chunks(a):
    a = jnp.moveaxis(a, 0, 1)
    return a.reshape((a.shape[0], a.shape[1] * a.shape[2]) + a.shape[3:])


def over_query_blocks(fn, *qs):
    bsz, t = qs[0].shape[:2]
    nb = t // Q_BLOCK
    blocks = tuple(jnp.moveaxis(q.reshape((bsz, nb, Q_BLOCK) + q.shape[2:]), 1, 0) for q in qs)
    out = lax.map(lambda a: fn(*a), blocks)
    out = jnp.moveaxis(out, 0, 1)
    return out.reshape((bsz, t) + out.shape[3:])


def softmax_attend(q, k, v, scale):
    s = jnp.einsum('bqhd,bkhd->bhqk', q, k).astype(jnp.float32) * scale
    p = jax.nn.softmax(s, axis=-1).astype(v.dtype)
    o = jnp.einsum('bhqk,bkhe->bqhe', p, v)
    return o.reshape(o.shape[:2] + (-1,))


def flip_t(a):
    return jnp.flip(a, axis=1)


def bidirectional_scan(scan_fn, ctx_dirs, lat_dirs, state0):
    y_ctx, y_lat = 0.0, 0.0
    for d in range(2):
        ca, la = ctx_dirs[d], lat_dirs[d]
        if d == 1:
            ca = tuple(flip_t(a) for a in ca)
            la = tuple(flip_t(a) for a in la)
        yc, st = scan_fn(*ca, state0)
        yl, _ = scan_fn(*la, st)
        if d == 1:
            yc, yl = flip_t(yc), flip_t(yl)
        y_ctx = y_ctx + yc
        y_lat = y_lat + yl
    return y_ctx, y_lat


def mlstm_chunk_scan(q, k, v, ig, lf, state):
    dtype = v.dtype
    causal = jnp.tril(jnp.ones((CHUNK, CHUNK), bool))[None, :, :, None]

    def body(carry, chunk):
        cmat, nvec, m = carry
        qc, kc, vc, ic, fc = chunk
        g = jnp.cumsum(fc, axis=1)
        logw = jnp.where(causal, g[:, :, None] - g[:, None] + ic[:, None], -jnp.inf)
        log_inter = g + m[:, None]
        m_j = jnp.maximum(log_inter, jnp.max(logw, axis=2))
        w = jnp.exp(logw - m_j[:, :, None]) * jnp.einsum('bjhd,bshd->bjsh', qc, kc)
        a_inter = jnp.exp(log_inter - m_j)
        num = jnp.einsum('bjsh,bshe->bjhe', w, vc) + a_inter[..., None] * jnp.einsum('bjhd,bhde->bjhe', qc, cmat)
        den = jnp.sum(w, axis=2) + a_inter * jnp.einsum('bjhd,bhd->bjh', qc, nvec)
        h = num / jnp.maximum(jnp.abs(den), jnp.exp(-m_j))[..., None]
        g_end = g[:, -1]
        log_s = g_end[:, None] - g + ic
        m_new = jnp.maximum(g_end + m, jnp.max(log_s, axis=1))
        ws = jnp.exp(log_s - m_new[:, None])
        decay = jnp.exp(g_end + m - m_new)
        cmat = decay[..., None, None] * cmat + jnp.einsum('bsh,bshd,bshe->bhde', ws, kc, vc)
        nvec = decay[..., None] * nvec + jnp.einsum('bsh,bshd->bhd', ws, kc)
        return (cmat, nvec, m_new), h

    chunks = tuple(to_chunks(a.astype(jnp.float32)) for a in (q, k, v, ig, lf))
    state, h = lax.scan(body, state, chunks)
    return from_chunks(h).astype(dtype), state


def ssd_chunk_scan(x, dt, a, bm, cm, h0):
    dtype = x.dtype
    causal = jnp.tril(jnp.ones((CHUNK, CHUNK), bool))[None, :, :, None]

    def body(h, chunk):
        xc, dtc, ac, bc, cc = chunk
        s = jnp.cumsum(ac, axis=1)
        decay = jnp.exp(jnp.where(causal, s[:, :, None] - s[:, None], -jnp.inf))
        scores = decay * jnp.einsum('bjhn,bihn->bjih', cc, bc)
        xdt = xc * dtc[..., None]
        y = jnp.einsum('bjih,bihp->bjhp', scores, xdt) + jnp.exp(s)[..., None] * jnp.einsum('bjhn,bhpn->bjhp', cc, h)
        s_end = s[:, -1]
        h = jnp.exp(s_end)[..., None, None] * h + jnp.einsum('bih,bihp,bihn->bhpn', jnp.exp(s_end[:, None] - s), xdt, bc)
        return h, y

    chunks = tuple(to_chunks(t.astype(jnp.float32)) for t in (x, dt, a, bm, cm))
    h, y = lax.scan(body, h0, chunks)
    return from_chunks(y).astype(dtype), h


def diff_attention(lat, ctx, lam, subln_w, lam_init, row, col, ctx_out):
    def heads(q, k, v, pos):
        bsz, t = q.shape[:2]
        q = q.reshape(bsz, t, DIFF_H, 2, DIFF_HD)
        k = k.reshape(bsz, t, DIFF_H, 2, DIFF_HD)
        if pos is not None:
            q = rope_2d(q, *pos)
            k = rope_2d(k, *pos)
        return q, k, v.reshape(bsz, t, DIFF_H, 2 * DIFF_HD)

    lam = lam.astype(jnp.float32)
    lam_full = jnp.exp(jnp.sum(lam[0] * lam[1])) - jnp.exp(jnp.sum(lam[2] * lam[3])) + lam_init
    scale = DIFF_HD ** -0.5

    def attend(q, k, v):
        s = jnp.einsum('bqhcd,bkhcd->bhcqk', q, k).astype(jnp.float32) * scale
        p = jax.nn.softmax(s, axis=-1)
        amap = (p[:, :, 0] - lam_full * p[:, :, 1]).astype(v.dtype)
        o = jnp.einsum('bhqk,bkhe->bqhe', amap, v)
        o = rms_norm(o, subln_w) * (1.0 - lam_init)
        return o.reshape(o.shape[:2] + (-1,))

    q_l, k_l, v_l = heads(*lat, (row, col))
    q_c, k_c, v_c = heads(*ctx, None)
    k_all = jnp.concatenate([k_l, k_c], 1)
    v_all = jnp.concatenate([v_l, v_c], 1)
    y_lat = over_query_blocks(lambda qb: attend(qb, k_all, v_all), q_l)
    y_ctx = attend(q_c, k_c, v_c) if ctx_out else None
    return y_lat, y_ctx


def mla_mixer(lat, ctx, q_norm, kv_norm, w_uq, w_ukv, row, col, ctx_out):
    def qkv(cq, ckv, kr, pos):
        bsz, t = cq.shape[:2]
        q = (rms_norm(cq, q_norm) @ w_uq).reshape(bsz, t, MLA_H, MLA_NOPE + MLA_ROPE)
        kv = (rms_norm(ckv, kv_norm) @ w_ukv).reshape(bsz, t, MLA_H, MLA_NOPE + MLA_V)
        q_nope, q_rope = q[..., :MLA_NOPE], q[..., MLA_NOPE:]
        k_nope, v = kv[..., :MLA_NOPE], kv[..., MLA_NOPE:]
        kr = kr[:, :, None, :]
        if pos is not None:
            q_rope = rope_2d(q_rope, *pos)
            kr = rope_2d(kr, *pos)
        q = jnp.concatenate([q_nope, q_rope], -1)
        k = jnp.concatenate([k_nope, jnp.broadcast_to(kr, (bsz, t, MLA_H, MLA_ROPE))], -1)
        return q, k, v

    q_l, k_l, v_l = qkv(*lat, (row, col))
    q_c, k_c, v_c = qkv(*ctx, None)
    scale = (MLA_NOPE + MLA_ROPE) ** -0.5
    k_all = jnp.concatenate([k_l, k_c], 1)
    v_all = jnp.concatenate([v_l, v_c], 1)
    y_lat = over_query_blocks(lambda qb: softmax_attend(qb, k_all, v_all, scale), q_l)
    y_ctx = softmax_attend(q_c, k_c, v_c, scale) if ctx_out else None
    return y_lat, y_ctx


def mlstm_mixer(lat, ctx, conv_w, conv_b, gate_b, norm_w, ctx_out):
    def directions(qk, v, gates):
        bsz, t = qk.shape[:2]
        qk = jax.nn.silu(dwconv(qk, conv_w, conv_b))
        q = qk[..., :ML_H * ML_DK].reshape(bsz, t, ML_H, ML_DK)
        k = qk[..., ML_H * ML_DK:].reshape(bsz, t, ML_H, ML_DK) * (ML_DK ** -0.5)
        v = v.reshape(bsz, t, ML_H, ML_DV)
        gt = gates.reshape(bsz, t, 4, ML_H).astype(jnp.float32) + gate_b
        return [(q, k, v, gt[:, :, 2 * d], jax.nn.log_sigmoid(gt[:, :, 2 * d + 1])) for d in range(2)]

    def finish(h, o):
        bsz, t = h.shape[:2]
        h = rms_norm(h, norm_w.reshape(ML_H, ML_DV)).reshape(bsz, t, ML_V)
        return h * jax.nn.sigmoid(o)

    bsz = lat[0].shape[0]
    state0 = (jnp.zeros((bsz, ML_H, ML_DK, ML_DV), jnp.float32),
              jnp.zeros((bsz, ML_H, ML_DK), jnp.float32),
              jnp.zeros((bsz, ML_H), jnp.float32))
    h_ctx, h_lat = bidirectional_scan(mlstm_chunk_scan, directions(ctx[0], ctx[1], ctx[3]),
                                      directions(lat[0], lat[1], lat[3]), state0)
    y_lat = finish(h_lat, lat[2])
    y_ctx = finish(h_ctx, ctx[2]) if ctx_out else None
    return y_lat, y_ctx


def ssd_mixer(lat, ctx, conv_w, conv_b, dt_bias, a_log, d_skip, norm_w, ctx_out):
    a_mat = -jnp.exp(a_log.astype(jnp.float32))
    rep = SSD_H // SSD_G

    def prep(xbc, dt):
        bsz, t = xbc.shape[:2]
        xbc = jax.nn.silu(dwconv(xbc, conv_w, conv_b))
        xs = xbc[..., :D_INNER].reshape(bsz, t, SSD_H, SSD_P)
        bm = jnp.repeat(xbc[..., D_INNER:D_INNER + SSD_G * SSD_N].reshape(bsz, t, SSD_G, SSD_N), rep, axis=2)
        cm = jnp.repeat(xbc[..., D_INNER + SSD_G * SSD_N:].reshape(bsz, t, SSD_G, SSD_N), rep, axis=2)
        dt = jax.nn.softplus(dt.reshape(bsz, t, 2, SSD_H).astype(jnp.float32) + dt_bias)
        dirs = [(xs, dt[:, :, d], dt[:, :, d] * a_mat[d], bm, cm) for d in range(2)]
        return xs, dirs

    def finish(y, xs, z):
        bsz, t = y.shape[:2]
        y = (y + d_skip[:, None] * xs).reshape(bsz, t, D_INNER) * jax.nn.silu(z)
        y = rms_norm(y.reshape(bsz, t, SSD_G, D_INNER // SSD_G), norm_w.reshape(SSD_G, D_INNER // SSD_G))
        return y.reshape(bsz, t, D_INNER)

    xs_l, dirs_l = prep(lat[1], lat[2])
    xs_c, dirs_c = prep(ctx[1], ctx[2])
    h0 = jnp.zeros((lat[0].shape[0], SSD_H, SSD_P, SSD_N), jnp.float32)
    y_ctx, y_lat = bidirectional_scan(ssd_chunk_scan, dirs_c, dirs_l, h0)
    out_lat = finish(y_lat, xs_l, lat[0])
    out_ctx = finish(y_ctx, xs_c, ctx[0]) if ctx_out else None
    return out_lat, out_ctx


def gated_merge(h, branches, w_gate, b_gate, w_branch, w_o):
    y = 0.0
    for k, br in enumerate(branches):
        y = y + jax.nn.sigmoid(h @ w_gate[k] + b_gate[k]) * (br @ w_branch[k])
    return y @ w_o


def sq_relu_mlp(h, w_up, b_up, w_down, b_down):
    u = jax.nn.relu(h @ w_up + b_up)
    return (u * u) @ w_down + b_down


def setup_inputs(seed: int = 0) -> dict:
    key = jax.random.key(seed)
    ks = iter(jax.random.split(key, 48))
    f32 = jnp.float32
    L, D = DEPTH, D_MODEL

    def nrm(shape, std):
        return std * jax.random.normal(next(ks), shape, f32)

    def gain(shape):
        return 1.0 + nrm(shape, 0.02)

    fb = jnp.linspace(3.0, 6.0, ML_H, dtype=f32)
    zb = jnp.zeros((ML_H,), f32)
    dt0 = jnp.exp(jax.random.uniform(next(ks), (L, 2, SSD_H), f32, math.log(1e-3), math.log(1e-1)))
    return {
        'x': nrm((BATCH, SEQ, D), 1.0),
        'c': nrm((BATCH, D), 1.0),
        'ctx': nrm((BATCH, CTX_LEN, D), 1.0),
        'c_ctx': nrm((D,), 1.0),
        'w_mod': nrm((L, D, 6 * D), 0.5 * D ** -0.5),
        'b_mod': nrm((L, 6 * D), 0.01),
        'w_in': nrm((L, D, IN_COLS), D ** -0.5),
        'diff_lambda': nrm((L, 4, DIFF_HD), 0.1),
        'diff_subln': gain((L, 2 * DIFF_HD)),
        'ml_conv_w': nrm((L, CONV_K, ML_QK), CONV_K ** -0.5),
        'ml_conv_b': nrm((L, ML_QK), 0.02),
        'ml_gate_b': jnp.stack([zb, fb, zb, fb])[None] + nrm((L, 4, ML_H), 0.1),
        'ml_norm': gain((L, ML_V)),
        'mla_q_norm': gain((L, MLA_Q_RANK)),
        'mla_kv_norm': gain((L, MLA_KV_RANK)),
        'mla_w_uq': nrm((L, MLA_Q_RANK, MLA_H * (MLA_NOPE + MLA_ROPE)), MLA_Q_RANK ** -0.5),
        'mla_w_ukv': nrm((L, MLA_KV_RANK, MLA_H * (MLA_NOPE + MLA_V)), MLA_KV_RANK ** -0.5),
        'ssd_conv_w': nrm((L, CONV_K, SSD_XBC), CONV_K ** -0.5),
        'ssd_conv_b': nrm((L, SSD_XBC), 0.02),
        'ssd_dt_bias': dt0 + jnp.log(-jnp.expm1(-dt0)),
        'ssd_a_log': jnp.log(jax.random.uniform(next(ks), (L, 2, SSD_H), f32, 1.0, 16.0)),
        'ssd_d': gain((L, SSD_H)),
        'ssd_norm': gain((L, D_INNER)),
        'w_gate': nrm((L, N_BRANCH, D, D), D ** -0.5),
        'b_gate': nrm((L, N_BRANCH, D), 0.01),
        'w_branch': nrm((L, N_BRANCH, BRANCH_W, D), BRANCH_W ** -0.5),
        'w_o': nrm((L, D, D), DN_BETA * D ** -0.5),
        'ln1_g': gain((L, D)),
        'ln1_b': nrm((L, D), 0.01),
        'w_up': nrm((L, D, D_FF), D ** -0.5),
        'b_up': nrm((L, D_FF), 0.01),
        'w_down': nrm((L, D_FF, D), DN_BETA * D_FF ** -0.5),
        'b_down': nrm((L, D), 0.01),
        'ln2_g': gain((L, D)),
        'ln2_b': nrm((L, D), 0.01),
    }


def reference(x, c, ctx, c_ctx, w_mod, b_mod, w_in, diff_lambda, diff_subln, ml_conv_w, ml_conv_b,
              ml_gate_b, ml_norm, mla_q_norm, mla_kv_norm, mla_w_uq, mla_w_ukv, ssd_conv_w, ssd_conv_b,
              ssd_dt_bias, ssd_a_log, ssd_d, ssd_norm, w_gate, b_gate, w_branch, w_o, ln1_g, ln1_b,
              w_up, b_up, w_down, b_down, ln2_g, ln2_b):
    t = x.shape[1]
    rows = t // GRID_W
    row = jnp.repeat(jnp.arange(rows, dtype=jnp.int32), GRID_W)
    col = jnp.arange(t, dtype=jnp.int32) % GRID_W
    split_at = [int(s) for s in np.cumsum(IN_SPLITS)[:-1]]
    xc = ctx
    for l in range(DEPTH):
        ctx_out = l < DEPTH - 1
        lam_init = 0.8 - 0.6 * math.exp(-0.3 * l)
        mod_l = jax.nn.silu(c) @ w_mod[l] + b_mod[l]
        mod_c = jax.nn.silu(c_ctx) @ w_mod[l] + b_mod[l]
        sh_a, sc_a, g_a, sh_m, sc_m, g_m = jnp.split(mod_l[:, None, :], 6, axis=-1)
        csh_a, csc_a, cg_a, csh_m, csc_m, cg_m = jnp.split(mod_c, 6, axis=-1)

        h_l = modulate(x, sh_a, sc_a)
        h_c = modulate(xc, csh_a, csc_a)
        p_l = jnp.split(h_l @ w_in[l], split_at, axis=-1)
        p_c = jnp.split(h_c @ w_in[l], split_at, axis=-1)
        a_l, a_c = diff_attention(tuple(p_l[0:3]), tuple(p_c[0:3]), diff_lambda[l], diff_subln[l],
                                  lam_init, row, col, ctx_out)
        b_l, b_c = mlstm_mixer(tuple(p_l[3:7]), tuple(p_c[3:7]), ml_conv_w[l], ml_conv_b[l],
                               ml_gate_b[l], ml_norm[l], ctx_out)
        m_l, m_c = mla_mixer(tuple(p_l[7:10]), tuple(p_c[7:10]), mla_q_norm[l], mla_kv_norm[l],
                             mla_w_uq[l], mla_w_ukv[l], row, col, ctx_out)
        s_l, s_c = ssd_mixer(tuple(p_l[10:13]), tuple(p_c[10:13]), ssd_conv_w[l], ssd_conv_b[l],
                             ssd_dt_bias[l], ssd_a_log[l], ssd_d[l], ssd_norm[l], ctx_out)
        y_l = gated_merge(h_l, (a_l, b_l, m_l, s_l), w_gate[l], b_gate[l], w_branch[l], w_o[l])
        x = layer_norm(DN_ALPHA * x + g_a * y_l, ln1_g[l], ln1_b[l])
        f_l = sq_relu_mlp(modulate(x, sh_m, sc_m), w_up[l], b_up[l], w_down[l], b_down[l])
        x = layer_norm(DN_ALPHA * x + g_m * f_l, ln2_g[l], ln2_b[l])

        if ctx_out:
            y_c = gated_merge(h_c, (a_c, b_c, m_c, s_c), w_gate[l], b_gate[l], w_branch[l], w_o[l])
            xc = layer_norm(DN_ALPHA * xc + cg_a * y_c, ln1_g[l], ln1_b[l])
            f_c = sq_relu_mlp(modulate(xc, csh_m, csc_m), w_up[l], b_up[l], w_down[l], b_down[l])
            xc = layer_norm(DN_ALPHA * xc + cg_m * f_c, ln2_g[l], ln2_b[l])
    return x
```

```python
import math
import numpy as np
import ml_dtypes
import concourse.bass as bass
import concourse.mybir as mybir
from concourse.bass_utils import run_bass_kernel_spmd

F32 = mybir.dt.float32
BF16 = mybir.dt.bfloat16
AF = mybir.ActivationFunctionType
ALU = mybir.AluOpType
AX = mybir.AxisListType
NPBF = ml_dtypes.bfloat16

ENGS = ("pe", "act", "dve", "pool", "sp")


class Buf:
    def __init__(self, t, name, kind):
        self.t = t
        self.name = name
        self.kind = kind
        self.last_w = None
        self.reads = []
        self.dsem = None

    def __getitem__(self, idx):
        return self.t[idx]


class Prog:
    def __init__(self, nc, n_dma_sems=90):
        self.nc = nc
        self.ops = {e: [] for e in ENGS}
        self.n_dma_sems = n_dma_sems
        self.dma_sem_next = 0
        self.dma_sem_counts = [0] * n_dma_sems
        self.ctx = []
        self.uid = 0

    def sb(self, name, shape, dtype):
        self.uid += 1
        g = self.nc.sbuf_tensor("%s_%d" % (name, self.uid), list(shape), dtype)
        t = g.__enter__()
        self.ctx.append(g)
        return Buf(t, name, "sb")

    def ps(self, name, shape, dtype):
        self.uid += 1
        g = self.nc.psum_tensor("%s_%d" % (name, self.uid), list(shape), dtype)
        t = g.__enter__()
        self.ctx.append(g)
        return Buf(t, name, "ps")

    def dram(self, name, shape, dtype, kind="Internal"):
        t = self.nc.dram_tensor(name, list(shape), dtype, kind=kind)
        return Buf(t.ap(), name, "dram")

    def mark(self):
        return len(self.ctx)

    def barrier(self):
        toks = [("eng", e, len(self.ops[e]) - 1) for e in ENGS if self.ops[e]]
        toks += [("dma", s_, c) for s_, c in enumerate(self.dma_sem_counts) if c > 0]
        for e in ENGS:
            self.ops[e].append(dict(fn=(lambda en: en.nop()), deps=list(toks), dma=None, waited=False))

    def release(self, mark):
        self.barrier()
        while len(self.ctx) > mark:
            self.ctx.pop().__exit__(None, None, None)

    def _deps(self, reads, writes, eng=None):
        deps = []
        for b in reads:
            if b.last_w is not None:
                deps.append(b.last_w)
            if b.kind == "ps":
                deps.extend(r for r in b.reads if not (r[0] == "eng" and r[1] == eng))
        for b in writes:
            if b.last_w is not None:
                deps.append(b.last_w)
            deps.extend(b.reads)
        return deps

    def _commit(self, tok, reads, writes):
        for b in reads:
            b.reads.append(tok)
            if len(b.reads) > 64:
                b.reads = b.reads[-64:] if False else b.reads
        for b in writes:
            b.last_w = tok
            b.reads = []

    def op(self, eng, meth, *args, reads=(), writes=(), **kw):
        fn = (lambda e: getattr(e, meth)(*args, **kw))
        deps = self._deps(reads, writes, eng)
        idx = len(self.ops[eng])
        self.ops[eng].append(dict(fn=fn, deps=deps, dma=None, waited=False))
        self._commit(("eng", eng, idx), reads, writes)

    def dma(self, eng, out_ap, in_ap, reads=(), writes=(), **kw):
        deps = self._deps(reads, writes)
        owner = None
        for b in list(writes) + list(reads):
            if b.kind != "dram":
                owner = b
                break
        if owner is None:
            owner = (list(writes) + list(reads))[0]
        if owner.dsem is None:
            owner.dsem = self.dma_sem_next % self.n_dma_sems
            self.dma_sem_next += 1
        s = owner.dsem
        self.dma_sem_counts[s] += 16
        val = self.dma_sem_counts[s]
        self.ops[eng].append(dict(fn=(lambda e: e.dma_start(out=out_ap, in_=in_ap, **kw)),
                                  deps=deps, dma=(s, val), waited=False))
        self._commit(("dma", s, val), reads, writes)

    def emit(self):
        nc = self.nc
        ops = self.ops
        for e in ENGS:
            for rec in ops[e]:
                for d in rec["deps"]:
                    if d[0] == "eng" and not (d[1] == e and e == "pe"):
                        ops[d[1]][d[2]]["waited"] = True
        for e in ENGS:
            c = 0
            for rec in ops[e]:
                if rec["waited"] and rec["dma"] is None:
                    c += 1
                rec["cnt"] = c
        sem_ctx = []
        esem = {}
        for e in ENGS:
            g = nc.semaphore("es_" + e)
            esem[e] = g.__enter__()
            sem_ctx.append(g)
        dsem = []
        for i in range(min(self.n_dma_sems, max(1, self.dma_sem_next))):
            g = nc.semaphore("ds_%d" % i)
            dsem.append(g.__enter__())
            sem_ctx.append(g)
        counts = self.dma_sem_counts

        def emit_engine(e, eng):
            waited_e = {x: 0 for x in ENGS}
            waited_d = {}
            for rec in ops[e]:
                need_e = {}
                need_d = {}
                for d in rec["deps"]:
                    if d[0] == "eng":
                        if d[1] == e and e == "pe":
                            continue
                        v = ops[d[1]][d[2]]["cnt"]
                        if v > waited_e[d[1]]:
                            need_e[d[1]] = max(need_e.get(d[1], 0), v)
                    else:
                        s, v = d[1], d[2]
                        if v > waited_d.get(s, 0):
                            need_d[s] = max(need_d.get(s, 0), v)
                for pe_, v in need_e.items():
                    eng.wait_ge(esem[pe_], v)
                    waited_e[pe_] = v
                for s, v in need_d.items():
                    eng.wait_ge(dsem[s], v)
                    waited_d[s] = v
                ins = rec["fn"](eng)
                if rec["dma"] is not None:
                    ins.then_inc(dsem[rec["dma"][0]], 16)
                elif rec["waited"]:
                    ins.then_inc(esem[e], 1)
            if e == "sp":
                for s in range(len(dsem)):
                    if counts[s] > waited_d.get(s, 0):
                        eng.wait_ge(dsem[s], counts[s])

        with nc.Block() as block:
            @block.tensor
            def _(eng):
                emit_engine("pe", eng)

            @block.scalar
            def _(eng):
                emit_engine("act", eng)

            @block.vector
            def _(eng):
                emit_engine("dve", eng)

            @block.gpsimd
            def _(eng):
                emit_engine("pool", eng)

            @block.sync
            def _(eng):
                emit_engine("sp", eng)
        for g in reversed(sem_ctx):
            g.__exit__(None, None, None)
        while self.ctx:
            self.ctx.pop().__exit__(None, None, None)


class Rot:
    def __init__(self, bufs):
        self.bufs = bufs
        self.i = 0

    def next(self):
        b = self.bufs[self.i % len(self.bufs)]
        self.i += 1
        return b


D = 1024
SEQ = 16384
CTX = 256
NCORE = 8
TL = SEQ // NCORE
NT = TL + CTX
NTILE = NT // 128
NK = SEQ + CTX
NKT = NK // 128
IN_COLS = 5056
EPS = 1e-6
ALPHA = 4.0 ** 0.25
ROPE_BASE = 10000.0
O_DQ, O_DK, O_DV = 0, 512, 1024
O_MLQK, O_MLV, O_MLO, O_MLG = 1536, 2048, 2560, 3072
O_CQ, O_CKV, O_KR = 3088, 3472, 3728
O_SZ, O_SXBC, O_SDT = 3760, 4272, 5040


def ln_stats(P, xt, scr, epst):
    st, mv, rstd, nb = scr
    for j in range(2):
        P.op("dve", "bn_stats", st[:, 6 * j:6 * j + 6], xt[:, 512 * j:512 * j + 512], reads=[xt], writes=[st])
    P.op("dve", "bn_aggr", mv[:], st[:], reads=[st], writes=[mv])
    P.op("act", "activation", rstd[:], mv[:, 1:2], AF.Sqrt, bias=epst[:], scale=1.0, reads=[mv, epst], writes=[rstd])
    P.op("dve", "reciprocal", rstd[:], rstd[:], reads=[rstd], writes=[rstd])
    P.op("dve", "scalar_tensor_tensor", nb[:], mv[:, 0:1], -1.0, rstd[:], ALU.mult, ALU.mult, reads=[mv, rstd], writes=[nb])
    return rstd, nb


def build_A(ntile=NTILE, nlat_tiles=TL // 128):
    nt = ntile * 128
    nc = bass.Bass("TRN2", target_bir_lowering=False)
    P = Prog(nc)
    xin = P.dram("xt", [nt, D], F32, kind="ExternalInput")
    ccT = P.dram("ccT", [D, 2], F32, kind="ExternalInput")
    w_mod = P.dram("w_mod", [D, 6 * D], F32, kind="ExternalInput")
    b_mod = P.dram("b_mod", [1, 6 * D], F32, kind="ExternalInput")
    w_in = P.dram("w_in", [D, IN_COLS], F32, kind="ExternalInput")
    rope = P.dram("rope", [nt, 192], F32, kind="ExternalInput")
    qn_d = P.dram("mla_q_norm", [1, 384], F32, kind="ExternalInput")
    kvn_d = P.dram("mla_kv_norm", [1, 256], F32, kind="ExternalInput")
    wuq_d = P.dram("mla_w_uq", [384, 384], F32, kind="ExternalInput")
    wukv_d = P.dram("mla_w_ukv", [256, 768], F32, kind="ExternalInput")
    identd = P.dram("ident", [128, 128], F32, kind="ExternalInput")

    mod_o = P.dram("mod", [2, 6 * D], F32, kind="ExternalOutput")
    hT_o = P.dram("hT", [D, nt], BF16, kind="ExternalOutput")
    p_o = P.dram("p", [nt, IN_COLS], F32, kind="ExternalOutput")
    qd_o = P.dram("qd", [nt, 512], BF16, kind="ExternalOutput")
    kd_o = P.dram("kd", [nt, 512], BF16, kind="ExternalOutput")
    vd_o = P.dram("vd", [nt, 512], BF16, kind="ExternalOutput")
    qm_o = P.dram("qm", [nt, 384], BF16, kind="ExternalOutput")
    kvm_o = P.dram("kvm", [nt, 768], BF16, kind="ExternalOutput")
    krm_o = P.dram("krm", [nt, 32], BF16, kind="ExternalOutput")

    identf = P.sb("identf", [128, 128], F32)
    ident = P.sb("ident", [128, 128], BF16)
    P.dma("sp", identf[:], identd[:, :], reads=[identd], writes=[identf])
    P.op("dve", "tensor_copy", ident[:], identf[:], reads=[identf], writes=[ident])
    epst = P.sb("epst", [128, 1], F32)
    P.op("pool", "memset", epst[:], EPS, writes=[epst])

    ccs = P.sb("ccs", [128, 8, 2], F32)
    P.dma("sp", ccs[:], ccT[:, :].rearrange("(k p) r -> p k r", p=128), reads=[ccT], writes=[ccs])
    ccb = P.sb("ccb", [128, 8, 2], BF16)
    P.op("act", "activation", ccb[:], ccs[:], AF.Silu, reads=[ccs], writes=[ccb])
    modT = P.sb("modT", [128, 32], F32)
    winb = P.sb("winb", [128, 8, IN_COLS], BF16)
    wuqb = P.sb("wuqb", [128, 3, 384], BF16)
    wukvb = P.sb("wukvb", [128, 2, 768], BF16)
    qn_b = P.sb("qn_b", [128, 384], F32)
    kvn_b = P.sb("kvn_b", [128, 256], F32)
    mk = P.mark()
    mod_sb = P.sb("mod_sb", [2, 6 * D], F32)
    for r in range(2):
        P.dma("sp", mod_sb[r:r + 1, :], b_mod[:, :], reads=[b_mod], writes=[mod_sb])
    wst = Rot([P.sb("wst%d" % i, [128, 8, 512], F32) for i in range(2)])
    wbf = Rot([P.sb("wbf%d" % i, [128, 8, 512], BF16) for i in range(2)])
    accm = Rot([P.ps("accm%d" % i, [128, 512], F32) for i in range(2)])
    for nb_ in range(12):
        ws, wb = wst.next(), wbf.next()
        P.dma("sp", ws[:], w_mod[:, nb_ * 512:(nb_ + 1) * 512].rearrange("(k p) n -> p k n", p=128), reads=[w_mod], writes=[ws])
        P.op("pool", "tensor_copy", wb[:], ws[:], reads=[ws], writes=[wb])
        acc = accm.next()
        for k in range(8):
            P.op("pe", "matmul", acc[0:2, :], ccb[:, k, :], wb[:, k, :], start=(k == 0), stop=(k == 7), reads=[ccb, wb], writes=[acc])
        P.op("dve", "tensor_tensor", mod_sb[:, nb_ * 512:(nb_ + 1) * 512], acc[0:2, :], mod_sb[:, nb_ * 512:(nb_ + 1) * 512], ALU.add,
             reads=[acc, mod_sb], writes=[mod_sb])
    P.dma("pool", mod_o[:, :], mod_sb[:], reads=[mod_sb], writes=[mod_o])
    mtp = P.ps("mtp", [128, 512], F32)
    for v in range(2):
        for k in range(8):
            c0 = (v * 8 + k) * 2
            P.op("pe", "transpose", mtp[:, c0:c0 + 2], mod_sb[0:2, v * D + k * 128: v * D + (k + 1) * 128], identf[0:2, 0:2],
                 reads=[mod_sb, identf], writes=[mtp])
    P.op("dve", "tensor_copy", modT[:], mtp[:, 0:32], reads=[mtp], writes=[modT])
    P.op("dve", "tensor_scalar_add", modT[:, 16:32], modT[:, 16:32], 1.0, reads=[modT], writes=[modT])

    HC = IN_COLS // 2
    wstg = Rot([P.sb("wstg%d" % i, [128, HC], F32) for i in range(2)])
    for k in range(8):
        for hh in range(2):
            ws = wstg.next()
            P.dma("sp", ws[:], w_in[k * 128:(k + 1) * 128, hh * HC:(hh + 1) * HC], reads=[w_in], writes=[ws])
            P.op("pool" if hh else "dve", "tensor_copy", winb[:, k, hh * HC:(hh + 1) * HC], ws[:], reads=[ws], writes=[winb])
    ws = wstg.next()
    P.dma("sp", ws[:, 0:1152].rearrange("p (k n) -> p k n", k=3), wuq_d[:, :].rearrange("(k p) n -> p k n", p=128), reads=[wuq_d], writes=[ws])
    P.op("dve", "tensor_copy", wuqb[:], ws[:, 0:1152].rearrange("p (k n) -> p k n", k=3), reads=[ws], writes=[wuqb])
    ws = wstg.next()
    P.dma("sp", ws[:, 0:1536].rearrange("p (k n) -> p k n", k=2), wukv_d[:, :].rearrange("(k p) n -> p k n", p=128), reads=[wukv_d], writes=[ws])
    P.op("dve", "tensor_copy", wukvb[:], ws[:, 0:1536].rearrange("p (k n) -> p k n", k=2), reads=[ws], writes=[wukvb])
    P.dma("sp", qn_b[:], qn_d[:, :].partition_broadcast(128), reads=[qn_d], writes=[qn_b])
    P.dma("sp", kvn_b[:], kvn_d[:, :].partition_broadcast(128), reads=[kvn_d], writes=[kvn_b])

    P.release(mk)
    xts = Rot([P.sb("xt%d" % i, [128, D], F32) for i in range(2)])
    ropes = Rot([P.sb("rp%d" % i, [128, 192], F32) for i in range(2)])
    scrs = Rot([(P.sb("st%d" % i, [128, 12], F32), P.sb("mv%d" % i, [128, 2], F32),
                 P.sb("rstd%d" % i, [128, 1], F32), P.sb("nb%d" % i, [128, 1], F32)) for i in range(2)])
    xns = Rot([P.sb("xn%d" % i, [128, D], BF16) for i in range(2)])
    tps = Rot([P.ps("tp%d" % i, [128, 1024], BF16) for i in range(1)])
    hTs = Rot([P.sb("hTs%d" % i, [128, 8, 128], BF16) for i in range(2)])
    accs = Rot([P.ps("acc%d" % i, [128, 512], F32) for i in range(3)])
    pts = Rot([P.sb("pt%d" % i, [128, IN_COLS], F32) for i in range(2)])
    t1s = Rot([P.sb("t1_%d" % i, [128, 512], F32) for i in range(2)])
    t2s = Rot([P.sb("t2_%d" % i, [128, 512], F32) for i in range(2)])
    obf = Rot([P.sb("obf%d" % i, [128, 768], BF16) for i in range(6)])
    sq_s = Rot([P.sb("sq%d" % i, [128, 384], F32) for i in range(2)])
    col_s = Rot([P.sb("col%d" % i, [128, 1], F32) for i in range(4)])
    cnb = Rot([P.sb("cnb%d" % i, [128, 384], BF16) for i in range(2)])
    cnT = Rot([P.sb("cnT%d" % i, [128, 3, 128], BF16) for i in range(2)])
    tp2 = Rot([P.ps("tp2_%d" % i, [128, 1024], BF16) for i in range(1)])
    qms = Rot([P.sb("qms%d" % i, [128, 384], F32) for i in range(2)])

    def rope_apply(src, n_hm, hd, cos, sin, dst_bf, t1, t2):
        prt = hd // 2
        hf = prt // 2
        nel = n_hm * hd
        x4 = src.rearrange("p (m a h f) -> p m a h f", m=n_hm, a=2, h=2, f=hf)
        t14 = t1[:, 0:nel].rearrange("p (m a h f) -> p m a h f", m=n_hm, a=2, h=2, f=hf)
        t24 = t2[:, 0:nel].rearrange("p (m a h f) -> p m a h f", m=n_hm, a=2, h=2, f=hf)
        c4 = cos.rearrange("p (a h f) -> p a h f", a=2, h=2, f=hf)
        s4 = sin.rearrange("p (a h f) -> p a h f", a=2, h=2, f=hf)
        for m in range(n_hm):
            e = "dve" if m % 2 == 0 else "pool"
            P.op(e, "tensor_tensor", t14[:, m], x4[:, m], c4, ALU.mult, reads=[srcbuf[0], rpt], writes=[t1])
            for h in range(2):
                P.op(e, "tensor_tensor", t24[:, m, :, h, :], x4[:, m, :, 1 - h, :], s4[:, :, h, :], ALU.mult,
                     reads=[srcbuf[0], rpt], writes=[t2])
        P.op("dve", "tensor_tensor", dst_bf, t1[:, 0:nel], t2[:, 0:nel], ALU.add, reads=[t1, t2], writes=[dstbuf[0]])

    srcbuf = [None]
    dstbuf = [None]
    for t in range(ntile):
        r_ = 0 if t < nlat_tiles else 1
        tok = slice(t * 128, (t + 1) * 128)
        xt = xts.next()
        P.dma("sp", xt[:], xin[tok, :], reads=[xin], writes=[xt])
        rpt = ropes.next()
        P.dma("sp", rpt[:], rope[tok, :], reads=[rope], writes=[rpt])
        rstd, nb = ln_stats(P, xt, scrs.next(), epst)
        xn = xns.next()
        P.op("act", "activation", xn[:], xt[:], AF.Identity, bias=nb[:], scale=rstd[:], reads=[xt, nb, rstd], writes=[xn])
        tp = tps.next()
        for k in range(8):
            P.op("pe", "transpose", tp[:, k * 128:(k + 1) * 128], xn[:, k * 128:(k + 1) * 128], ident[:], reads=[xn, ident], writes=[tp])
        hT = hTs.next()
        for k in range(8):
            P.op("act", "activation", hT[:, k, :], tp[:, k * 128:(k + 1) * 128], AF.Identity,
                 bias=modT[:, 2 * k + r_: 2 * k + r_ + 1], scale=modT[:, 16 + 2 * k + r_: 16 + 2 * k + r_ + 1],
                 reads=[tp, modT], writes=[hT])
        P.dma("pool", hT_o[:, tok].rearrange("(k p) t -> p k t", p=128), hT[:], reads=[hT], writes=[hT_o])
        pt = pts.next()
        for nb_ in range(10):
            n0 = nb_ * 512
            nw = min(512, IN_COLS - n0)
            acc = accs.next()
            for k in range(8):
                P.op("pe", "matmul", acc[:, 0:nw], hT[:, k, :], winb[:, k, n0:n0 + nw], start=(k == 0), stop=(k == 7),
                     reads=[hT, winb], writes=[acc])
            if nb_ % 2 == 0:
                P.op("act", "copy", pt[:, n0:n0 + nw], acc[:, 0:nw], reads=[acc], writes=[pt])
            else:
                P.op("dve", "tensor_copy", pt[:, n0:n0 + nw], acc[:, 0:nw], reads=[acc], writes=[pt])
        P.dma("pool", p_o[tok, :], pt[:], reads=[pt], writes=[p_o])
        srcbuf[0] = pt
        for (off, dst) in ((O_DQ, qd_o), (O_DK, kd_o)):
            ob = obf.next()
            dstbuf[0] = ob
            rope_apply(pt[:, off:off + 512], 8, 64, rpt[:, 0:64], rpt[:, 64:128], ob[:, 0:512], t1s.next(), t2s.next())
            P.dma("pool", dst[tok, :], ob[:, 0:512], reads=[ob], writes=[dst])
        ob = obf.next()
        P.op("act", "copy", ob[:, 0:512], pt[:, O_DV:O_DV + 512], reads=[pt], writes=[ob])
        P.dma("pool", vd_o[tok, :], ob[:, 0:512], reads=[ob], writes=[vd_o])
        for (off, width, nrm, wub, nout, kind) in ((O_CQ, 384, qn_b, wuqb, 384, "q"), (O_CKV, 256, kvn_b, wukvb, 768, "kv")):
            sq = sq_s.next()
            ss = col_s.next()
            P.op("pool", "memset", ss[:], 0.0, writes=[ss])
            P.op("act", "activation", sq[:, 0:width], pt[:, off:off + width], AF.Square, accum_out=ss[:], reads=[pt, ss], writes=[sq, ss])
            P.op("act", "activation", ss[:], ss[:], AF.Sqrt, bias=epst[:], scale=1.0 / width, reads=[ss, epst], writes=[ss])
            P.op("dve", "reciprocal", ss[:], ss[:], reads=[ss], writes=[ss])
            cn = cnb.next()
            P.op("dve", "scalar_tensor_tensor", cn[:, 0:width], pt[:, off:off + width], ss[:], nrm[:, 0:width], ALU.mult, ALU.mult,
                 reads=[pt, ss, nrm], writes=[cn])
            kc = width // 128
            tq = tp2.next()
            for k in range(kc):
                P.op("pe", "transpose", tq[:, k * 128:(k + 1) * 128], cn[:, k * 128:(k + 1) * 128], ident[:], reads=[cn, ident], writes=[tq])
            ct = cnT.next()
            P.op("act", "copy", ct[:, 0:kc, :], tq[:, 0:kc * 128].rearrange("p (k t) -> p k t", k=kc), reads=[tq], writes=[ct])
            if kind == "q":
                acc = accs.next()
                for k in range(kc):
                    P.op("pe", "matmul", acc[:, 0:384], ct[:, k, :], wub[:, k, :], start=(k == 0), stop=(k == kc - 1), reads=[ct, wub], writes=[acc])
                qf = qms.next()
                P.op("act", "copy", qf[:], acc[:, 0:384], reads=[acc], writes=[qf])
                ob = obf.next()
                q3 = qf[:].rearrange("p (h d) -> p h d", h=4)
                o3 = ob[:, 0:384].rearrange("p (h d) -> p h d", h=4)
                P.op("pool", "tensor_copy", o3[:, :, 0:64], q3[:, :, 0:64], reads=[qf], writes=[ob])
                t1 = t1s.next()
                t2 = t2s.next()
                qr = sq_s.next()
                P.op("dve", "tensor_copy", qr[:, 0:128].rearrange("p (h d) -> p h d", h=4), q3[:, :, 64:96], reads=[qf], writes=[qr])
                srcbuf[0] = qr
                rb = cnb.next()
                dstbuf[0] = rb
                rope_apply(qr[:, 0:128], 4, 32, rpt[:, 128:160], rpt[:, 160:192], rb[:, 0:128], t1, t2)
                P.op("dve", "tensor_copy", o3[:, :, 64:96], rb[:, 0:128].rearrange("p (h d) -> p h d", h=4), reads=[rb], writes=[ob])
                P.dma("pool", qm_o[tok, :], ob[:, 0:384], reads=[ob], writes=[qm_o])
            else:
                ob = obf.next()
                for (c0, cw) in ((0, 512), (512, 256)):
                    acc = accs.next()
                    for k in range(kc):
                        P.op("pe", "matmul", acc[:, 0:cw], ct[:, k, :], wub[:, k, c0:c0 + cw], start=(k == 0), stop=(k == kc - 1),
                             reads=[ct, wub], writes=[acc])
                    P.op("act", "copy", ob[:, c0:c0 + cw], acc[:, 0:cw], reads=[acc], writes=[ob])
                P.dma("pool", kvm_o[tok, :], ob[:, 0:768], reads=[ob], writes=[kvm_o])
        srcbuf[0] = pt
        ob = obf.next()
        dstbuf[0] = ob
        rope_apply(pt[:, O_KR:O_KR + 32], 1, 32, rpt[:, 128:160], rpt[:, 160:192], ob[:, 0:32], t1s.next(), t2s.next())
        P.dma("pool", krm_o[tok, :], ob[:, 0:32], reads=[ob], writes=[krm_o])
    P.emit()
    return nc


def rope_tables(pos_row, pos_col, is_ctx):
    n = pos_row.shape[0]
    out = np.zeros((n, 192), np.float32)

    def tab(part):
        half = part // 2
        inv = (ROPE_BASE ** (-np.arange(half, dtype=np.float32) * 2.0 / part)).astype(np.float32)
        cs, sn = [], []
        for pos in (pos_row, pos_col):
            ang = pos.astype(np.float32)[:, None] * inv[None, :]
            c, s_ = np.cos(ang).astype(np.float32), np.sin(ang).astype(np.float32)
            cs.append(np.concatenate([c, c], 1))
            sn.append(np.concatenate([-s_, s_], 1))
        return np.concatenate(cs, 1), np.concatenate(sn, 1)

    cD, sD = tab(32)
    cM, sM = tab(16)
    out[:, 0:64], out[:, 64:128], out[:, 128:160], out[:, 160:192] = cD, sD, cM, sM
    out[is_ctx, 0:64] = 1.0
    out[is_ctx, 64:128] = 0.0
    out[is_ctx, 128:160] = 1.0
    out[is_ctx, 160:192] = 0.0
    return out


def build_B1(nq_lat=TL, nq_ctx=CTX, nkt=NKT, nkt_ctx=CTX // 128, which="both"):
    nq = nq_lat + nq_ctx
    nk = nkt * 128
    nc = bass.Bass("TRN2", target_bir_lowering=False)
    P = Prog(nc)
    do_d, do_m = which in ("both", "diff"), which in ("both", "mla")
    QdT = KdT = Vd = QmT = KmT = Vm = a_o = m_o = None
    if do_d:
        QdT = P.dram("QdT", [4, 128, nq], BF16, kind="ExternalInput")
        KdT = P.dram("KdT", [4, 128, nk], BF16, kind="ExternalInput")
        Vd = P.dram("Vd", [4, 128, nkt, 128], BF16, kind="ExternalInput")
        a_o = P.dram("a_out", [nq, 512], BF16, kind="ExternalOutput")
    if do_m:
        QmT = P.dram("QmT", [4, 96, nq], BF16, kind="ExternalInput")
        KmT = P.dram("KmT", [4, 96, nk], BF16, kind="ExternalInput")
        Vm = P.dram("Vm", [4, 128, nkt, 128], BF16, kind="ExternalInput")
        m_o = P.dram("m_out", [nq, 512], BF16, kind="ExternalOutput")
    lam_d = P.dram("diff_lambda", [1, 256], F32, kind="ExternalInput")
    subln_d = P.dram("diff_subln", [1, 128], F32, kind="ExternalInput")
    lamc_d = P.dram("lamc", [1, 2], F32, kind="ExternalInput")

    epst = P.sb("epst", [128, 1], F32)
    P.op("pool", "memset", epst[:], EPS, writes=[epst])
    lam_b = P.sb("lam_b", [128, 256], F32)
    P.dma("sp", lam_b[:], lam_d[:, :].partition_broadcast(128), reads=[lam_d], writes=[lam_b])
    subln_b = P.sb("subln_b", [128, 128], F32)
    P.dma("sp", subln_b[:], subln_d[:, :].partition_broadcast(128), reads=[subln_d], writes=[subln_b])
    lamc_b = P.sb("lamc_b", [128, 2], F32)
    P.dma("sp", lamc_b[:], lamc_d[:, :].partition_broadcast(128), reads=[lamc_d], writes=[lamc_b])
    lprod = P.sb("lprod", [128, 128], F32)
    lsum = P.sb("lsum", [128, 2], F32)
    for i in range(2):
        P.op("dve", "tensor_tensor", lprod[:, i * 64:(i + 1) * 64], lam_b[:, (2 * i) * 64:(2 * i + 1) * 64],
             lam_b[:, (2 * i + 1) * 64:(2 * i + 2) * 64], ALU.mult, reads=[lam_b], writes=[lprod])
        P.op("dve", "reduce_sum", lsum[:, i:i + 1], lprod[:, i * 64:(i + 1) * 64], AX.X, reads=[lprod], writes=[lsum])
    P.op("act", "activation", lsum[:], lsum[:], AF.Exp, reads=[lsum], writes=[lsum])
    nlam = P.sb("nlam", [128, 1], F32)
    P.op("dve", "tensor_tensor", nlam[:], lsum[:, 1:2], lsum[:, 0:1], ALU.subtract, reads=[lsum], writes=[nlam])
    P.op("dve", "tensor_tensor", nlam[:], nlam[:], lamc_b[:, 0:1], ALU.subtract, reads=[nlam, lamc_b], writes=[nlam])
    P.op("dve", "tensor_scalar", subln_b[:], subln_b[:], lamc_b[:, 1:2], None, ALU.mult, reads=[subln_b, lamc_b], writes=[subln_b])

    KTs = Rot([P.sb("KT%d" % i, [128, nk], BF16) for i in range(2)])
    Vs = Rot([P.sb("V%d" % i, [128, nkt, 129], BF16) for i in range(2)])
    for vb in Vs.bufs:
        P.op("pool", "memset", vb[:, :, 128:129], 1.0, writes=[vb])
    QTs = Rot([P.sb("QT%d" % i, [128, nq], BF16) for i in range(2)])
    sps = Rot([P.ps("sp%d" % i, [128, 512], F32) for i in range(3)])
    accs = Rot([P.ps("pacc%d" % i, [128, 2, 512], F32) for i in range(2)])
    pts = Rot([P.sb("pT%d" % i, [128, 512], BF16) for i in range(3)])
    t0s = Rot([P.sb("t0_%d" % i, [128, 4, 128], F32) for i in range(2)])
    o_s = Rot([P.sb("o_%d" % i, [128, 128], F32) for i in range(3)])
    sq_s = Rot([P.sb("sqb%d" % i, [128, 128], F32) for i in range(2)])
    rs = Rot([P.sb("r%d" % i, [128, 1], F32) for i in range(8)])
    obs = Rot([P.sb("ob%d" % i, [128, 128], BF16) for i in range(4)])

    blocks = [(i * 512, 512, 0, nkt) for i in range(nq_lat // 512)]
    if nq_ctx:
        blocks.append((nq_lat, nq_ctx, nkt - nkt_ctx, nkt))

    for hu in ([0, 1, 2, 3] if do_d else []) + ([4, 5, 6, 7] if do_m else []):
        diff = hu < 4
        h = hu % 4
        rows = 128 if diff else 96
        KT, V, QT = KTs.next(), Vs.next(), QTs.next()
        ksrc = KdT if diff else KmT
        vsrc = Vd if diff else Vm
        qsrc = QdT if diff else QmT
        nch = 4
        cw = nk // nch
        for c in range(nch):
            P.dma("sp", KT[0:rows, c * cw:(c + 1) * cw], ksrc[h, :, c * cw:(c + 1) * cw], reads=[ksrc], writes=[KT])
        tw = 8
        for t0_ in range(0, nkt, tw):
            t1_ = min(nkt, t0_ + tw)
            P.dma("sp", V[:, t0_:t1_, 0:128], vsrc[h, :, t0_:t1_, :], reads=[vsrc], writes=[V])
        P.dma("sp", QT[0:rows, :], qsrc[h, :, :], reads=[qsrc], writes=[QT])
        scale = 0.125 if diff else 96.0 ** -0.5
        nmap = 2 if diff else 1
        for (q0, qw, kt0, kt1) in blocks:
            nj = qw // 128
            t0 = t0s.next()
            for mp in range(nmap):
                r0, r1 = (mp * 64, mp * 64 + 64) if diff else (0, 96)
                acc = accs.next()
                for kt in range(kt0, kt1):
                    sp_ = sps.next()
                    P.op("pe", "matmul", sp_[:, 0:qw], KT[r0:r1, kt * 128:(kt + 1) * 128], QT[r0:r1, q0:q0 + qw], start=True, stop=True,
                         reads=[KT, QT], writes=[sp_])
                    pT = pts.next()
                    P.op("act", "activation", pT[:, 0:qw], sp_[:, 0:qw], AF.Exp, scale=scale, reads=[sp_], writes=[pT])
                    for j in range(nj):
                        P.op("pe", "matmul", acc[:, j // 2, (j % 2) * 256:(j % 2) * 256 + 129], pT[:, j * 128:(j + 1) * 128], V[:, kt, :],
                             start=(kt == kt0 and j % 2 == 0), stop=(kt == kt1 - 1), skip_group_check=True, reads=[pT, V], writes=[acc])
                for j in range(nj):
                    av = acc[:, j // 2, (j % 2) * 256:(j % 2) * 256 + 128]
                    sv = acc[:, j // 2, (j % 2) * 256 + 128:(j % 2) * 256 + 129]
                    tok = slice(q0 + j * 128, q0 + (j + 1) * 128)
                    r = rs.next()
                    P.op("dve", "reciprocal", r[:], sv, reads=[acc], writes=[r])
                    if diff and mp == 0:
                        P.op("act", "activation", t0[:, j, :], av, AF.Copy, scale=r[:], reads=[acc, r], writes=[t0])
                    elif diff:
                        P.op("dve", "tensor_tensor", r[:], r[:], nlam[:], ALU.mult, reads=[r, nlam], writes=[r])
                        o = o_s.next()
                        P.op("dve", "scalar_tensor_tensor", o[:], av, r[:], t0[:, j, :], ALU.mult, ALU.add, reads=[acc, r, t0], writes=[o])
                        sq = sq_s.next()
                        ss = rs.next()
                        P.op("pool", "memset", ss[:], 0.0, writes=[ss])
                        P.op("act", "activation", sq[:], o[:], AF.Square, accum_out=ss[:], reads=[o, ss], writes=[sq, ss])
                        P.op("act", "activation", ss[:], ss[:], AF.Sqrt, bias=epst[:], scale=1.0 / 128, reads=[ss, epst], writes=[ss])
                        P.op("dve", "reciprocal", ss[:], ss[:], reads=[ss], writes=[ss])
                        ob = obs.next()
                        P.op("dve", "scalar_tensor_tensor", ob[:], o[:], ss[:], subln_b[:], ALU.mult, ALU.mult, reads=[o, ss, subln_b], writes=[ob])
                        P.dma("pool", a_o[tok, h * 128:(h + 1) * 128], ob[:], reads=[ob], writes=[a_o])
                    else:
                        ob = obs.next()
                        P.op("act", "activation", ob[:], av, AF.Copy, scale=r[:], reads=[acc, r], writes=[ob])
                        P.dma("pool", m_o[tok, h * 128:(h + 1) * 128], ob[:], reads=[ob], writes=[m_o])
    P.emit()
    return nc


TPAD = (CTX + 4) + (SEQ + 4)


def build_B2(n_ctx=CTX, n_lat=SEQ):
    T = n_ctx + n_lat
    tpad = (n_ctx + 4) + (n_lat + 4)
    nc = bass.Bass("TRN2", target_bir_lowering=False)
    P = Prog(nc)
    di = lambda n, s, dt=F32: P.dram(n, s, dt, kind="ExternalInput")
    qpre, kpre = di("ml_qpre", [64, tpad]), di("ml_kpre", [64, tpad])
    ml_cw = di("ml_cw", [128, 6])
    ml_v = di("ml_v", [T, 128])
    ml_g = di("ml_g", [T, 2])
    ml_gb = di("ml_gb", [1, 2])
    xpre, bpre, cpre = di("sd_xpre", [128, tpad]), di("sd_bpre", [64, tpad]), di("sd_cpre", [64, tpad])
    sd_cw = di("sd_cw", [128, 3, 6])
    sd_dt = di("sd_dt", [T, 2])
    sd_par = di("sd_par", [1, 4])
    consts = di("consts", [128, 4, 128])
    h_o = P.dram("ml_h", [T, 128], F32, kind="ExternalOutput")
    y_o = P.dram("sd_y", [T, 128], F32, kind="ExternalOutput")
    xs_o = P.dram("sd_xs", [T, 128], F32, kind="ExternalOutput")

    cst = P.sb("cst", [128, 4, 128], F32)
    P.dma("sp", cst[:], consts[:, :, :], reads=[consts], writes=[cst])
    identf, U, LS, ONES = cst[:, 0, :], cst[:, 1, :], cst[:, 2, :], cst[:, 3, :]
    identb = P.sb("identb", [128, 128], BF16)
    P.op("dve", "tensor_copy", identb[:], identf, reads=[cst], writes=[identb])
    one_c = P.sb("one_c", [128, 1], F32)
    P.op("pool", "memset", one_c[:], 1.0, writes=[one_c])
    mlcw = P.sb("mlcw", [128, 6], F32)
    P.dma("sp", mlcw[:], ml_cw[:, :], reads=[ml_cw], writes=[mlcw])
    mlcwk = P.sb("mlcwk", [64, 6], F32)
    P.dma("sp", mlcwk[:], ml_cw[64:128, :], reads=[ml_cw], writes=[mlcwk])
    sdcw = P.sb("sdcw", [128, 3, 6], F32)
    P.dma("sp", sdcw[:], sd_cw[:, :, :], reads=[sd_cw], writes=[sdcw])
    gb_b = P.sb("gb_b", [128, 2], F32)
    P.dma("sp", gb_b[:], ml_gb[:, :].partition_broadcast(128), reads=[ml_gb], writes=[gb_b])
    par_b = P.sb("par_b", [128, 4], F32)
    P.dma("sp", par_b[:], sd_par[:, :].partition_broadcast(128), reads=[sd_par], writes=[par_b])
    ealog = P.sb("ealog", [128, 2], F32)
    P.op("act", "activation", ealog[:], par_b[:, 2:4], AF.Exp, reads=[par_b], writes=[ealog])
    P.op("dve", "tensor_scalar", ealog[:], ealog[:], -1.0, None, ALU.mult, reads=[ealog], writes=[ealog])

    Cst = P.sb("Cst", [64, 129], F32)
    Cbf = P.sb("Cbf", [64, 129], BF16)
    Hst = P.sb("Hst", [64, 2, 64], F32)
    Hbf = P.sb("Hbf", [64, 2, 64], BF16)
    for b_ in (Cst, Cbf, Hst, Hbf):
        P.op("pool", "memset", b_[:], 0.0, writes=[b_])

    pG = P.ps("pG", [128, 512], F32)
    pSeg = Rot([P.ps("pSeg%d" % i, [128, 512], F32) for i in range(2)])
    pSc = P.ps("pSc", [128, 512], F32)
    pIO = P.ps("pIO", [128, 512], F32)
    pTb = P.ps("pTb", [128, 1024], BF16)
    pTf = P.ps("pTf", [128, 512], F32)
    pU = P.ps("pU", [128, 512], F32)

    R2 = lambda name, shape, dt, n=2: Rot([P.sb("%s%d" % (name, i), shape, dt) for i in range(n)])
    qin, kin = R2("qin", [64, 516], F32), R2("kin", [64, 516], F32)
    xin_, bin_, cin_ = R2("xin", [128, 516], F32), R2("bin", [64, 516], F32), R2("cin", [64, 516], F32)
    cacc = R2("cacc", [128, 512], F32, 3)
    qTb, kTb = R2("qTb", [64, 512], BF16), R2("kTb", [64, 512], BF16)
    xTf = R2("xTf", [128, 512], F32)
    BTb, CTb = R2("BTb", [64, 512], BF16), R2("CTb", [64, 512], BF16)
    vin = R2("vin", [128, 4, 128], F32)
    vaug = R2("vaug", [128, 4, 129], BF16)
    for vb in vaug.bufs:
        P.op("pool", "memset", vb[:, :, 128:129], 1.0, writes=[vb])
    gin, dtin = R2("gin", [128, 4, 2], F32), R2("dtin", [128, 4, 2], F32)
    gcol = R2("gcol", [128, 4, 8], F32)
    scol = R2("scol", [128, 4, 2, 6], F32)
    Amat = R2("Amat", [128, 128], F32, 3)
    Dmat = R2("Dmat", [128, 128], F32, 3)
    Wb = R2("Wb", [128, 128], BF16, 3)
    scm = R2("scm", [128, 128], F32)
    intra = R2("intra", [128, 129], F32)
    tot_s = R2("tot_s", [128, 129], F32)
    hout = R2("hout", [128, 128], F32, 3)
    yout = R2("yout", [128, 128], F32, 3)
    xsout = R2("xsout", [128, 128], F32, 3)
    xdt = R2("xdt", [128, 2, 64], BF16)
    kw = R2("kw", [128, 64], BF16)
    Bw = R2("Bw", [128, 2, 64], BF16)
    col1 = R2("col1", [128, 4], F32, 4)

    def conv_silu(eng, src, wt, rows, nw, out_f32=None, out_bf=None, out_scale=None):
        acc = cacc.next()
        P.op(eng, "tensor_scalar", acc[0:rows, 0:nw], src[0:rows, 0:nw], wt[0:rows, 0:1], wt[0:rows, 5:6], ALU.mult, ALU.add,
             reads=[srcb[0], wtb[0]], writes=[acc])
        for j in range(1, 5):
            P.op(eng, "scalar_tensor_tensor", acc[0:rows, 0:nw], src[0:rows, j:j + nw], wt[0:rows, j:j + 1], acc[0:rows, 0:nw], ALU.mult, ALU.add,
                 reads=[srcb[0], wtb[0], acc], writes=[acc])
        if out_f32 is not None:
            P.op("act", "activation", out_f32[0][0:rows, 0:nw], acc[0:rows, 0:nw], AF.Silu, reads=[acc], writes=[out_f32[1]])
            if out_bf is not None:
                P.op("pool", "tensor_copy", out_bf[0][0:rows, 0:nw], out_f32[0][0:rows, 0:nw], reads=[out_f32[1]], writes=[out_bf[1]])
        else:
            if out_scale is None:
                P.op("act", "activation", out_bf[0][0:rows, 0:nw], acc[0:rows, 0:nw], AF.Silu, reads=[acc], writes=[out_bf[1]])
            else:
                P.op("act", "activation", acc[0:rows, 0:nw], acc[0:rows, 0:nw], AF.Silu, reads=[acc], writes=[acc])
                P.op("dve", "tensor_scalar", out_bf[0][0:rows, 0:nw], acc[0:rows, 0:nw], out_scale, None, ALU.mult, reads=[acc], writes=[out_bf[1]])

    srcb = [None]
    wtb = [None]
    blocks = []
    if n_ctx:
        blocks.append((0, 0, n_ctx))
    for i in range(n_lat // 512):
        blocks.append((n_ctx + 4 + i * 512, n_ctx + i * 512, 512))

    for (poff, toff, nw) in blocks:
        nch = nw // 128
        qi, ki, xi, bi, ci = qin.next(), kin.next(), xin_.next(), bin_.next(), cin_.next()
        P.dma("sp", qi[:, 0:nw + 4], qpre[:, poff:poff + nw + 4], reads=[qpre], writes=[qi])
        P.dma("sp", ki[:, 0:nw + 4], kpre[:, poff:poff + nw + 4], reads=[kpre], writes=[ki])
        P.dma("sp", xi[:, 0:nw + 4], xpre[:, poff:poff + nw + 4], reads=[xpre], writes=[xi])
        P.dma("sp", bi[:, 0:nw + 4], bpre[:, poff:poff + nw + 4], reads=[bpre], writes=[bi])
        P.dma("sp", ci[:, 0:nw + 4], cpre[:, poff:poff + nw + 4], reads=[cpre], writes=[ci])
        vi, va, gi, dti = vin.next(), vaug.next(), gin.next(), dtin.next()
        P.dma("sp", vi[:, 0:nch, :], ml_v[toff:toff + nw, :].rearrange("(c p) e -> p c e", p=128), reads=[ml_v], writes=[vi])
        P.dma("sp", gi[:, 0:nch, :], ml_g[toff:toff + nw, :].rearrange("(c p) e -> p c e", p=128), reads=[ml_g], writes=[gi])
        P.dma("sp", dti[:, 0:nch, :], sd_dt[toff:toff + nw, :].rearrange("(c p) e -> p c e", p=128), reads=[sd_dt], writes=[dti])
        P.op("pool", "tensor_copy", va[:, 0:nch, 0:128], vi[:, 0:nch, :], reads=[vi], writes=[va])
        qT, kT, xT, BT, CT = qTb.next(), kTb.next(), xTf.next(), BTb.next(), CTb.next()
        srcb[0], wtb[0] = qi, mlcw
        conv_silu("dve", qi, mlcw, 64, nw, out_bf=(qT, qT))
        srcb[0], wtb[0] = ki, mlcwk
        conv_silu("dve", ki, mlcwk, 64, nw, out_bf=(kT, kT), out_scale=0.125)
        srcb[0], wtb[0] = xi, sdcw
        conv_silu("dve", xi, sdcw[:, 0, :], 128, nw, out_f32=(xT, xT))
        srcb[0] = bi
        conv_silu("dve", bi, sdcw[:, 1, :], 64, nw, out_bf=(BT, BT))
        srcb[0] = ci
        conv_silu("dve", ci, sdcw[:, 2, :], 64, nw, out_bf=(CT, CT))
        gc = gcol.next()
        P.op("dve", "tensor_scalar", gc[:, 0:nch, 0], gi[:, 0:nch, 0], gb_b[:, 0:1], None, ALU.add, reads=[gi, gb_b], writes=[gc])
        P.op("dve", "tensor_scalar", gc[:, 0:nch, 1], gi[:, 0:nch, 1], gb_b[:, 1:2], None, ALU.add, reads=[gi, gb_b], writes=[gc])
        P.op("act", "activation", gc[:, 0:nch, 1], gc[:, 0:nch, 1], AF.Exp, scale=-1.0, reads=[gc], writes=[gc])
        P.op("act", "activation", gc[:, 0:nch, 1], gc[:, 0:nch, 1], AF.Ln, bias=one_c[:], scale=1.0, reads=[gc, one_c], writes=[gc])
        P.op("dve", "tensor_scalar", gc[:, 0:nch, 1], gc[:, 0:nch, 1], -1.0, None, ALU.mult, reads=[gc], writes=[gc])
        sc_ = scol.next()
        for hh in range(2):
            P.op("dve", "tensor_scalar", sc_[:, 0:nch, hh, 0], dti[:, 0:nch, hh], par_b[:, hh:hh + 1], None, ALU.add, reads=[dti, par_b], writes=[sc_])
        P.op("act", "activation", sc_[:, 0:nch, :, 0], sc_[:, 0:nch, :, 0], AF.Exp, reads=[sc_], writes=[sc_])
        P.op("act", "activation", sc_[:, 0:nch, :, 0], sc_[:, 0:nch, :, 0], AF.Ln, bias=one_c[:], scale=1.0, reads=[sc_, one_c], writes=[sc_])
        for hh in range(2):
            P.op("dve", "tensor_scalar", sc_[:, 0:nch, hh, 1], sc_[:, 0:nch, hh, 0], ealog[:, hh:hh + 1], None, ALU.mult, reads=[sc_, ealog], writes=[sc_])
        pk = cacc.next()
        pk3 = pk[:, 0:nch * 3].rearrange("p (c k) -> p c k", k=3)
        P.op("dve", "tensor_copy", pk3[:, :, 0], gc[:, 0:nch, 1], reads=[gc], writes=[pk])
        P.op("dve", "tensor_copy", pk3[:, :, 1:3], sc_[:, 0:nch, :, 1], reads=[sc_], writes=[pk])
        n3 = nch * 3
        P.op("pe", "matmul", pG[:, 0:n3], U, pk[:, 0:n3], start=True, stop=True, reads=[cst, pk], writes=[pG])
        P.op("pe", "matmul", pG[:, 16:16 + n3], ONES, pk[:, 0:n3], start=True, stop=True, reads=[cst, pk], writes=[pG])
        cum3 = pG[:, 0:n3].rearrange("p (c k) -> p c k", k=3)
        tot3 = pG[:, 16:16 + n3].rearrange("p (c k) -> p c k", k=3)
        P.op("dve", "tensor_copy", gc[:, 0:nch, 2], cum3[:, :, 0], reads=[pG], writes=[gc])
        P.op("dve", "tensor_copy", gc[:, 0:nch, 3], tot3[:, :, 0], reads=[pG], writes=[gc])
        P.op("dve", "tensor_copy", sc_[:, 0:nch, :, 2], cum3[:, :, 1:3], reads=[pG], writes=[sc_])
        P.op("dve", "tensor_copy", sc_[:, 0:nch, :, 3], tot3[:, :, 1:3], reads=[pG], writes=[sc_])
        P.op("act", "activation", gc[:, 0:nch, 4], gc[:, 0:nch, 2], AF.Exp, reads=[gc], writes=[gc])
        P.op("dve", "tensor_tensor", gc[:, 0:nch, 5], gc[:, 0:nch, 3], gc[:, 0:nch, 2], ALU.subtract, reads=[gc], writes=[gc])
        P.op("dve", "tensor_tensor", gc[:, 0:nch, 5], gc[:, 0:nch, 5], gc[:, 0:nch, 0], ALU.add, reads=[gc], writes=[gc])
        P.op("act", "activation", gc[:, 0:nch, 5], gc[:, 0:nch, 5], AF.Exp, reads=[gc], writes=[gc])
        P.op("act", "activation", gc[:, 0:nch, 6], gc[:, 0:nch, 3], AF.Exp, reads=[gc], writes=[gc])
        P.op("act", "activation", sc_[:, 0:nch, :, 4], sc_[:, 0:nch, :, 2], AF.Exp, reads=[sc_], writes=[sc_])
        P.op("dve", "tensor_tensor", sc_[:, 0:nch, :, 5], sc_[:, 0:nch, :, 3], sc_[:, 0:nch, :, 2], ALU.subtract, reads=[sc_], writes=[sc_])
        P.op("act", "activation", sc_[:, 0:nch, :, 5], sc_[:, 0:nch, :, 5], AF.Exp, reads=[sc_], writes=[sc_])
        P.op("act", "activation", sc_[:, 0:nch, :, 3], sc_[:, 0:nch, :, 3], AF.Exp, reads=[sc_], writes=[sc_])

        for c in range(nch):
            cs = slice(c * 128, (c + 1) * 128)
            tok = slice(toff + c * 128, toff + (c + 1) * 128)
            A = Amat.next()
            P.op("dve", "tensor_scalar", A[:], LS, gc[:, c, 1:2], None, ALU.mult, reads=[cst, gc], writes=[A])
            seg = pSeg.next()
            P.op("pe", "matmul", seg[:, 0:128], A[:], U, start=True, stop=True, reads=[A, cst], writes=[seg])
            Dm = Dmat.next()
            P.op("act", "activation", Dm[:], seg[:, 0:128], AF.Exp, bias=gc[:, c, 0:1], scale=1.0, reads=[seg, gc], writes=[Dm])
            P.op("pool", "tensor_tensor", Dm[:], Dm[:], U, ALU.mult, reads=[Dm, cst], writes=[Dm])
            P.op("pe", "matmul", pSc[:, 0:128], kT[:, cs], qT[:, cs], start=True, stop=True, reads=[kT, qT], writes=[pSc])
            W = Wb.next()
            P.op("dve", "tensor_tensor", W[:], pSc[:, 0:128], Dm[:], ALU.mult, reads=[pSc, Dm], writes=[W])
            P.op("pe", "matmul", pIO[:, 0:129], W[:], va[:, c, :], start=True, stop=True, reads=[W, va], writes=[pIO])
            P.op("pe", "matmul", pIO[:, 256:385], qT[:, cs], Cbf[:], start=True, stop=True, reads=[qT, Cbf], writes=[pIO])
            it = intra.next()
            P.op("act", "copy", it[:], pIO[:, 0:129], reads=[pIO], writes=[it])
            tt = tot_s.next()
            P.op("dve", "scalar_tensor_tensor", tt[:], pIO[:, 256:385], gc[:, c, 4:5], it[:], ALU.mult, ALU.add, reads=[pIO, gc, it], writes=[tt])
            rr = col1.next()
            P.op("dve", "tensor_scalar", rr[:, 1:2], tt[:, 128:129], -1.0, None, ALU.mult, reads=[tt], writes=[rr])
            P.op("dve", "tensor_tensor", rr[:, 0:1], tt[:, 128:129], rr[:, 1:2], ALU.max, reads=[tt, rr], writes=[rr])
            P.op("dve", "tensor_scalar", rr[:, 0:1], rr[:, 0:1], 1.0, None, ALU.max, reads=[rr], writes=[rr])
            P.op("dve", "reciprocal", rr[:, 0:1], rr[:, 0:1], reads=[rr], writes=[rr])
            ho = hout.next()
            P.op("act", "activation", ho[:], tt[:, 0:128], AF.Copy, scale=rr[:, 0:1], reads=[tt, rr], writes=[ho])
            P.dma("pool", h_o[tok, :], ho[:], reads=[ho], writes=[h_o])
            P.op("pe", "transpose", pTb[:, 0:64], kT[:, cs], identb[0:64, 0:64], reads=[kT, identb], writes=[pTb])
            kw_ = kw.next()
            P.op("dve", "tensor_scalar", kw_[:], pTb[:, 0:64], gc[:, c, 5:6], None, ALU.mult, reads=[pTb, gc], writes=[kw_])
            P.op("pe", "matmul", pU[0:64, 0:129], kw_[:], va[:, c, :], start=True, stop=True, reads=[kw_, va], writes=[pU])
            P.op("dve", "scalar_tensor_tensor", Cst[:], Cst[:], gc[0:64, c, 6:7], pU[0:64, 0:129], ALU.mult, ALU.add, reads=[Cst, gc, pU], writes=[Cst])
            P.op("pool", "tensor_copy", Cbf[:], Cst[:], reads=[Cst], writes=[Cbf])
            P.op("pe", "matmul", pSc[:, 256:384], BT[:, cs], CT[:, cs], start=True, stop=True, reads=[BT, CT], writes=[pSc])
            sm = scm.next()
            P.op("dve", "tensor_tensor", sm[:], pSc[:, 256:384], U, ALU.mult, reads=[pSc, cst], writes=[sm])
            P.op("pe", "transpose", pTf[:, 0:128], xT[:, cs], identf, reads=[xT, cst], writes=[pTf])
            xo = xsout.next()
            P.op("act", "copy", xo[:], pTf[:, 0:128], reads=[pTf], writes=[xo])
            P.dma("pool", xs_o[tok, :], xo[:], reads=[xo], writes=[xs_o])
            xd = xdt.next()
            for hh in range(2):
                P.op("dve", "tensor_scalar", xd[:, hh, :], xo[:, hh * 64:(hh + 1) * 64], sc_[:, c, hh, 0:1], None, ALU.mult, reads=[xo, sc_], writes=[xd])
            P.op("pe", "transpose", pTb[:, 512:576], BT[:, cs], identb[0:64, 0:64], reads=[BT, identb], writes=[pTb])
            bw_ = Bw.next()
            for hh in range(2):
                P.op("dve", "tensor_scalar", bw_[:, hh, :], pTb[:, 512:576], sc_[:, c, hh, 5:6], None, ALU.mult, reads=[pTb, sc_], writes=[bw_])
            yo = yout.next()
            for hh in range(2):
                A = Amat.next()
                P.op("pool", "tensor_scalar", A[:], LS, sc_[:, c, hh, 1:2], None, ALU.mult, reads=[cst, sc_], writes=[A])
                seg = pSeg.next()
                P.op("pe", "matmul", seg[:, 0:128], A[:], U, start=True, stop=True, reads=[A, cst], writes=[seg])
                Dm = Dmat.next()
                P.op("act", "activation", Dm[:], seg[:, 0:128], AF.Exp, reads=[seg], writes=[Dm])
                W = Wb.next()
                P.op("dve", "tensor_tensor", W[:], Dm[:], sm[:], ALU.mult, reads=[Dm, sm], writes=[W])
                P.op("pe", "matmul", pIO[:, 0:64], W[:], xd[:, hh, :], start=True, stop=True, reads=[W, xd], writes=[pIO])
                P.op("pe", "matmul", pIO[:, 256:320], CT[:, cs], Hbf[:, hh, :], start=True, stop=True, reads=[CT, Hbf], writes=[pIO])
                it = intra.next()
                P.op("act", "copy", it[:, 0:64], pIO[:, 0:64], reads=[pIO], writes=[it])
                P.op("dve", "scalar_tensor_tensor", yo[:, hh * 64:(hh + 1) * 64], pIO[:, 256:320], sc_[:, c, hh, 4:5], it[:, 0:64], ALU.mult, ALU.add,
                     reads=[pIO, sc_, it], writes=[yo])
                P.op("pe", "matmul", pU[0:64, 256:320], bw_[:, hh, :], xd[:, hh, :], start=True, stop=True, reads=[bw_, xd], writes=[pU])
                P.op("dve", "scalar_tensor_tensor", Hst[:, hh, :], Hst[:, hh, :], sc_[0:64, c, hh, 3:4], pU[0:64, 256:320], ALU.mult, ALU.add,
                     reads=[Hst, sc_, pU], writes=[Hst])
                P.op("pool", "tensor_copy", Hbf[:, hh, :], Hst[:, hh, :], reads=[Hst], writes=[Hbf])
            P.dma("pool", y_o[tok, :], yo[:], reads=[yo], writes=[y_o])
    P.emit()
    return nc


def cast_to_bf16_dram(P, src, dst, rows, cols, stg, stgb, engs=("dve", "pool")):
    i = 0
    for r0 in range(0, rows, 128):
        for c0 in range(0, cols, 2048):
            cw = min(2048, cols - c0)
            s, sb_ = stg.next(), stgb.next()
            P.dma("sp", s[:, 0:cw], src[r0:r0 + 128, c0:c0 + cw], reads=[src], writes=[s])
            P.op(engs[i % len(engs)], "tensor_copy", sb_[:, 0:cw], s[:, 0:cw], reads=[s], writes=[sb_])
            P.dma("pool", dst[r0:r0 + 128, c0:c0 + cw], sb_[:, 0:cw], reads=[sb_], writes=[dst])
            i += 1


def build_C(n_lat=TL, n_ctx=CTX, dbg=0):
    nt = n_lat + n_ctx
    nc = bass.Bass("TRN2", target_bir_lowering=False)
    P = Prog(nc)
    di = lambda n, s, dt=F32: P.dram(n, s, dt, kind="ExternalInput")
    x_d = di("x", [nt, D])
    hT_d = di("hT", [D, nt], BF16)
    mod_d = di("mod", [2, 6 * D])
    aT_d, mT_d = di("aT", [512, nt], BF16), di("mT", [512, nt], BF16)
    hf_d, hb_d, og_d = di("hf", [nt, 512]), di("hb", [nt, 512]), di("og", [nt, 512])
    yf_d, yb_d, xs_d, z_d = di("yf", [nt, 512]), di("yb", [nt, 512]), di("xs", [nt, 512]), di("z", [nt, 512])
    mlnorm_d, ssdnorm_d, dsk_d = di("ml_norm", [1, 512]), di("ssd_norm", [1, 512]), di("dskip", [1, 512])
    wg_d, bg_d = di("w_gate", [4 * D, D]), di("b_gateT", [128, 32])
    wbr_d, wo_d = di("w_branch", [4 * 512, D]), di("w_o", [D, D])
    ln1g_d, ln1b_d, ln2g_d, ln2b_d = di("ln1_g", [1, D]), di("ln1_b", [1, D]), di("ln2_g", [1, D]), di("ln2_b", [1, D])
    wup_d, bup_d = di("w_up", [D, 4 * D]), di("b_upT", [128, 32])
    wdn_d, bdn_d = di("w_down", [4 * D, D]), di("b_down", [1, D])
    identd = di("ident", [128, 128])
    xo_d = P.dram("x_out", [nt, D], F32, kind="ExternalOutput")
    wg_bf = P.dram("wg_bf", [4 * D, D], BF16)
    wbr_bf = P.dram("wbr_bf", [4 * 512, D], BF16)
    wup_bf = P.dram("wup_bf", [D, 4 * D], BF16)
    wdn_bf = P.dram("wdn_bf", [4 * D, D], BF16)
    x1_d = P.dram("x1_scr", [nt, D], F32)

    identf = P.sb("identf", [128, 128], F32)
    ident = P.sb("ident", [128, 128], BF16)
    P.dma("sp", identf[:], identd[:, :], reads=[identd], writes=[identf])
    P.op("dve", "tensor_copy", ident[:], identf[:], reads=[identf], writes=[ident])
    epst = P.sb("epst", [128, 1], F32)
    P.op("pool", "memset", epst[:], EPS, writes=[epst])
    def bcast(nm, d_ap, w):
        t = P.sb(nm, [128, w], F32)
        P.dma("sp", t[:], d_ap.partition_broadcast(128), reads=[bcsrc.get(nm, mod_d if nm[1] != "m" or nm[2] != "b" else bdn_d)], writes=[t])
        return t
    bcsrc = dict(ln1g=ln1g_d, ln1b=ln1b_d, ln2g=ln2g_d, ln2b=ln2b_d, bdn=bdn_d, mln=mlnorm_d, ssn=ssdnorm_d, dsk=dsk_d)
    modT = P.sb("modT", [128, 32], F32)
    bgT = P.sb("bgT", [128, 32], F32)
    P.dma("sp", bgT[:], bg_d[:, :], reads=[bg_d], writes=[bgT])
    bupT = P.sb("bupT", [128, 32], F32)
    P.dma("sp", bupT[:], bup_d[:, :], reads=[bup_d], writes=[bupT])
    scrs = Rot([(P.sb("st%d" % i, [128, 12], F32), P.sb("mv%d" % i, [128, 2], F32),
                 P.sb("rstd%d" % i, [128, 1], F32), P.sb("nb%d" % i, [128, 1], F32)) for i in range(2)])
    mkm = P.mark()
    mod_sb = P.sb("mod_sb", [2, 2 * D], F32)
    P.dma("sp", mod_sb[:], mod_d[:, 3 * D:5 * D], reads=[mod_d], writes=[mod_sb])
    mtp = P.ps("mtp", [128, 512], F32)
    for v in range(2):
        for k in range(8):
            c0 = (v * 8 + k) * 2
            P.op("pe", "transpose", mtp[:, c0:c0 + 2], mod_sb[0:2, v * D + k * 128: v * D + (k + 1) * 128], identf[0:2, 0:2],
                 reads=[mod_sb, identf], writes=[mtp])
    P.op("dve", "tensor_copy", modT[:], mtp[:, 0:32], reads=[mtp], writes=[modT])
    P.op("dve", "tensor_scalar_add", modT[:, 16:32], modT[:, 16:32], 1.0, reads=[modT], writes=[modT])
    P.release(mkm)
    mk0 = P.mark()
    stg = Rot([P.sb("stg%d" % i, [128, 2048], F32) for i in range(2)])
    stgb = Rot([P.sb("stgb%d" % i, [128, 2048], BF16) for i in range(2)])
    cast_to_bf16_dram(P, wg_d, wg_bf, 4 * D, D, stg, stgb)
    cast_to_bf16_dram(P, wbr_d, wbr_bf, 4 * 512, D, stg, stgb)
    cast_to_bf16_dram(P, wup_d, wup_bf, D, 4 * D, stg, stgb)
    cast_to_bf16_dram(P, wdn_d, wdn_bf, 4 * D, D, stg, stgb)
    wo_bf = P.dram("wo_bf", [D, D], BF16)
    cast_to_bf16_dram(P, wo_d, wo_bf, D, D, stg, stgb)
    P.release(mk0)

    blocks = [(i * 512, 512, 0) for i in range(n_lat // 512)]
    if n_ctx:
        blocks.append((n_lat, n_ctx, 1))
    if dbg == 1:
        P.emit()
        return nc

    mk1 = P.mark()
    wob = P.sb("wob", [128, 8, D], BF16)
    P.dma("sp", wob[:], wo_bf[:, :].rearrange("(c p) n -> p c n", p=128), reads=[wo_bf], writes=[wob])
    gat = [bcast("ga%d" % r, mod_d[r:r + 1, 2 * D:3 * D], D) for r in range(2)]
    bc = {nm: bcast(nm, bcsrc[nm][:, :], w) for nm, w in (("ln1g", D), ("ln1b", D), ("mln", 512), ("ssn", 512), ("dsk", 512))}
    wgs = Rot([P.sb("wg%d" % i, [128, 8, D], BF16) for i in range(2)])
    wbs = Rot([P.sb("wb%d" % i, [128, 4, D], BF16) for i in range(2)])
    hTb = P.sb("hTb", [128, 8, 512], BF16)
    brT = Rot([P.sb("brT%d" % i, [128, 4, 512], BF16) for i in range(2)])
    bTs = P.sb("bTs", [128, 4, 512], BF16)
    sTs = P.sb("sTs", [128, 4, 512], BF16)
    yT = P.sb("yT", [128, 8, 512], F32)
    yTb = P.sb("yTb", [128, 8, 512], BF16)
    Gs = Rot([P.sb("G%d" % i, [128, 512], F32) for i in range(2)])
    prs = Rot([P.sb("pr%d" % i, [128, 512], F32) for i in range(2)])
    tA = Rot([P.sb("tA%d" % i, [128, 512], F32) for i in range(2)])
    tB = Rot([P.sb("tB%d" % i, [128, 512], F32) for i in range(2)])
    tC = Rot([P.sb("tC%d" % i, [128, 512], F32) for i in range(2)])
    tD = Rot([P.sb("tD%d" % i, [128, 512], F32) for i in range(2)])
    tbf = Rot([P.sb("tbf%d" % i, [128, 512], BF16) for i in range(2)])
    cols = Rot([P.sb("cl%d" % i, [128, 4], F32) for i in range(4)])
    xts = Rot([P.sb("xt%d" % i, [128, D], F32) for i in range(2)])
    rts = Rot([P.sb("rt%d" % i, [128, D], F32) for i in range(2)])
    pg = Rot([P.ps("pg%d" % i, [128, 512], F32) for i in range(2)])
    pb = Rot([P.ps("pb%d" % i, [128, 512], F32) for i in range(2)])
    po = Rot([P.ps("po%d" % i, [128, 512], F32) for i in range(2)])
    ptp = P.ps("ptp", [128, 1024], BF16)

    def rms_groups(src, ngrp, gw, wtile, dst, sq):
        cl = cols.next()
        P.op("pool", "memset", cl[:], 0.0, writes=[cl])
        for g in range(ngrp):
            P.op("act", "activation", sq[:, g * gw:(g + 1) * gw], src[:, g * gw:(g + 1) * gw], AF.Square, accum_out=cl[:, g:g + 1],
                 reads=[src, cl], writes=[sq, cl])
        P.op("act", "activation", cl[:, 0:ngrp], cl[:, 0:ngrp], AF.Sqrt, bias=epst[:], scale=1.0 / gw, reads=[cl, epst], writes=[cl])
        P.op("dve", "reciprocal", cl[:, 0:ngrp], cl[:, 0:ngrp], reads=[cl], writes=[cl])
        for g in range(ngrp):
            P.op("dve", "scalar_tensor_tensor", dst[:, g * gw:(g + 1) * gw], src[:, g * gw:(g + 1) * gw], cl[:, g:g + 1],
                 wtile[:, g * gw:(g + 1) * gw], ALU.mult, ALU.mult, reads=[src, cl, wtile], writes=[dst])

    for (t0, bw, r_) in blocks:
        nj = bw // 128
        P.dma("sp", hTb[:, :, 0:bw], hT_d[:, t0:t0 + bw].rearrange("(k p) t -> p k t", p=128), reads=[hT_d], writes=[hTb])
        for j in range(nj):
            tok = slice(t0 + j * 128, t0 + (j + 1) * 128)
            a_, b_, c_, d_ = tA.next(), tB.next(), tC.next(), tD.next()
            P.dma("sp", a_[:], hf_d[tok, :], reads=[hf_d], writes=[a_])
            P.dma("sp", b_[:], hb_d[tok, :], reads=[hb_d], writes=[b_])
            P.dma("sp", c_[:], og_d[tok, :], reads=[og_d], writes=[c_])
            P.op("pool", "tensor_tensor", a_[:], a_[:], b_[:], ALU.add, reads=[a_, b_], writes=[a_])
            rms_groups(a_, 4, 128, bc["mln"], b_, d_)
            P.op("act", "activation", c_[:], c_[:], AF.Sigmoid, reads=[c_], writes=[c_])
            tb_ = tbf.next()
            P.op("dve", "tensor_tensor", tb_[:], b_[:], c_[:], ALU.mult, reads=[b_, c_], writes=[tb_])
            for k in range(4):
                P.op("pe", "transpose", ptp[:, k * 128:(k + 1) * 128], tb_[:, k * 128:(k + 1) * 128], ident[:], reads=[tb_, ident], writes=[ptp])
            P.op("act", "copy", bTs[:, :, j * 128:(j + 1) * 128], ptp[:, 0:512].rearrange("p (k t) -> p k t", k=4), reads=[ptp], writes=[bTs])
            a_, b_, c_, d_ = tA.next(), tB.next(), tC.next(), tD.next()
            P.dma("sp", a_[:], yf_d[tok, :], reads=[yf_d], writes=[a_])
            P.dma("sp", b_[:], yb_d[tok, :], reads=[yb_d], writes=[b_])
            P.dma("sp", c_[:], xs_d[tok, :], reads=[xs_d], writes=[c_])
            P.dma("sp", d_[:], z_d[tok, :], reads=[z_d], writes=[d_])
            P.op("pool", "tensor_tensor", a_[:], a_[:], b_[:], ALU.add, reads=[a_, b_], writes=[a_])
            P.op("pool", "tensor_tensor", c_[:], c_[:], bc["dsk"][:], ALU.mult, reads=[c_, bc["dsk"]], writes=[c_])
            P.op("pool", "tensor_tensor", a_[:], a_[:], c_[:], ALU.add, reads=[a_, c_], writes=[a_])
            P.op("act", "activation", d_[:], d_[:], AF.Silu, reads=[d_], writes=[d_])
            P.op("dve", "tensor_tensor", a_[:], a_[:], d_[:], ALU.mult, reads=[a_, d_], writes=[a_])
            tb_ = tbf.next()
            rms_groups(a_, 2, 256, bc["ssn"], b_, c_)
            P.op("pool", "tensor_copy", tb_[:], b_[:], reads=[b_], writes=[tb_])
            for k in range(4):
                P.op("pe", "transpose", ptp[:, k * 128:(k + 1) * 128], tb_[:, k * 128:(k + 1) * 128], ident[:], reads=[tb_, ident], writes=[ptp])
            P.op("act", "copy", sTs[:, :, j * 128:(j + 1) * 128], ptp[:, 0:512].rearrange("p (k t) -> p k t", k=4), reads=[ptp], writes=[sTs])
        for k in range(4):
            wg, wb = wgs.next(), wbs.next()
            P.dma("sp", wg[:], wg_bf[k * D:(k + 1) * D, :].rearrange("(c p) n -> p c n", p=128), reads=[wg_bf], writes=[wg])
            P.dma("sp", wb[:], wbr_bf[k * 512:(k + 1) * 512, :].rearrange("(c p) n -> p c n", p=128), reads=[wbr_bf], writes=[wb])
            if k == 0 or k == 2:
                br = brT.next()
                src_ = aT_d if k == 0 else mT_d
                P.dma("sp", br[:, :, 0:bw], src_[:, t0:t0 + bw].rearrange("(c p) t -> p c t", p=128), reads=[src_], writes=[br])
            else:
                br = bTs if k == 1 else sTs
            for fc in range(8):
                g_ps = pg.next()
                for kc in range(8):
                    P.op("pe", "matmul", g_ps[:, 0:bw], wg[:, kc, fc * 128:(fc + 1) * 128], hTb[:, kc, 0:bw], start=(kc == 0), stop=(kc == 7),
                         reads=[wg, hTb], writes=[g_ps])
                G = Gs.next()
                P.op("act", "activation", G[:, 0:bw], g_ps[:, 0:bw], AF.Sigmoid, bias=bgT[:, k * 8 + fc:k * 8 + fc + 1], scale=1.0,
                     reads=[g_ps, bgT], writes=[G])
                b_ps = pb.next()
                for kc in range(4):
                    P.op("pe", "matmul", b_ps[:, 0:bw], wb[:, kc, fc * 128:(fc + 1) * 128], br[:, kc, 0:bw], start=(kc == 0), stop=(kc == 3),
                         reads=[wb, br], writes=[b_ps])
                if k == 0:
                    P.op("dve", "tensor_tensor", yT[:, fc, 0:bw], b_ps[:, 0:bw], G[:, 0:bw], ALU.mult, reads=[b_ps, G], writes=[yT])
                else:
                    pr = prs.next()
                    P.op("dve", "tensor_tensor", pr[:, 0:bw], b_ps[:, 0:bw], G[:, 0:bw], ALU.mult, reads=[b_ps, G], writes=[pr])
                    if k < 3:
                        P.op("pool", "tensor_tensor", yT[:, fc, 0:bw], yT[:, fc, 0:bw], pr[:, 0:bw], ALU.add, reads=[yT, pr], writes=[yT])
                    else:
                        P.op("pool", "tensor_tensor", yTb[:, fc, 0:bw], yT[:, fc, 0:bw], pr[:, 0:bw], ALU.add, reads=[yT, pr], writes=[yTb])
        for j in range(nj):
            tok = slice(t0 + j * 128, t0 + (j + 1) * 128)
            xt = xts.next()
            P.dma("sp", xt[:], x_d[tok, :], reads=[x_d], writes=[xt])
            rt = rts.next()
            for nb_ in range(2):
                o_ps = po.next()
                for fc in range(8):
                    P.op("pe", "matmul", o_ps[:], yTb[:, fc, j * 128:(j + 1) * 128], wob[:, fc, nb_ * 512:(nb_ + 1) * 512], start=(fc == 0), stop=(fc == 7),
                         reads=[yTb, wob], writes=[o_ps])
                P.op("dve", "tensor_tensor", rt[:, nb_ * 512:(nb_ + 1) * 512], o_ps[:], gat[r_][:, nb_ * 512:(nb_ + 1) * 512], ALU.mult,
                     reads=[o_ps, gat[r_]], writes=[rt])
            P.op("dve", "scalar_tensor_tensor", rt[:], xt[:], ALPHA, rt[:], ALU.mult, ALU.add, reads=[xt, rt], writes=[rt])
            rstd, nb = ln_stats(P, rt, scrs.next(), epst)
            P.op("act", "activation", xt[:], rt[:], AF.Identity, bias=nb[:], scale=rstd[:], reads=[rt, nb, rstd], writes=[xt])
            P.op("pool", "tensor_tensor", xt[:], xt[:], bc["ln1g"][:], ALU.mult, reads=[xt, bc["ln1g"]], writes=[xt])
            P.op("pool", "tensor_tensor", xt[:], xt[:], bc["ln1b"][:], ALU.add, reads=[xt, bc["ln1b"]], writes=[xt])
            P.dma("pool", x1_d[tok, :], xt[:], reads=[xt], writes=[x1_d])
    P.release(mk1)
    if dbg == 2:
        P.emit()
        return nc
    gmt = [bcast("gm%d" % r, mod_d[r:r + 1, 5 * D:6 * D], D) for r in range(2)]
    bc = {nm: bcast(nm, bcsrc[nm][:, :], w) for nm, w in (("ln2g", D), ("ln2b", D))}
    gmb = [bcast("gmb%d" % r, bdn_d[:, :], D) for r in range(2)]
    for r in range(2):
        P.op("dve", "tensor_tensor", gmb[r][:], gmb[r][:], gmt[r][:], ALU.mult, reads=[gmt[r], gmb[r]], writes=[gmb[r]])
    wus = Rot([P.sb("wu%d" % i, [128, 8, D], BF16) for i in range(2)])
    wds = Rot([P.sb("wd%d" % i, [128, 8, D], BF16) for i in range(2)])
    h2T = P.sb("h2T", [128, 8, 512], BF16)
    u2T = P.sb("u2T", [128, 32, 512], BF16)
    us = Rot([P.sb("u%d" % i, [128, 512], F32) for i in range(2)])
    x1s = [P.sb("x1_%d" % i, [128, D], F32) for i in range(4)]
    xns = Rot([P.sb("xn%d" % i, [128, D], BF16) for i in range(2)])
    rts = Rot([P.sb("rt2_%d" % i, [128, D], F32) for i in range(2)])
    pu = Rot([P.ps("pu%d" % i, [128, 512], F32) for i in range(2)])
    pd = [P.ps("pd%d" % i, [128, 512], F32) for i in range(4)]
    ptp2 = P.ps("ptp2", [128, 1024], BF16)
    for (t0, bw, r_) in blocks:
        nj = bw // 128
        for j in range(nj):
            tok = slice(t0 + j * 128, t0 + (j + 1) * 128)
            xt = x1s[j]
            P.dma("sp", xt[:], x1_d[tok, :], reads=[x1_d], writes=[xt])
            rstd, nb = ln_stats(P, xt, scrs.next(), epst)
            xn = xns.next()
            P.op("act", "activation", xn[:], xt[:], AF.Identity, bias=nb[:], scale=rstd[:], reads=[xt, nb, rstd], writes=[xn])
            for k in range(8):
                P.op("pe", "transpose", ptp2[:, k * 128:(k + 1) * 128], xn[:, k * 128:(k + 1) * 128], ident[:], reads=[xn, ident], writes=[ptp2])
            for k in range(8):
                P.op("act", "activation", h2T[:, k, j * 128:(j + 1) * 128], ptp2[:, k * 128:(k + 1) * 128], AF.Identity,
                     bias=modT[:, 2 * k + r_: 2 * k + r_ + 1], scale=modT[:, 16 + 2 * k + r_: 16 + 2 * k + r_ + 1], reads=[ptp2, modT], writes=[h2T])
        for q in range(4):
            wu = wus.next()
            P.dma("sp", wu[:], wup_bf[:, q * D:(q + 1) * D].rearrange("(c p) n -> p c n", p=128), reads=[wup_bf], writes=[wu])
            for fl in range(8):
                ffc = q * 8 + fl
                u_ps = pu.next()
                for kc in range(8):
                    P.op("pe", "matmul", u_ps[:, 0:bw], wu[:, kc, fl * 128:(fl + 1) * 128], h2T[:, kc, 0:bw], start=(kc == 0), stop=(kc == 7),
                         reads=[wu, h2T], writes=[u_ps])
                u = us.next()
                P.op("act", "activation", u[:, 0:bw], u_ps[:, 0:bw], AF.Relu, bias=bupT[:, ffc:ffc + 1], scale=1.0, reads=[u_ps, bupT], writes=[u])
                P.op("dve" if fl % 2 else "pool", "tensor_tensor", u2T[:, ffc, 0:bw], u[:, 0:bw], u[:, 0:bw], ALU.mult, reads=[u], writes=[u2T])
        for half in range((nj + 1) // 2):
            js = [j for j in (2 * half, 2 * half + 1) if j < nj]
            for q in range(4):
                wd = wds.next()
                P.dma("sp", wd[:], wdn_bf[q * D:(q + 1) * D, :].rearrange("(c p) n -> p c n", p=128), reads=[wdn_bf], writes=[wd])
                for ji, j in enumerate(js):
                    for nb_ in range(2):
                        d_ps = pd[ji * 2 + nb_]
                        for fl in range(8):
                            ffc = q * 8 + fl
                            P.op("pe", "matmul", d_ps[:], u2T[:, ffc, j * 128:(j + 1) * 128], wd[:, fl, nb_ * 512:(nb_ + 1) * 512],
                                 start=(ffc == 0), stop=(ffc == 31), reads=[u2T, wd], writes=[d_ps])
            for ji, j in enumerate(js):
                tok = slice(t0 + j * 128, t0 + (j + 1) * 128)
                xt = x1s[j]
                rt = rts.next()
                for nb_ in range(2):
                    cs_ = slice(nb_ * 512, (nb_ + 1) * 512)
                    P.op("dve", "tensor_tensor", rt[:, cs_], pd[ji * 2 + nb_][:], gmt[r_][:, cs_], ALU.mult, reads=[pd[ji * 2 + nb_], gmt[r_]], writes=[rt])
                P.op("pool", "tensor_tensor", rt[:], rt[:], gmb[r_][:], ALU.add, reads=[rt, gmb[r_]], writes=[rt])
                P.op("dve", "scalar_tensor_tensor", rt[:], xt[:], ALPHA, rt[:], ALU.mult, ALU.add, reads=[xt, rt], writes=[rt])
                rstd, nb = ln_stats(P, rt, scrs.next(), epst)
                P.op("act", "activation", rt[:], rt[:], AF.Identity, bias=nb[:], scale=rstd[:], reads=[rt, nb, rstd], writes=[rt])
                P.op("pool", "tensor_tensor", rt[:], rt[:], bc["ln2g"][:], ALU.mult, reads=[rt, bc["ln2g"]], writes=[rt])
                P.op("pool", "tensor_tensor", rt[:], rt[:], bc["ln2b"][:], ALU.add, reads=[rt, bc["ln2b"]], writes=[rt])
                P.dma("pool", xo_d[tok, :], rt[:], reads=[rt], writes=[xo_d])
    P.emit()
    return nc


_PROGS = {}
_CONSTS = {}
VERBOSE = False


def _prog(name, builder):
    if name not in _PROGS:
        _PROGS[name] = builder()
    return _PROGS[name]


def _run(name, builder, in_maps):
    import time
    t0 = time.time()
    nc = _prog(name, builder)
    res = run_bass_kernel_spmd(nc, in_maps, core_ids=list(range(NCORE)))
    if VERBOSE:
        print("launch", name, "%.1fs" % (time.time() - t0), flush=True)
    return res.results


def _c(a):
    return np.ascontiguousarray(a)


def _pad_streams(a):
    z = np.zeros((a.shape[0], 2), a.dtype)
    return _c(np.concatenate([z, a[:, :CTX], z, z, a[:, CTX:], z], 1))


def _flip_streams(a):
    return np.concatenate([a[:CTX][::-1], a[CTX:][::-1]], 0)


def kernel(x, c, ctx, c_ctx, w_mod, b_mod, w_in, diff_lambda, diff_subln, ml_conv_w, ml_conv_b,
           ml_gate_b, ml_norm, mla_q_norm, mla_kv_norm, mla_w_uq, mla_w_ukv, ssd_conv_w, ssd_conv_b,
           ssd_dt_bias, ssd_a_log, ssd_d, ssd_norm, w_gate, b_gate, w_branch, w_o, ln1_g, ln1_b,
           w_up, b_up, w_down, b_down, ln2_g, ln2_b):
    f32 = np.float32
    A_ = lambda v: np.asarray(v, dtype=f32)
    x = A_(x)[0]
    xc = A_(ctx)[0]
    ccT = _c(np.stack([A_(c)[0], A_(c_ctx)], 1))
    ident = np.eye(128, dtype=f32)
    k_ = np.arange(128)
    consts = _c(np.stack([np.eye(128), (k_[:, None] <= k_[None, :]), (k_[:, None] > k_[None, :]), np.ones((128, 128))], 1).astype(f32))
    ropes = []
    for cid in range(NCORE):
        pos = np.arange(cid * TL, (cid + 1) * TL)
        prow = np.concatenate([pos // 64, np.zeros(CTX, np.int64)])
        pcol = np.concatenate([pos % 64, np.zeros(CTX, np.int64)])
        isctx = np.concatenate([np.zeros(TL, bool), np.ones(CTX, bool)])
        ropes.append(rope_tables(prow, pcol, isctx))
    depth = A_(w_mod).shape[0]
    for l in range(depth):
        W = lambda v: A_(v)[l]
        lam_init = 0.8 - 0.6 * math.exp(-0.3 * l)
        ims = []
        for cid in range(NCORE):
            ims.append(dict(xt=_c(np.concatenate([x[cid * TL:(cid + 1) * TL], xc], 0)), ccT=ccT, w_mod=W(w_mod), b_mod=W(b_mod)[None],
                            w_in=W(w_in), rope=ropes[cid], mla_q_norm=W(mla_q_norm)[None], mla_kv_norm=W(mla_kv_norm)[None],
                            mla_w_uq=W(mla_w_uq), mla_w_ukv=W(mla_w_ukv), ident=ident))
        ra = _run("A", build_A, ims)

        def gather(name):
            return np.concatenate([ra[cid][name][:TL] for cid in range(NCORE)] + [ra[0][name][TL:]], 0)
        kd_all, vd_all = gather("kd"), gather("vd")
        kvm_all, krm_all = gather("kvm").reshape(NK, 4, 192), gather("krm")
        KdT = _c(kd_all.reshape(NK, 4, 128).transpose(1, 2, 0))
        Vd = _c(vd_all.reshape(NKT, 128, 4, 128).transpose(2, 1, 0, 3))
        km = np.concatenate([kvm_all[:, :, :64], np.broadcast_to(krm_all[:, None, :], (NK, 4, 32))], 2)
        KmT = _c(km.transpose(1, 2, 0))
        Vm = _c(kvm_all[:, :, 64:].reshape(NKT, 128, 4, 128).transpose(2, 1, 0, 3))
        lam_in = _c(W(diff_lambda).reshape(1, 256))
        common = dict(diff_lambda=lam_in, diff_subln=W(diff_subln)[None], lamc=np.array([[lam_init, 1.0 - lam_init]], f32))
        ims = [dict(QdT=_c(ra[cid]["qd"].reshape(NT, 4, 128).transpose(1, 2, 0)), KdT=KdT, Vd=Vd, **common) for cid in range(NCORE)]
        rb1d = _run("B1d", lambda: build_B1(which="diff"), ims)
        ims = [dict(QmT=_c(ra[cid]["qm"].reshape(NT, 4, 96).transpose(1, 2, 0)), KmT=KmT, Vm=Vm, **common) for cid in range(NCORE)]
        rb1m = _run("B1m", lambda: build_B1(which="mla"), ims)
        rb1 = [dict(a_out=rb1d[cid]["a_out"], m_out=rb1m[cid]["m_out"]) for cid in range(NCORE)]
        del rb1d, rb1m
        del KdT, Vd, KmT, Vm, km, kd_all, vd_all, kvm_all
        p_lat = np.concatenate([ra[cid]["p"][:TL] for cid in range(NCORE)], 0)
        p_seq = np.concatenate([ra[0]["p"][TL:], p_lat], 0)
        del p_lat
        p_dir = [p_seq, _flip_streams(p_seq)]
        mcw, mcb = W(ml_conv_w), W(ml_conv_b)
        scw, scb = W(ssd_conv_w), W(ssd_conv_b)
        ims = []
        for cid in range(NCORE):
            hm, d_ = cid % 4, cid // 4
            ps = p_dir[d_]
            taps = (lambda w: w[::-1]) if d_ else (lambda w: w)
            qf = slice(hm * 64, hm * 64 + 64)
            kf = slice(256 + hm * 64, 256 + hm * 64 + 64)
            ml_cw = np.concatenate([np.concatenate([taps(mcw[:, qf]).T, mcb[qf, None]], 1),
                                    np.concatenate([taps(mcw[:, kf]).T, mcb[kf, None]], 1)], 0).astype(f32)
            gates = ps[:, O_MLG:O_MLG + 16]
            hA = 2 * hm
            g_ = hA // 4
            xf = slice(hA * 64, hA * 64 + 128)
            bf_ = slice(512 + g_ * 64, 512 + g_ * 64 + 64)
            cf = slice(640 + g_ * 64, 640 + g_ * 64 + 64)
            sd_cw = np.zeros((128, 3, 6), f32)
            sd_cw[:, 0, :5], sd_cw[:, 0, 5] = taps(scw[:, xf]).T, scb[xf]
            sd_cw[:64, 1, :5], sd_cw[:64, 1, 5] = taps(scw[:, bf_]).T, scb[bf_]
            sd_cw[:64, 2, :5], sd_cw[:64, 2, 5] = taps(scw[:, cf]).T, scb[cf]
            dtb, alg = W(ssd_dt_bias), W(ssd_a_log)
            ims.append(dict(
                ml_qpre=_pad_streams(ps[:, O_MLQK + hm * 64: O_MLQK + hm * 64 + 64].T),
                ml_kpre=_pad_streams(ps[:, O_MLQK + 256 + hm * 64: O_MLQK + 256 + hm * 64 + 64].T),
                ml_cw=_c(ml_cw), ml_v=_c(ps[:, O_MLV + hm * 128: O_MLV + hm * 128 + 128]),
                ml_g=_c(np.stack([gates[:, (2 * d_) * 4 + hm], gates[:, (2 * d_ + 1) * 4 + hm]], 1)),
                ml_gb=np.array([[W(ml_gate_b)[2 * d_, hm], W(ml_gate_b)[2 * d_ + 1, hm]]], f32),
                sd_xpre=_pad_streams(ps[:, O_SXBC + hA * 64: O_SXBC + hA * 64 + 128].T),
                sd_bpre=_pad_streams(ps[:, O_SXBC + 512 + g_ * 64: O_SXBC + 512 + g_ * 64 + 64].T),
                sd_cpre=_pad_streams(ps[:, O_SXBC + 640 + g_ * 64: O_SXBC + 640 + g_ * 64 + 64].T),
                sd_cw=sd_cw, sd_dt=_c(ps[:, [O_SDT + d_ * 8 + hA, O_SDT + d_ * 8 + hA + 1]]),
                sd_par=np.array([[dtb[d_, hA], dtb[d_, hA + 1], alg[d_, hA], alg[d_, hA + 1]]], f32),
                consts=consts))
        rb2 = _run("B2", build_B2, ims)
        del p_dir, ims
        unf = lambda cid, a: (_flip_streams(a) if cid >= 4 else a)
        hf_all = np.concatenate([rb2[cid]["ml_h"] for cid in range(4)], 1)
        hb_all = np.concatenate([unf(cid, rb2[cid]["ml_h"]) for cid in range(4, 8)], 1)
        yf_all = np.concatenate([rb2[cid]["sd_y"] for cid in range(4)], 1)
        yb_all = np.concatenate([unf(cid, rb2[cid]["sd_y"]) for cid in range(4, 8)], 1)
        xs_all = np.concatenate([rb2[cid]["sd_xs"] for cid in range(4)], 1)
        dsk = _c(np.repeat(W(ssd_d), 64)[None])
        bgT = _c(W(b_gate).reshape(4, 8, 128).transpose(2, 0, 1).reshape(128, 32))
        bupT = _c(W(b_up).reshape(32, 128).T)
        ims = []
        for cid in range(NCORE):
            def tok(a):
                return _c(np.concatenate([a[CTX + cid * TL: CTX + (cid + 1) * TL], a[:CTX]], 0))
            ims.append(dict(
                x=_c(np.concatenate([x[cid * TL:(cid + 1) * TL], xc], 0)), hT=ra[cid]["hT"], mod=ra[cid]["mod"],
                aT=_c(rb1[cid]["a_out"].T), mT=_c(rb1[cid]["m_out"].T),
                hf=tok(hf_all), hb=tok(hb_all), og=_c(ra[cid]["p"][:, O_MLO:O_MLO + 512]),
                yf=tok(yf_all), yb=tok(yb_all), xs=tok(xs_all), z=_c(ra[cid]["p"][:, O_SZ:O_SZ + 512]),
                ml_norm=W(ml_norm)[None], ssd_norm=W(ssd_norm)[None], dskip=dsk,
                w_gate=_c(W(w_gate).reshape(4 * D, D)), b_gateT=bgT, w_branch=_c(W(w_branch).reshape(4 * 512, D)), w_o=W(w_o),
                ln1_g=W(ln1_g)[None], ln1_b=W(ln1_b)[None], ln2_g=W(ln2_g)[None], ln2_b=W(ln2_b)[None],
                w_up=W(w_up), b_upT=bupT, w_down=W(w_down), b_down=W(b_down)[None], ident=ident))
        rc = _run("C", build_C, ims)
        x = np.concatenate([rc[cid]["x_out"][:TL] for cid in range(NCORE)], 0)
        xc = rc[0]["x_out"][TL:]
        del ra, rb1, rb2, rc, ims
    return x[None].astype(f32)
```

```python
import math
import numpy as np
import ml_dtypes
import concourse.bass as bass
import concourse.mybir as mybir
from concourse.bass_utils import run_bass_kernel_spmd

F32 = mybir.dt.float32
BF16 = mybir.dt.bfloat16
AF = mybir.ActivationFunctionType
ALU = mybir.AluOpType
AX = mybir.AxisListType
NPBF = ml_dtypes.bfloat16

ENGS = ("pe", "act", "dve", "pool", "sp")


class Buf:
    def __init__(self, t, name, kind):
        self.t = t
        self.name = name
        self.kind = kind
        self.last_w = None
        self.reads = []
        self.dsem = None

    def __getitem__(self, idx):
        return self.t[idx]


class Prog:
    def __init__(self, nc, n_dma_sems=90):
        self.nc = nc
        self.ops = {e: [] for e in ENGS}
        self.n_dma_sems = n_dma_sems
        self.dma_sem_next = 0
        self.dma_sem_counts = [0] * n_dma_sems
        self.ctx = []
        self.uid = 0

    def sb(self, name, shape, dtype):
        self.uid += 1
        g = self.nc.sbuf_tensor("%s_%d" % (name, self.uid), list(shape), dtype)
        t = g.__enter__()
        self.ctx.append(g)
        return Buf(t, name, "sb")

    def ps(self, name, shape, dtype):
        self.uid += 1
        g = self.nc.psum_tensor("%s_%d" % (name, self.uid), list(shape), dtype)
        t = g.__enter__()
        self.ctx.append(g)
        return Buf(t, name, "ps")

    def dram(self, name, shape, dtype, kind="Internal"):
        t = self.nc.dram_tensor(name, list(shape), dtype, kind=kind)
        return Buf(t.ap(), name, "dram")

    def mark(self):
        return len(self.ctx)

    def barrier(self):
        toks = [("eng", e, len(self.ops[e]) - 1) for e in ENGS if self.ops[e]]
        toks += [("dma", s_, c) for s_, c in enumerate(self.dma_sem_counts) if c > 0]
        for e in ENGS:
            self.ops[e].append(dict(fn=(lambda en: en.nop()), deps=list(toks), dma=None, waited=False))

    def release(self, mark):
        self.barrier()
        while len(self.ctx) > mark:
            self.ctx.pop().__exit__(None, None, None)

    def _deps(self, reads, writes, eng=None):
        deps = []
        for b in reads:
            if b.last_w is not None:
                deps.append(b.last_w)
            if b.kind == "ps":
                deps.extend(r for r in b.reads if not (r[0] == "eng" and r[1] == eng))
        for b in writes:
            if b.last_w is not None:
                deps.append(b.last_w)
            deps.extend(b.reads)
        return deps

    def _commit(self, tok, reads, writes):
        for b in reads:
            b.reads.append(tok)
            if len(b.reads) > 64:
                b.reads = b.reads[-64:] if False else b.reads
        for b in writes:
            b.last_w = tok
            b.reads = []

    def op(self, eng, meth, *args, reads=(), writes=(), **kw):
        fn = (lambda e: getattr(e, meth)(*args, **kw))
        deps = self._deps(reads, writes, eng)
        idx = len(self.ops[eng])
        self.ops[eng].append(dict(fn=fn, deps=deps, dma=None, waited=False))
        self._commit(("eng", eng, idx), reads, writes)

    def dma(self, eng, out_ap, in_ap, reads=(), writes=(), **kw):
        deps = self._deps(reads, writes)
        owner = None
        for b in list(writes) + list(reads):
            if b.kind != "dram":
                owner = b
                break
        if owner is None:
            owner = (list(writes) + list(reads))[0]
        if owner.dsem is None:
            owner.dsem = self.dma_sem_next % self.n_dma_sems
            self.dma_sem_next += 1
        s = owner.dsem
        self.dma_sem_counts[s] += 16
        val = self.dma_sem_counts[s]
        self.ops[eng].append(dict(fn=(lambda e: e.dma_start(out=out_ap, in_=in_ap, **kw)),
                                  deps=deps, dma=(s, val), waited=False))
        self._commit(("dma", s, val), reads, writes)

    def emit(self):
        nc = self.nc
        ops = self.ops
        for e in ENGS:
            for rec in ops[e]:
                for d in rec["deps"]:
                    if d[0] == "eng" and not (d[1] == e and e == "pe"):
                        ops[d[1]][d[2]]["waited"] = True
        for e in ENGS:
            c = 0
            for rec in ops[e]:
                if rec["waited"] and rec["dma"] is None:
                    c += 1
                rec["cnt"] = c
        sem_ctx = []
        esem = {}
        for e in ENGS:
            g = nc.semaphore("es_" + e)
            esem[e] = g.__enter__()
            sem_ctx.append(g)
        dsem = []
        for i in range(min(self.n_dma_sems, max(1, self.dma_sem_next))):
            g = nc.semaphore("ds_%d" % i)
            dsem.append(g.__enter__())
            sem_ctx.append(g)
        counts = self.dma_sem_counts

        def emit_engine(e, eng):
            waited_e = {x: 0 for x in ENGS}
            waited_d = {}
            for rec in ops[e]:
                need_e = {}
                need_d = {}
                for d in rec["deps"]:
                    if d[0] == "eng":
                        if d[1] == e and e == "pe":
                            continue
                        v = ops[d[1]][d[2]]["cnt"]
                        if v > waited_e[d[1]]:
                            need_e[d[1]] = max(need_e.get(d[1], 0), v)
                    else:
                        s, v = d[1], d[2]
                        if v > waited_d.get(s, 0):
                            need_d[s] = max(need_d.get(s, 0), v)
                for pe_, v in need_e.items():
                    eng.wait_ge(esem[pe_], v)
                    waited_e[pe_] = v
                for s, v in need_d.items():
                    eng.wait_ge(dsem[s], v)
                    waited_d[s] = v
                ins = rec["fn"](eng)
                if rec["dma"] is not None:
                    ins.then_inc(dsem[rec["dma"][0]], 16)
                elif rec["waited"]:
                    ins.then_inc(esem[e], 1)
            if e == "sp":
                for s in range(len(dsem)):
                    if counts[s] > waited_d.get(s, 0):
                        eng.wait_ge(dsem[s], counts[s])

        with nc.Block() as block:
            @block.tensor
            def _(eng):
                emit_engine("pe", eng)

            @block.scalar
            def _(eng):
                emit_engine("act", eng)

            @block.vector
            def _(eng):
                emit_engine("dve", eng)

            @block.gpsimd
            def _(eng):
                emit_engine("pool", eng)

            @block.sync
            def _(eng):
                emit_engine("sp", eng)
        for g in reversed(sem_ctx):
            g.__exit__(None, None, None)
        while self.ctx:
            self.ctx.pop().__exit__(None, None, None)


class Rot:
    def __init__(self, bufs):
        self.bufs = bufs
        self.i = 0

    def next(self):
        b = self.bufs[self.i % len(self.bufs)]
        self.i += 1
        return b


D = 1024
SEQ = 16384
CTX = 256
NCORE = 8
TL = SEQ // NCORE
NT = TL + CTX
NTILE = NT // 128
NK = SEQ + CTX
NKT = NK // 128
IN_COLS = 5056
EPS = 1e-6
ALPHA = 4.0 ** 0.25
ROPE_BASE = 10000.0
O_DQ, O_DK, O_DV = 0, 512, 1024
O_MLQK, O_MLV, O_MLO, O_MLG = 1536, 2048, 2560, 3072
O_CQ, O_CKV, O_KR = 3088, 3472, 3728
O_SZ, O_SXBC, O_SDT = 3760, 4272, 5040


def ln_stats(P, xt, scr, epst):
    st, mv, rstd, nb = scr
    for j in range(2):
        P.op("dve", "bn_stats", st[:, 6 * j:6 * j + 6], xt[:, 512 * j:512 * j + 512], reads=[xt], writes=[st])
    P.op("dve", "bn_aggr", mv[:], st[:], reads=[st], writes=[mv])
    P.op("act", "activation", rstd[:], mv[:, 1:2], AF.Sqrt, bias=epst[:], scale=1.0, reads=[mv, epst], writes=[rstd])
    P.op("dve", "reciprocal", rstd[:], rstd[:], reads=[rstd], writes=[rstd])
    P.op("dve", "scalar_tensor_tensor", nb[:], mv[:, 0:1], -1.0, rstd[:], ALU.mult, ALU.mult, reads=[mv, rstd], writes=[nb])
    return rstd, nb


def build_A(ntile=NTILE, nlat_tiles=TL // 128):
    nt = ntile * 128
    nc = bass.Bass("TRN2", target_bir_lowering=False)
    P = Prog(nc)
    xin = P.dram("xt", [nt, D], F32, kind="ExternalInput")
    ccT = P.dram("ccT", [D, 2], F32, kind="ExternalInput")
    w_mod = P.dram("w_mod", [D, 6 * D], F32, kind="ExternalInput")
    b_mod = P.dram("b_mod", [1, 6 * D], F32, kind="ExternalInput")
    w_in = P.dram("w_in", [D, IN_COLS], F32, kind="ExternalInput")
    rope = P.dram("rope", [nt, 192], F32, kind="ExternalInput")
    qn_d = P.dram("mla_q_norm", [1, 384], F32, kind="ExternalInput")
    kvn_d = P.dram("mla_kv_norm", [1, 256], F32, kind="ExternalInput")
    wuq_d = P.dram("mla_w_uq", [384, 384], F32, kind="ExternalInput")
    wukv_d = P.dram("mla_w_ukv", [256, 768], F32, kind="ExternalInput")
    identd = P.dram("ident", [128, 128], F32, kind="ExternalInput")

    mod_o = P.dram("mod", [2, 6 * D], F32, kind="ExternalOutput")
    hT_o = P.dram("hT", [D, nt], BF16, kind="ExternalOutput")
    p_o = P.dram("p", [nt, IN_COLS], F32, kind="ExternalOutput")
    qd_o = P.dram("qd", [nt, 512], BF16, kind="ExternalOutput")
    kd_o = P.dram("kd", [nt, 512], BF16, kind="ExternalOutput")
    vd_o = P.dram("vd", [nt, 512], BF16, kind="ExternalOutput")
    qm_o = P.dram("qm", [nt, 384], BF16, kind="ExternalOutput")
    kvm_o = P.dram("kvm", [nt, 768], BF16, kind="ExternalOutput")
    krm_o = P.dram("krm", [nt, 32], BF16, kind="ExternalOutput")

    identf = P.sb("identf", [128, 128], F32)
    ident = P.sb("ident", [128, 128], BF16)
    P.dma("sp", identf[:], identd[:, :], reads=[identd], writes=[identf])
    P.op("dve", "tensor_copy", ident[:], identf[:], reads=[identf], writes=[ident])
    epst = P.sb("epst", [128, 1], F32)
    P.op("pool", "memset", epst[:], EPS, writes=[epst])

    ccs = P.sb("ccs", [128, 8, 2], F32)
    P.dma("sp", ccs[:], ccT[:, :].rearrange("(k p) r -> p k r", p=128), reads=[ccT], writes=[ccs])
    ccb = P.sb("ccb", [128, 8, 2], BF16)
    P.op("act", "activation", ccb[:], ccs[:], AF.Silu, reads=[ccs], writes=[ccb])
    modT = P.sb("modT", [128, 32], F32)
    winb = P.sb("winb", [128, 8, IN_COLS], BF16)
    wuqb = P.sb("wuqb", [128, 3, 384], BF16)
    wukvb = P.sb("wukvb", [128, 2, 768], BF16)
    qn_b = P.sb("qn_b", [128, 384], F32)
    kvn_b = P.sb("kvn_b", [128, 256], F32)
    mk = P.mark()
    mod_sb = P.sb("mod_sb", [2, 6 * D], F32)
    for r in range(2):
        P.dma("sp", mod_sb[r:r + 1, :], b_mod[:, :], reads=[b_mod], writes=[mod_sb])
    wst = Rot([P.sb("wst%d" % i, [128, 8, 512], F32) for i in range(2)])
    wbf = Rot([P.sb("wbf%d" % i, [128, 8, 512], BF16) for i in range(2)])
    accm = Rot([P.ps("accm%d" % i, [128, 512], F32) for i in range(2)])
    for nb_ in range(12):
        ws, wb = wst.next(), wbf.next()
        P.dma("sp", ws[:], w_mod[:, nb_ * 512:(nb_ + 1) * 512].rearrange("(k p) n -> p k n", p=128), reads=[w_mod], writes=[ws])
        P.op("pool", "tensor_copy", wb[:], ws[:], reads=[ws], writes=[wb])
        acc = accm.next()
        for k in range(8):
            P.op("pe", "matmul", acc[0:2, :], ccb[:, k, :], wb[:, k, :], start=(k == 0), stop=(k == 7), reads=[ccb, wb], writes=[acc])
        P.op("dve", "tensor_tensor", mod_sb[:, nb_ * 512:(nb_ + 1) * 512], acc[0:2, :], mod_sb[:, nb_ * 512:(nb_ + 1) * 512], ALU.add,
             reads=[acc, mod_sb], writes=[mod_sb])
    P.dma("pool", mod_o[:, :], mod_sb[:], reads=[mod_sb], writes=[mod_o])
    mtp = P.ps("mtp", [128, 512], F32)
    for v in range(2):
        for k in range(8):
            c0 = (v * 8 + k) * 2
            P.op("pe", "transpose", mtp[:, c0:c0 + 2], mod_sb[0:2, v * D + k * 128: v * D + (k + 1) * 128], identf[0:2, 0:2],
                 reads=[mod_sb, identf], writes=[mtp])
    P.op("dve", "tensor_copy", modT[:], mtp[:, 0:32], reads=[mtp], writes=[modT])
    P.op("dve", "tensor_scalar_add", modT[:, 16:32], modT[:, 16:32], 1.0, reads=[modT], writes=[modT])

    HC = IN_COLS // 2
    wstg = Rot([P.sb("wstg%d" % i, [128, HC], F32) for i in range(2)])
    for k in range(8):
        for hh in range(2):
            ws = wstg.next()
            P.dma("sp", ws[:], w_in[k * 128:(k + 1) * 128, hh * HC:(hh + 1) * HC], reads=[w_in], writes=[ws])
            P.op("pool" if hh else "dve", "tensor_copy", winb[:, k, hh * HC:(hh + 1) * HC], ws[:], reads=[ws], writes=[winb])
    ws = wstg.next()
    P.dma("sp", ws[:, 0:1152].rearrange("p (k n) -> p k n", k=3), wuq_d[:, :].rearrange("(k p) n -> p k n", p=128), reads=[wuq_d], writes=[ws])
    P.op("dve", "tensor_copy", wuqb[:], ws[:, 0:1152].rearrange("p (k n) -> p k n", k=3), reads=[ws], writes=[wuqb])
    ws = wstg.next()
    P.dma("sp", ws[:, 0:1536].rearrange("p (k n) -> p k n", k=2), wukv_d[:, :].rearrange("(k p) n -> p k n", p=128), reads=[wukv_d], writes=[ws])
    P.op("dve", "tensor_copy", wukvb[:], ws[:, 0:1536].rearrange("p (k n) -> p k n", k=2), reads=[ws], writes=[wukvb])
    P.dma("sp", qn_b[:], qn_d[:, :].partition_broadcast(128), reads=[qn_d], writes=[qn_b])
    P.dma("sp", kvn_b[:], kvn_d[:, :].partition_broadcast(128), reads=[kvn_d], writes=[kvn_b])

    P.release(mk)
    xts = Rot([P.sb("xt%d" % i, [128, D], F32) for i in range(2)])
    ropes = Rot([P.sb("rp%d" % i, [128, 192], F32) for i in range(2)])
    scrs = Rot([(P.sb("st%d" % i, [128, 12], F32), P.sb("mv%d" % i, [128, 2], F32),
                 P.sb("rstd%d" % i, [128, 1], F32), P.sb("nb%d" % i, [128, 1], F32)) for i in range(2)])
    xns = Rot([P.sb("xn%d" % i, [128, D], BF16) for i in range(2)])
    tps = Rot([P.ps("tp%d" % i, [128, 1024], BF16) for i in range(1)])
    hTs = Rot([P.sb("hTs%d" % i, [128, 8, 128], BF16) for i in range(2)])
    accs = Rot([P.ps("acc%d" % i, [128, 512], F32) for i in range(3)])
    pts = Rot([P.sb("pt%d" % i, [128, IN_COLS], F32) for i in range(2)])
    t1s = Rot([P.sb("t1_%d" % i, [128, 512], F32) for i in range(2)])
    t2s = Rot([P.sb("t2_%d" % i, [128, 512], F32) for i in range(2)])
    obf = Rot([P.sb("obf%d" % i, [128, 768], BF16) for i in range(6)])
    sq_s = Rot([P.sb("sq%d" % i, [128, 384], F32) for i in range(2)])
    col_s = Rot([P.sb("col%d" % i, [128, 1], F32) for i in range(4)])
    cnb = Rot([P.sb("cnb%d" % i, [128, 384], BF16) for i in range(2)])
    cnT = Rot([P.sb("cnT%d" % i, [128, 3, 128], BF16) for i in range(2)])
    tp2 = Rot([P.ps("tp2_%d" % i, [128, 1024], BF16) for i in range(1)])
    qms = Rot([P.sb("qms%d" % i, [128, 384], F32) for i in range(2)])

    def rope_apply(src, n_hm, hd, cos, sin, dst_bf, t1, t2):
        prt = hd // 2
        hf = prt // 2
        nel = n_hm * hd
        x4 = src.rearrange("p (m a h f) -> p m a h f", m=n_hm, a=2, h=2, f=hf)
        t14 = t1[:, 0:nel].rearrange("p (m a h f) -> p m a h f", m=n_hm, a=2, h=2, f=hf)
        t24 = t2[:, 0:nel].rearrange("p (m a h f) -> p m a h f", m=n_hm, a=2, h=2, f=hf)
        c4 = cos.rearrange("p (a h f) -> p a h f", a=2, h=2, f=hf)
        s4 = sin.rearrange("p (a h f) -> p a h f", a=2, h=2, f=hf)
        for m in range(n_hm):
            e = "dve" if m % 2 == 0 else "pool"
            P.op(e, "tensor_tensor", t14[:, m], x4[:, m], c4, ALU.mult, reads=[srcbuf[0], rpt], writes=[t1])
            for h in range(2):
                P.op(e, "tensor_tensor", t24[:, m, :, h, :], x4[:, m, :, 1 - h, :], s4[:, :, h, :], ALU.mult,
                     reads=[srcbuf[0], rpt], writes=[t2])
        P.op("dve", "tensor_tensor", dst_bf, t1[:, 0:nel], t2[:, 0:nel], ALU.add, reads=[t1, t2], writes=[dstbuf[0]])

    srcbuf = [None]
    dstbuf = [None]
    for t in range(ntile):
        r_ = 0 if t < nlat_tiles else 1
        tok = slice(t * 128, (t + 1) * 128)
        xt = xts.next()
        P.dma("sp", xt[:], xin[tok, :], reads=[xin], writes=[xt])
        rpt = ropes.next()
        P.dma("sp", rpt[:], rope[tok, :], reads=[rope], writes=[rpt])
        rstd, nb = ln_stats(P, xt, scrs.next(), epst)
        xn = xns.next()
        P.op("act", "activation", xn[:], xt[:], AF.Identity, bias=nb[:], scale=rstd[:], reads=[xt, nb, rstd], writes=[xn])
        tp = tps.next()
        for k in range(8):
            P.op("pe", "transpose", tp[:, k * 128:(k + 1) * 128], xn[:, k * 128:(k + 1) * 128], ident[:], reads=[xn, ident], writes=[tp])
        hT = hTs.next()
        for k in range(8):
            P.op("act", "activation", hT[:, k, :], tp[:, k * 128:(k + 1) * 128], AF.Identity,
                 bias=modT[:, 2 * k + r_: 2 * k + r_ + 1], scale=modT[:, 16 + 2 * k + r_: 16 + 2 * k + r_ + 1],
                 reads=[tp, modT], writes=[hT])
        P.dma("pool", hT_o[:, tok].rearrange("(k p) t -> p k t", p=128), hT[:], reads=[hT], writes=[hT_o])
        pt = pts.next()
        for nb_ in range(10):
            n0 = nb_ * 512
            nw = min(512, IN_COLS - n0)
            acc = accs.next()
            for k in range(8):
                P.op("pe", "matmul", acc[:, 0:nw], hT[:, k, :], winb[:, k, n0:n0 + nw], start=(k == 0), stop=(k == 7),
                     reads=[hT, winb], writes=[acc])
            if nb_ % 2 == 0:
                P.op("act", "copy", pt[:, n0:n0 + nw], acc[:, 0:nw], reads=[acc], writes=[pt])
            else:
                P.op("dve", "tensor_copy", pt[:, n0:n0 + nw], acc[:, 0:nw], reads=[acc], writes=[pt])
        P.dma("pool", p_o[tok, :], pt[:], reads=[pt], writes=[p_o])
        srcbuf[0] = pt
        for (off, dst) in ((O_DQ, qd_o), (O_DK, kd_o)):
            ob = obf.next()
            dstbuf[0] = ob
            rope_apply(pt[:, off:off + 512], 8, 64, rpt[:, 0:64], rpt[:, 64:128], ob[:, 0:512], t1s.next(), t2s.next())
            P.dma("pool", dst[tok, :], ob[:, 0:512], reads=[ob], writes=[dst])
        ob = obf.next()
        P.op("act", "copy", ob[:, 0:512], pt[:, O_DV:O_DV + 512], reads=[pt], writes=[ob])
        P.dma("pool", vd_o[tok, :], ob[:, 0:512], reads=[ob], writes=[vd_o])
        for (off, width, nrm, wub, nout, kind) in ((O_CQ, 384, qn_b, wuqb, 384, "q"), (O_CKV, 256, kvn_b, wukvb, 768, "kv")):
            sq = sq_s.next()
            ss = col_s.next()
            P.op("pool", "memset", ss[:], 0.0, writes=[ss])
            P.op("act", "activation", sq[:, 0:width], pt[:, off:off + width], AF.Square, accum_out=ss[:], reads=[pt, ss], writes=[sq, ss])
            P.op("act", "activation", ss[:], ss[:], AF.Sqrt, bias=epst[:], scale=1.0 / width, reads=[ss, epst], writes=[ss])
            P.op("dve", "reciprocal", ss[:], ss[:], reads=[ss], writes=[ss])
            cn = cnb.next()
            P.op("dve", "scalar_tensor_tensor", cn[:, 0:width], pt[:, off:off + width], ss[:], nrm[:, 0:width], ALU.mult, ALU.mult,
                 reads=[pt, ss, nrm], writes=[cn])
            kc = width // 128
            tq = tp2.next()
            for k in range(kc):
                P.op("pe", "transpose", tq[:, k * 128:(k + 1) * 128], cn[:, k * 128:(k + 1) * 128], ident[:], reads=[cn, ident], writes=[tq])
            ct = cnT.next()
            P.op("act", "copy", ct[:, 0:kc, :], tq[:, 0:kc * 128].rearrange("p (k t) -> p k t", k=kc), reads=[tq], writes=[ct])
            if kind == "q":
                acc = accs.next()
                for k in range(kc):
                    P.op("pe", "matmul", acc[:, 0:384], ct[:, k, :], wub[:, k, :], start=(k == 0), stop=(k == kc - 1), reads=[ct, wub], writes=[acc])
                qf = qms.next()
                P.op("act", "copy", qf[:], acc[:, 0:384], reads=[acc], writes=[qf])
                ob = obf.next()
                q3 = qf[:].rearrange("p (h d) -> p h d", h=4)
                o3 = ob[:, 0:384].rearrange("p (h d) -> p h d", h=4)
                P.op("pool", "tensor_copy", o3[:, :, 0:64], q3[:, :, 0:64], reads=[qf], writes=[ob])
                t1 = t1s.next()
                t2 = t2s.next()
                qr = sq_s.next()
                P.op("dve", "tensor_copy", qr[:, 0:128].rearrange("p (h d) -> p h d", h=4), q3[:, :, 64:96], reads=[qf], writes=[qr])
                srcbuf[0] = qr
                rb = cnb.next()
                dstbuf[0] = rb
                rope_apply(qr[:, 0:128], 4, 32, rpt[:, 128:160], rpt[:, 160:192], rb[:, 0:128], t1, t2)
                P.op("dve", "tensor_copy", o3[:, :, 64:96], rb[:, 0:128].rearrange("p (h d) -> p h d", h=4), reads=[rb], writes=[ob])
                P.dma("pool", qm_o[tok, :], ob[:, 0:384], reads=[ob], writes=[qm_o])
            else:
                ob = obf.next()
                for (c0, cw) in ((0, 512), (512, 256)):
                    acc = accs.next()
                    for k in range(kc):
                        P.op("pe", "matmul", acc[:, 0:cw], ct[:, k, :], wub[:, k, c0:c0 + cw], start=(k == 0), stop=(k == kc - 1),
                             reads=[ct, wub], writes=[acc])
                    P.op("act", "copy", ob[:, c0:c0 + cw], acc[:, 0:cw], reads=[acc], writes=[ob])
                P.dma("pool", kvm_o[tok, :], ob[:, 0:768], reads=[ob], writes=[kvm_o])
        srcbuf[0] = pt
        ob = obf.next()
        dstbuf[0] = ob
        rope_apply(pt[:, O_KR:O_KR + 32], 1, 32, rpt[:, 128:160], rpt[:, 160:192], ob[:, 0:32], t1s.next(), t2s.next())
        P.dma("pool", krm_o[tok, :], ob[:, 0:32], reads=[ob], writes=[krm_o])
    P.emit()
    return nc


def rope_tables(pos_row, pos_col, is_ctx):
    n = pos_row.shape[0]
    out = np.zeros((n, 192), np.float32)

    def tab(part):
        half = part // 2
        inv = (ROPE_BASE ** (-np.arange(half, dtype=np.float32) * 2.0 / part)).astype(np.float32)
        cs, sn = [], []
        for pos in (pos_row, pos_col):
            ang = pos.astype(np.float32)[:, None] * inv[None, :]
            c, s_ = np.cos(ang).astype(np.float32), np.sin(ang).astype(np.float32)
            cs.append(np.concatenate([c, c], 1))
            sn.append(np.concatenate([-s_, s_], 1))
        return np.concatenate(cs, 1), np.concatenate(sn, 1)

    cD, sD = tab(32)
    cM, sM = tab(16)
    out[:, 0:64], out[:, 64:128], out[:, 128:160], out[:, 160:192] = cD, sD, cM, sM
    out[is_ctx, 0:64] = 1.0
    out[is_ctx, 64:128] = 0.0
    out[is_ctx, 128:160] = 1.0
    out[is_ctx, 160:192] = 0.0
    return out


def build_B1(nq_lat=TL, nq_ctx=CTX, nkt=NKT, nkt_ctx=CTX // 128, which="both"):
    nq = nq_lat + nq_ctx
    nk = nkt * 128
    nc = bass.Bass("TRN2", target_bir_lowering=False)
    P = Prog(nc)
    do_d, do_m = which in ("both", "diff"), which in ("both", "mla")
    QdT = KdT = Vd = QmT = KmT = Vm = a_o = m_o = None
    if do_d:
        QdT = P.dram("QdT", [4, 128, nq], BF16, kind="ExternalInput")
        KdT = P.dram("KdT", [4, 128, nk], BF16, kind="ExternalInput")
        Vd = P.dram("Vd", [4, 128, nkt, 128], BF16, kind="ExternalInput")
        a_o = P.dram("a_out", [nq, 512], BF16, kind="ExternalOutput")
    if do_m:
        QmT = P.dram("QmT", [4, 96, nq], BF16, kind="ExternalInput")
        KmT = P.dram("KmT", [4, 96, nk], BF16, kind="ExternalInput")
        Vm = P.dram("Vm", [4, 128, nkt, 128], BF16, kind="ExternalInput")
        m_o = P.dram("m_out", [nq, 512], BF16, kind="ExternalOutput")
    lam_d = P.dram("diff_lambda", [1, 256], F32, kind="ExternalInput")
    subln_d = P.dram("diff_subln", [1, 128], F32, kind="ExternalInput")
    lamc_d = P.dram("lamc", [1, 2], F32, kind="ExternalInput")

    epst = P.sb("epst", [128, 1], F32)
    P.op("pool", "memset", epst[:], EPS, writes=[epst])
    lam_b = P.sb("lam_b", [128, 256], F32)
    P.dma("sp", lam_b[:], lam_d[:, :].partition_broadcast(128), reads=[lam_d], writes=[lam_b])
    subln_b = P.sb("subln_b", [128, 128], F32)
    P.dma("sp", subln_b[:], subln_d[:, :].partition_broadcast(128), reads=[subln_d], writes=[subln_b])
    lamc_b = P.sb("lamc_b", [128, 2], F32)
    P.dma("sp", lamc_b[:], lamc_d[:, :].partition_broadcast(128), reads=[lamc_d], writes=[lamc_b])
    lprod = P.sb("lprod", [128, 128], F32)
    lsum = P.sb("lsum", [128, 2], F32)
    for i in range(2):
        P.op("dve", "tensor_tensor", lprod[:, i * 64:(i + 1) * 64], lam_b[:, (2 * i) * 64:(2 * i + 1) * 64],
             lam_b[:, (2 * i + 1) * 64:(2 * i + 2) * 64], ALU.mult, reads=[lam_b], writes=[lprod])
        P.op("dve", "reduce_sum", lsum[:, i:i + 1], lprod[:, i * 64:(i + 1) * 64], AX.X, reads=[lprod], writes=[lsum])
    P.op("act", "activation", lsum[:], lsum[:], AF.Exp, reads=[lsum], writes=[lsum])
    nlam = P.sb("nlam", [128, 1], F32)
    P.op("dve", "tensor_tensor", nlam[:], lsum[:, 1:2], lsum[:, 0:1], ALU.subtract, reads=[lsum], writes=[nlam])
    P.op("dve", "tensor_tensor", nlam[:], nlam[:], lamc_b[:, 0:1], ALU.subtract, reads=[nlam, lamc_b], writes=[nlam])
    P.op("dve", "tensor_scalar", subln_b[:], subln_b[:], lamc_b[:, 1:2], None, ALU.mult, reads=[subln_b, lamc_b], writes=[subln_b])

    KTs = Rot([P.sb("KT%d" % i, [128, nk], BF16) for i in range(2)])
    Vs = Rot([P.sb("V%d" % i, [128, nkt, 129], BF16) for i in range(2)])
    for vb in Vs.bufs:
        P.op("pool", "memset", vb[:, :, 128:129], 1.0, writes=[vb])
    QTs = Rot([P.sb("QT%d" % i, [128, nq], BF16) for i in range(2)])
    sps = Rot([P.ps("sp%d" % i, [128, 512], F32) for i in range(4)])
    accs = Rot([P.ps("pacc%d" % i, [128, 2, 512], F32) for i in range(2)])
    pts = Rot([P.sb("pT%d" % i, [128, 512], BF16) for i in range(4)])
    t0s = Rot([P.sb("t0_%d" % i, [128, 4, 128], F32) for i in range(2)])
    o_s = Rot([P.sb("o_%d" % i, [128, 128], F32) for i in range(3)])
    sq_s = Rot([P.sb("sqb%d" % i, [128, 128], F32) for i in range(2)])
    rs = Rot([P.sb("r%d" % i, [128, 1], F32) for i in range(8)])
    obs = Rot([P.sb("ob%d" % i, [128, 128], BF16) for i in range(4)])

    blocks = [(i * 512, 512, 0, nkt) for i in range(nq_lat // 512)]
    if nq_ctx:
        blocks.append((nq_lat, nq_ctx, nkt - nkt_ctx, nkt))

    def load_head(hu):
        diff = hu < 4
        h = hu % 4
        rows = 128 if diff else 96
        KT, V, QT = KTs.next(), Vs.next(), QTs.next()
        ksrc = KdT if diff else KmT
        vsrc = Vd if diff else Vm
        qsrc = QdT if diff else QmT
        nch = 4
        cw = nk // nch
        for c in range(nch):
            P.dma("sp", KT[0:rows, c * cw:(c + 1) * cw], ksrc[h, :, c * cw:(c + 1) * cw], reads=[ksrc], writes=[KT])
        tw = 8
        for t0_ in range(0, nkt, tw):
            t1_ = min(nkt, t0_ + tw)
            P.dma("sp", V[:, t0_:t1_, 0:128], vsrc[h, :, t0_:t1_, :], reads=[vsrc], writes=[V])
        P.dma("sp", QT[0:rows, :], qsrc[h, :, :], reads=[qsrc], writes=[QT])
        return KT, V, QT

    def emit_qk(st):
        if st["load"] is not None:
            st["bufs"].extend(load_head(st["load"]))
        KT, V, QT = st["bufs"]
        sp_, pT = sps.next(), pts.next()
        st["pT"] = pT
        r0, r1, kt, q0, qw = st["r0"], st["r1"], st["kt"], st["q0"], st["qw"]
        P.op("pe", "matmul", sp_[:, 0:qw], KT[r0:r1, kt * 128:(kt + 1) * 128], QT[r0:r1, q0:q0 + qw], start=True, stop=True,
             reads=[KT, QT], writes=[sp_])
        P.op("act", "activation", pT[:, 0:qw], sp_[:, 0:qw], AF.Exp, scale=st["scale"], reads=[sp_], writes=[pT])

    def emit_pv(st):
        KT, V, QT = st["bufs"]
        g = st["grp"]
        if st["first"]:
            g["acc"] = accs.next()
            if g["diff"] and g["mp"] == 0:
                g["blk"]["t0"] = t0s.next()
        acc, pT, kt, nj = g["acc"], st["pT"], st["kt"], g["nj"]
        for j in range(nj):
            P.op("pe", "matmul", acc[:, j // 2, (j % 2) * 256:(j % 2) * 256 + 129], pT[:, j * 128:(j + 1) * 128], V[:, kt, :],
                 start=(st["first"] and j % 2 == 0), stop=st["last"], skip_group_check=True, reads=[pT, V], writes=[acc])
        if not st["last"]:
            return
        diff, mp, h, q0 = g["diff"], g["mp"], g["h"], g["q0"]
        t0 = g["blk"].get("t0")
        for j in range(nj):
            av = acc[:, j // 2, (j % 2) * 256:(j % 2) * 256 + 128]
            sv = acc[:, j // 2, (j % 2) * 256 + 128:(j % 2) * 256 + 129]
            tok = slice(q0 + j * 128, q0 + (j + 1) * 128)
            r = rs.next()
            P.op("dve", "reciprocal", r[:], sv, reads=[acc], writes=[r])
            if diff and mp == 0:
                P.op("dve", "tensor_scalar", t0[:, j, :], av, r[:], None, ALU.mult, reads=[acc, r], writes=[t0])
            elif diff:
                P.op("dve", "tensor_tensor", r[:], r[:], nlam[:], ALU.mult, reads=[r, nlam], writes=[r])
                o = o_s.next()
                P.op("dve", "scalar_tensor_tensor", o[:], av, r[:], t0[:, j, :], ALU.mult, ALU.add, reads=[acc, r, t0], writes=[o])
                sq = sq_s.next()
                ss = rs.next()
                P.op("pool", "tensor_tensor", sq[:], o[:], o[:], ALU.mult, reads=[o], writes=[sq])
                P.op("dve", "reduce_sum", ss[:], sq[:], AX.X, reads=[sq], writes=[ss])
                P.op("act", "activation", ss[:], ss[:], AF.Ln, bias=epst[:], scale=1.0 / 128, reads=[ss, epst], writes=[ss])
                P.op("act", "activation", ss[:], ss[:], AF.Exp, scale=-0.5, reads=[ss], writes=[ss])
                ob = obs.next()
                P.op("dve", "scalar_tensor_tensor", ob[:], o[:], ss[:], subln_b[:], ALU.mult, ALU.mult, reads=[o, ss, subln_b], writes=[ob])
                P.dma("pool", a_o[tok, h * 128:(h + 1) * 128], ob[:], reads=[ob], writes=[a_o])
            else:
                ob = obs.next()
                P.op("dve", "tensor_scalar", ob[:], av, r[:], None, ALU.mult, reads=[acc, r], writes=[ob])
                P.dma("pool", m_o[tok, h * 128:(h + 1) * 128], ob[:], reads=[ob], writes=[m_o])

    steps = []
    for hu in ([0, 1, 2, 3] if do_d else []) + ([4, 5, 6, 7] if do_m else []):
        diff = hu < 4
        h = hu % 4
        scale = 0.125 if diff else 96.0 ** -0.5
        nmap = 2 if diff else 1
        bufs = []
        first_of_head = True
        for (q0, qw, kt0, kt1) in blocks:
            blk = {}
            for mp in range(nmap):
                r0, r1 = (mp * 64, mp * 64 + 64) if diff else (0, 96)
                grp = dict(diff=diff, mp=mp, h=h, q0=q0, nj=qw // 128, blk=blk)
                for kt in range(kt0, kt1):
                    steps.append(dict(load=(hu if first_of_head else None), bufs=bufs, r0=r0, r1=r1, kt=kt, q0=q0, qw=qw, scale=scale,
                                      grp=grp, first=(kt == kt0), last=(kt == kt1 - 1)))
                    first_of_head = False
    DEPTH = 2
    for i, st in enumerate(steps):
        emit_qk(st)
        if i >= DEPTH:
            emit_pv(steps[i - DEPTH])
    for st in steps[max(0, len(steps) - DEPTH):]:
        emit_pv(st)
    P.emit()
    return nc


TPAD = (CTX + 4) + (SEQ + 4)


def build_B2(n_ctx=CTX, n_lat=SEQ):
    T = n_ctx + n_lat
    tpad = (n_ctx + 4) + (n_lat + 4)
    nc = bass.Bass("TRN2", target_bir_lowering=False)
    P = Prog(nc)
    di = lambda n, s, dt=F32: P.dram(n, s, dt, kind="ExternalInput")
    qpre, kpre = di("ml_qpre", [64, tpad]), di("ml_kpre", [64, tpad])
    ml_cw = di("ml_cw", [128, 6])
    ml_v = di("ml_v", [T, 128])
    ml_g = di("ml_g", [T, 2])
    ml_gb = di("ml_gb", [1, 2])
    xpre, bpre, cpre = di("sd_xpre", [128, tpad]), di("sd_bpre", [64, tpad]), di("sd_cpre", [64, tpad])
    sd_cw = di("sd_cw", [128, 3, 6])
    sd_dt = di("sd_dt", [T, 2])
    sd_par = di("sd_par", [1, 4])
    consts = di("consts", [128, 4, 128])
    h_o = P.dram("ml_h", [T, 128], F32, kind="ExternalOutput")
    y_o = P.dram("sd_y", [T, 128], F32, kind="ExternalOutput")
    xs_o = P.dram("sd_xs", [T, 128], F32, kind="ExternalOutput")

    cst = P.sb("cst", [128, 4, 128], F32)
    P.dma("sp", cst[:], consts[:, :, :], reads=[consts], writes=[cst])
    identf, U, LS, ONES = cst[:, 0, :], cst[:, 1, :], cst[:, 2, :], cst[:, 3, :]
    identb = P.sb("identb", [128, 128], BF16)
    P.op("dve", "tensor_copy", identb[:], identf, reads=[cst], writes=[identb])
    one_c = P.sb("one_c", [128, 1], F32)
    P.op("pool", "memset", one_c[:], 1.0, writes=[one_c])
    mlcw = P.sb("mlcw", [128, 6], F32)
    P.dma("sp", mlcw[:], ml_cw[:, :], reads=[ml_cw], writes=[mlcw])
    mlcwk = P.sb("mlcwk", [64, 6], F32)
    P.dma("sp", mlcwk[:], ml_cw[64:128, :], reads=[ml_cw], writes=[mlcwk])
    sdcw = P.sb("sdcw", [128, 3, 6], F32)
    P.dma("sp", sdcw[:], sd_cw[:, :, :], reads=[sd_cw], writes=[sdcw])
    gb_b = P.sb("gb_b", [128, 2], F32)
    P.dma("sp", gb_b[:], ml_gb[:, :].partition_broadcast(128), reads=[ml_gb], writes=[gb_b])
    par_b = P.sb("par_b", [128, 4], F32)
    P.dma("sp", par_b[:], sd_par[:, :].partition_broadcast(128), reads=[sd_par], writes=[par_b])
    ealog = P.sb("ealog", [128, 2], F32)
    P.op("act", "activation", ealog[:], par_b[:, 2:4], AF.Exp, reads=[par_b], writes=[ealog])
    P.op("dve", "tensor_scalar", ealog[:], ealog[:], -1.0, None, ALU.mult, reads=[ealog], writes=[ealog])

    Cst = P.sb("Cst", [64, 129], F32)
    Cbf = P.sb("Cbf", [64, 129], BF16)
    Hst = P.sb("Hst", [64, 2, 64], F32)
    Hbf = P.sb("Hbf", [64, 2, 64], BF16)
    for b_ in (Cst, Cbf, Hst, Hbf):
        P.op("pool", "memset", b_[:], 0.0, writes=[b_])

    pG = P.ps("pG", [128, 512], F32)
    pSeg = Rot([P.ps("pSeg%d" % i, [128, 512], F32) for i in range(2)])
    pSc = P.ps("pSc", [128, 512], F32)
    pIO = P.ps("pIO", [128, 512], F32)
    pTb = P.ps("pTb", [128, 1024], BF16)
    pTf = P.ps("pTf", [128, 512], F32)
    pU = P.ps("pU", [128, 512], F32)

    R2 = lambda name, shape, dt, n=2: Rot([P.sb("%s%d" % (name, i), shape, dt) for i in range(n)])
    qin, kin = R2("qin", [64, 516], F32), R2("kin", [64, 516], F32)
    xin_, bin_, cin_ = R2("xin", [128, 516], F32), R2("bin", [64, 516], F32), R2("cin", [64, 516], F32)
    cacc = R2("cacc", [128, 512], F32, 3)
    qTb, kTb = R2("qTb", [64, 512], BF16), R2("kTb", [64, 512], BF16)
    xTf = R2("xTf", [128, 512], F32)
    BTb, CTb = R2("BTb", [64, 512], BF16), R2("CTb", [64, 512], BF16)
    vin = R2("vin", [128, 4, 128], F32)
    vaug = R2("vaug", [128, 4, 129], BF16)
    for vb in vaug.bufs:
        P.op("pool", "memset", vb[:, :, 128:129], 1.0, writes=[vb])
    gin, dtin = R2("gin", [128, 4, 2], F32), R2("dtin", [128, 4, 2], F32)
    gcol = R2("gcol", [128, 4, 8], F32)
    scol = R2("scol", [128, 4, 2, 6], F32)
    Amat = R2("Amat", [128, 128], F32, 3)
    Dmat = R2("Dmat", [128, 128], F32, 3)
    Wb = R2("Wb", [128, 128], BF16, 3)
    scm = R2("scm", [128, 128], F32)
    intra = R2("intra", [128, 129], F32)
    tot_s = R2("tot_s", [128, 129], F32)
    hout = R2("hout", [128, 128], F32, 3)
    yout = R2("yout", [128, 128], F32, 3)
    xsout = R2("xsout", [128, 128], F32, 3)
    xdt = R2("xdt", [128, 2, 64], BF16)
    kw = R2("kw", [128, 64], BF16)
    Bw = R2("Bw", [128, 2, 64], BF16)
    col1 = R2("col1", [128, 4], F32, 4)

    def conv_silu(eng, src, wt, rows, nw, out_f32=None, out_bf=None, out_scale=None):
        acc = cacc.next()
        P.op(eng, "tensor_scalar", acc[0:rows, 0:nw], src[0:rows, 0:nw], wt[0:rows, 0:1], wt[0:rows, 5:6], ALU.mult, ALU.add,
             reads=[srcb[0], wtb[0]], writes=[acc])
        for j in range(1, 5):
            P.op(eng, "scalar_tensor_tensor", acc[0:rows, 0:nw], src[0:rows, j:j + nw], wt[0:rows, j:j + 1], acc[0:rows, 0:nw], ALU.mult, ALU.add,
                 reads=[srcb[0], wtb[0], acc], writes=[acc])
        if out_f32 is not None:
            P.op("act", "activation", out_f32[0][0:rows, 0:nw], acc[0:rows, 0:nw], AF.Silu, reads=[acc], writes=[out_f32[1]])
            if out_bf is not None:
                P.op("pool", "tensor_copy", out_bf[0][0:rows, 0:nw], out_f32[0][0:rows, 0:nw], reads=[out_f32[1]], writes=[out_bf[1]])
        else:
            if out_scale is None:
                P.op("act", "activation", out_bf[0][0:rows, 0:nw], acc[0:rows, 0:nw], AF.Silu, reads=[acc], writes=[out_bf[1]])
            else:
                P.op("act", "activation", acc[0:rows, 0:nw], acc[0:rows, 0:nw], AF.Silu, reads=[acc], writes=[acc])
                P.op("dve", "tensor_scalar", out_bf[0][0:rows, 0:nw], acc[0:rows, 0:nw], out_scale, None, ALU.mult, reads=[acc], writes=[out_bf[1]])

    srcb = [None]
    wtb = [None]
    blocks = []
    if n_ctx:
        blocks.append((0, 0, n_ctx))
    for i in range(n_lat // 512):
        blocks.append((n_ctx + 4 + i * 512, n_ctx + i * 512, 512))

    for (poff, toff, nw) in blocks:
        nch = nw // 128
        qi, ki, xi, bi, ci = qin.next(), kin.next(), xin_.next(), bin_.next(), cin_.next()
        P.dma("sp", qi[:, 0:nw + 4], qpre[:, poff:poff + nw + 4], reads=[qpre], writes=[qi])
        P.dma("sp", ki[:, 0:nw + 4], kpre[:, poff:poff + nw + 4], reads=[kpre], writes=[ki])
        P.dma("sp", xi[:, 0:nw + 4], xpre[:, poff:poff + nw + 4], reads=[xpre], writes=[xi])
        P.dma("sp", bi[:, 0:nw + 4], bpre[:, poff:poff + nw + 4], reads=[bpre], writes=[bi])
        P.dma("sp", ci[:, 0:nw + 4], cpre[:, poff:poff + nw + 4], reads=[cpre], writes=[ci])
        vi, va, gi, dti = vin.next(), vaug.next(), gin.next(), dtin.next()
        P.dma("sp", vi[:, 0:nch, :], ml_v[toff:toff + nw, :].rearrange("(c p) e -> p c e", p=128), reads=[ml_v], writes=[vi])
        P.dma("sp", gi[:, 0:nch, :], ml_g[toff:toff + nw, :].rearrange("(c p) e -> p c e", p=128), reads=[ml_g], writes=[gi])
        P.dma("sp", dti[:, 0:nch, :], sd_dt[toff:toff + nw, :].rearrange("(c p) e -> p c e", p=128), reads=[sd_dt], writes=[dti])
        P.op("pool", "tensor_copy", va[:, 0:nch, 0:128], vi[:, 0:nch, :], reads=[vi], writes=[va])
        qT, kT, xT, BT, CT = qTb.next(), kTb.next(), xTf.next(), BTb.next(), CTb.next()
        srcb[0], wtb[0] = qi, mlcw
        conv_silu("dve", qi, mlcw, 64, nw, out_bf=(qT, qT))
        srcb[0], wtb[0] = ki, mlcwk
        conv_silu("dve", ki, mlcwk, 64, nw, out_bf=(kT, kT), out_scale=0.125)
        srcb[0], wtb[0] = xi, sdcw
        conv_silu("dve", xi, sdcw[:, 0, :], 128, nw, out_f32=(xT, xT))
        srcb[0] = bi
        conv_silu("dve", bi, sdcw[:, 1, :], 64, nw, out_bf=(BT, BT))
        srcb[0] = ci
        conv_silu("dve", ci, sdcw[:, 2, :], 64, nw, out_bf=(CT, CT))
        gc = gcol.next()
        P.op("dve", "tensor_scalar", gc[:, 0:nch, 0], gi[:, 0:nch, 0], gb_b[:, 0:1], None, ALU.add, reads=[gi, gb_b], writes=[gc])
        P.op("dve", "tensor_scalar", gc[:, 0:nch, 1], gi[:, 0:nch, 1], gb_b[:, 1:2], None, ALU.add, reads=[gi, gb_b], writes=[gc])
        P.op("act", "activation", gc[:, 0:nch, 1], gc[:, 0:nch, 1], AF.Exp, scale=-1.0, reads=[gc], writes=[gc])
        P.op("act", "activation", gc[:, 0:nch, 1], gc[:, 0:nch, 1], AF.Ln, bias=one_c[:], scale=1.0, reads=[gc, one_c], writes=[gc])
        P.op("dve", "tensor_scalar", gc[:, 0:nch, 1], gc[:, 0:nch, 1], -1.0, None, ALU.mult, reads=[gc], writes=[gc])
        sc_ = scol.next()
        for hh in range(2):
            P.op("dve", "tensor_scalar", sc_[:, 0:nch, hh, 0], dti[:, 0:nch, hh], par_b[:, hh:hh + 1], None, ALU.add, reads=[dti, par_b], writes=[sc_])
        P.op("act", "activation", sc_[:, 0:nch, :, 0], sc_[:, 0:nch, :, 0], AF.Exp, reads=[sc_], writes=[sc_])
        P.op("act", "activation", sc_[:, 0:nch, :, 0], sc_[:, 0:nch, :, 0], AF.Ln, bias=one_c[:], scale=1.0, reads=[sc_, one_c], writes=[sc_])
        for hh in range(2):
            P.op("dve", "tensor_scalar", sc_[:, 0:nch, hh, 1], sc_[:, 0:nch, hh, 0], ealog[:, hh:hh + 1], None, ALU.mult, reads=[sc_, ealog], writes=[sc_])
        pk = cacc.next()
        pk3 = pk[:, 0:nch * 3].rearrange("p (c k) -> p c k", k=3)
        P.op("dve", "tensor_copy", pk3[:, :, 0], gc[:, 0:nch, 1], reads=[gc], writes=[pk])
        P.op("dve", "tensor_copy", pk3[:, :, 1:3], sc_[:, 0:nch, :, 1], reads=[sc_], writes=[pk])
        n3 = nch * 3
        P.op("pe", "matmul", pG[:, 0:n3], U, pk[:, 0:n3], start=True, stop=True, reads=[cst, pk], writes=[pG])
        P.op("pe", "matmul", pG[:, 16:16 + n3], ONES, pk[:, 0:n3], start=True, stop=True, reads=[cst, pk], writes=[pG])
        cum3 = pG[:, 0:n3].rearrange("p (c k) -> p c k", k=3)
        tot3 = pG[:, 16:16 + n3].rearrange("p (c k) -> p c k", k=3)
        P.op("dve", "tensor_copy", gc[:, 0:nch, 2], cum3[:, :, 0], reads=[pG], writes=[gc])
        P.op("dve", "tensor_copy", gc[:, 0:nch, 3], tot3[:, :, 0], reads=[pG], writes=[gc])
        P.op("dve", "tensor_copy", sc_[:, 0:nch, :, 2], cum3[:, :, 1:3], reads=[pG], writes=[sc_])
        P.op("dve", "tensor_copy", sc_[:, 0:nch, :, 3], tot3[:, :, 1:3], reads=[pG], writes=[sc_])
        P.op("act", "activation", gc[:, 0:nch, 4], gc[:, 0:nch, 2], AF.Exp, reads=[gc], writes=[gc])
        P.op("dve", "tensor_tensor", gc[:, 0:nch, 5], gc[:, 0:nch, 3], gc[:, 0:nch, 2], ALU.subtract, reads=[gc], writes=[gc])
        P.op("dve", "tensor_tensor", gc[:, 0:nch, 5], gc[:, 0:nch, 5], gc[:, 0:nch, 0], ALU.add, reads=[gc], writes=[gc])
        P.op("act", "activation", gc[:, 0:nch, 5], gc[:, 0:nch, 5], AF.Exp, reads=[gc], writes=[gc])
        P.op("act", "activation", gc[:, 0:nch, 6], gc[:, 0:nch, 3], AF.Exp, reads=[gc], writes=[gc])
        P.op("act", "activation", sc_[:, 0:nch, :, 4], sc_[:, 0:nch, :, 2], AF.Exp, reads=[sc_], writes=[sc_])
        P.op("dve", "tensor_tensor", sc_[:, 0:nch, :, 5], sc_[:, 0:nch, :, 3], sc_[:, 0:nch, :, 2], ALU.subtract, reads=[sc_], writes=[sc_])
        P.op("act", "activation", sc_[:, 0:nch, :, 5], sc_[:, 0:nch, :, 5], AF.Exp, reads=[sc_], writes=[sc_])
        P.op("act", "activation", sc_[:, 0:nch, :, 3], sc_[:, 0:nch, :, 3], AF.Exp, reads=[sc_], writes=[sc_])

        for c in range(nch):
            cs = slice(c * 128, (c + 1) * 128)
            tok = slice(toff + c * 128, toff + (c + 1) * 128)
            A = Amat.next()
            P.op("dve", "tensor_scalar", A[:], LS, gc[:, c, 1:2], None, ALU.mult, reads=[cst, gc], writes=[A])
            seg = pSeg.next()
            P.op("pe", "matmul", seg[:, 0:128], A[:], U, start=True, stop=True, reads=[A, cst], writes=[seg])
            Dm = Dmat.next()
            P.op("act", "activation", Dm[:], seg[:, 0:128], AF.Exp, bias=gc[:, c, 0:1], scale=1.0, reads=[seg, gc], writes=[Dm])
            P.op("pool", "tensor_tensor", Dm[:], Dm[:], U, ALU.mult, reads=[Dm, cst], writes=[Dm])
            P.op("pe", "matmul", pSc[:, 0:128], kT[:, cs], qT[:, cs], start=True, stop=True, reads=[kT, qT], writes=[pSc])
            W = Wb.next()
            P.op("dve", "tensor_tensor", W[:], pSc[:, 0:128], Dm[:], ALU.mult, reads=[pSc, Dm], writes=[W])
            P.op("pe", "matmul", pIO[:, 0:129], W[:], va[:, c, :], start=True, stop=True, reads=[W, va], writes=[pIO])
            P.op("pe", "matmul", pIO[:, 256:385], qT[:, cs], Cbf[:], start=True, stop=True, reads=[qT, Cbf], writes=[pIO])
            it = intra.next()
            P.op("act", "copy", it[:], pIO[:, 0:129], reads=[pIO], writes=[it])
            tt = tot_s.next()
            P.op("dve", "scalar_tensor_tensor", tt[:], pIO[:, 256:385], gc[:, c, 4:5], it[:], ALU.mult, ALU.add, reads=[pIO, gc, it], writes=[tt])
            rr = col1.next()
            P.op("dve", "tensor_scalar", rr[:, 1:2], tt[:, 128:129], -1.0, None, ALU.mult, reads=[tt], writes=[rr])
            P.op("dve", "tensor_tensor", rr[:, 0:1], tt[:, 128:129], rr[:, 1:2], ALU.max, reads=[tt, rr], writes=[rr])
            P.op("dve", "tensor_scalar", rr[:, 0:1], rr[:, 0:1], 1.0, None, ALU.max, reads=[rr], writes=[rr])
            P.op("dve", "reciprocal", rr[:, 0:1], rr[:, 0:1], reads=[rr], writes=[rr])
            ho = hout.next()
            P.op("act", "activation", ho[:], tt[:, 0:128], AF.Copy, scale=rr[:, 0:1], reads=[tt, rr], writes=[ho])
            P.dma("pool", h_o[tok, :], ho[:], reads=[ho], writes=[h_o])
            P.op("pe", "transpose", pTb[:, 0:64], kT[:, cs], identb[0:64, 0:64], reads=[kT, identb], writes=[pTb])
            kw_ = kw.next()
            P.op("dve", "tensor_scalar", kw_[:], pTb[:, 0:64], gc[:, c, 5:6], None, ALU.mult, reads=[pTb, gc], writes=[kw_])
            P.op("pe", "matmul", pU[0:64, 0:129], kw_[:], va[:, c, :], start=True, stop=True, reads=[kw_, va], writes=[pU])
            P.op("dve", "scalar_tensor_tensor", Cst[:], Cst[:], gc[0:64, c, 6:7], pU[0:64, 0:129], ALU.mult, ALU.add, reads=[Cst, gc, pU], writes=[Cst])
            P.op("pool", "tensor_copy", Cbf[:], Cst[:], reads=[Cst], writes=[Cbf])
            P.op("pe", "matmul", pSc[:, 256:384], BT[:, cs], CT[:, cs], start=True, stop=True, reads=[BT, CT], writes=[pSc])
            sm = scm.next()
            P.op("dve", "tensor_tensor", sm[:], pSc[:, 256:384], U, ALU.mult, reads=[pSc, cst], writes=[sm])
            P.op("pe", "transpose", pTf[:, 0:128], xT[:, cs], identf, reads=[xT, cst], writes=[pTf])
            xo = xsout.next()
            P.op("act", "copy", xo[:], pTf[:, 0:128], reads=[pTf], writes=[xo])
            P.dma("pool", xs_o[tok, :], xo[:], reads=[xo], writes=[xs_o])
            xd = xdt.next()
            for hh in range(2):
                P.op("dve", "tensor_scalar", xd[:, hh, :], xo[:, hh * 64:(hh + 1) * 64], sc_[:, c, hh, 0:1], None, ALU.mult, reads=[xo, sc_], writes=[xd])
            P.op("pe", "transpose", pTb[:, 512:576], BT[:, cs], identb[0:64, 0:64], reads=[BT, identb], writes=[pTb])
            bw_ = Bw.next()
            for hh in range(2):
                P.op("dve", "tensor_scalar", bw_[:, hh, :], pTb[:, 512:576], sc_[:, c, hh, 5:6], None, ALU.mult, reads=[pTb, sc_], writes=[bw_])
            yo = yout.next()
            for hh in range(2):
                A = Amat.next()
                P.op("pool", "tensor_scalar", A[:], LS, sc_[:, c, hh, 1:2], None, ALU.mult, reads=[cst, sc_], writes=[A])
                seg = pSeg.next()
                P.op("pe", "matmul", seg[:, 0:128], A[:], U, start=True, stop=True, reads=[A, cst], writes=[seg])
                Dm = Dmat.next()
                P.op("act", "activation", Dm[:], seg[:, 0:128], AF.Exp, reads=[seg], writes=[Dm])
                W = Wb.next()
                P.op("dve", "tensor_tensor", W[:], Dm[:], sm[:], ALU.mult, reads=[Dm, sm], writes=[W])
                P.op("pe", "matmul", pIO[:, 0:64], W[:], xd[:, hh, :], start=True, stop=True, reads=[W, xd], writes=[pIO])
                P.op("pe", "matmul", pIO[:, 256:320], CT[:, cs], Hbf[:, hh, :], start=True, stop=True, reads=[CT, Hbf], writes=[pIO])
                it = intra.next()
                P.op("act", "copy", it[:, 0:64], pIO[:, 0:64], reads=[pIO], writes=[it])
                P.op("dve", "scalar_tensor_tensor", yo[:, hh * 64:(hh + 1) * 64], pIO[:, 256:320], sc_[:, c, hh, 4:5], it[:, 0:64], ALU.mult, ALU.add,
                     reads=[pIO, sc_, it], writes=[yo])
                P.op("pe", "matmul", pU[0:64, 256:320], bw_[:, hh, :], xd[:, hh, :], start=True, stop=True, reads=[bw_, xd], writes=[pU])
                P.op("dve", "scalar_tensor_tensor", Hst[:, hh, :], Hst[:, hh, :], sc_[0:64, c, hh, 3:4], pU[0:64, 256:320], ALU.mult, ALU.add,
                     reads=[Hst, sc_, pU], writes=[Hst])
                P.op("pool", "tensor_copy", Hbf[:, hh, :], Hst[:, hh, :], reads=[Hst], writes=[Hbf])
            P.dma("pool", y_o[tok, :], yo[:], reads=[yo], writes=[y_o])
    P.emit()
    return nc


def cast_to_bf16_dram(P, src, dst, rows, cols, stg, stgb, engs=("dve", "pool")):
    i = 0
    for r0 in range(0, rows, 128):
        for c0 in range(0, cols, 2048):
            cw = min(2048, cols - c0)
            s, sb_ = stg.next(), stgb.next()
            P.dma("sp", s[:, 0:cw], src[r0:r0 + 128, c0:c0 + cw], reads=[src], writes=[s])
            P.op(engs[i % len(engs)], "tensor_copy", sb_[:, 0:cw], s[:, 0:cw], reads=[s], writes=[sb_])
            P.dma("pool", dst[r0:r0 + 128, c0:c0 + cw], sb_[:, 0:cw], reads=[sb_], writes=[dst])
            i += 1


def build_C(n_lat=TL, n_ctx=CTX, dbg=0):
    nt = n_lat + n_ctx
    nc = bass.Bass("TRN2", target_bir_lowering=False)
    P = Prog(nc)
    di = lambda n, s, dt=F32: P.dram(n, s, dt, kind="ExternalInput")
    x_d = di("x", [nt, D])
    hT_d = di("hT", [D, nt], BF16)
    mod_d = di("mod", [2, 6 * D])
    aT_d, mT_d = di("aT", [512, nt], BF16), di("mT", [512, nt], BF16)
    hf_d, hb_d, og_d = di("hf", [nt, 512]), di("hb", [nt, 512]), di("og", [nt, 512])
    yf_d, yb_d, xs_d, z_d = di("yf", [nt, 512]), di("yb", [nt, 512]), di("xs", [nt, 512]), di("z", [nt, 512])
    mlnorm_d, ssdnorm_d, dsk_d = di("ml_norm", [1, 512]), di("ssd_norm", [1, 512]), di("dskip", [1, 512])
    wg_d, bg_d = di("w_gate", [4 * D, D]), di("b_gateT", [128, 32])
    wbr_d, wo_d = di("w_branch", [4 * 512, D]), di("w_o", [D, D])
    ln1g_d, ln1b_d, ln2g_d, ln2b_d = di("ln1_g", [1, D]), di("ln1_b", [1, D]), di("ln2_g", [1, D]), di("ln2_b", [1, D])
    wup_d, bup_d = di("w_up", [D, 4 * D]), di("b_upT", [128, 32])
    wdn_d, bdn_d = di("w_down", [4 * D, D]), di("b_down", [1, D])
    identd = di("ident", [128, 128])
    xo_d = P.dram("x_out", [nt, D], F32, kind="ExternalOutput")
    wg_bf = P.dram("wg_bf", [4 * D, D], BF16)
    wbr_bf = P.dram("wbr_bf", [4 * 512, D], BF16)
    wup_bf = P.dram("wup_bf", [D, 4 * D], BF16)
    wdn_bf = P.dram("wdn_bf", [4 * D, D], BF16)
    x1_d = P.dram("x1_scr", [nt, D], F32)

    identf = P.sb("identf", [128, 128], F32)
    ident = P.sb("ident", [128, 128], BF16)
    P.dma("sp", identf[:], identd[:, :], reads=[identd], writes=[identf])
    P.op("dve", "tensor_copy", ident[:], identf[:], reads=[identf], writes=[ident])
    epst = P.sb("epst", [128, 1], F32)
    P.op("pool", "memset", epst[:], EPS, writes=[epst])
    def bcast(nm, d_ap, w):
        t = P.sb(nm, [128, w], F32)
        P.dma("sp", t[:], d_ap.partition_broadcast(128), reads=[bcsrc.get(nm, mod_d if nm[1] != "m" or nm[2] != "b" else bdn_d)], writes=[t])
        return t
    bcsrc = dict(ln1g=ln1g_d, ln1b=ln1b_d, ln2g=ln2g_d, ln2b=ln2b_d, bdn=bdn_d, mln=mlnorm_d, ssn=ssdnorm_d, dsk=dsk_d)
    modT = P.sb("modT", [128, 32], F32)
    bgT = P.sb("bgT", [128, 32], F32)
    P.dma("sp", bgT[:], bg_d[:, :], reads=[bg_d], writes=[bgT])
    bupT = P.sb("bupT", [128, 32], F32)
    P.dma("sp", bupT[:], bup_d[:, :], reads=[bup_d], writes=[bupT])
    scrs = Rot([(P.sb("st%d" % i, [128, 12], F32), P.sb("mv%d" % i, [128, 2], F32),
                 P.sb("rstd%d" % i, [128, 1], F32), P.sb("nb%d" % i, [128, 1], F32)) for i in range(2)])
    mkm = P.mark()
    mod_sb = P.sb("mod_sb", [2, 2 * D], F32)
    P.dma("sp", mod_sb[:], mod_d[:, 3 * D:5 * D], reads=[mod_d], writes=[mod_sb])
    mtp = P.ps("mtp", [128, 512], F32)
    for v in range(2):
        for k in range(8):
            c0 = (v * 8 + k) * 2
            P.op("pe", "transpose", mtp[:, c0:c0 + 2], mod_sb[0:2, v * D + k * 128: v * D + (k + 1) * 128], identf[0:2, 0:2],
                 reads=[mod_sb, identf], writes=[mtp])
    P.op("dve", "tensor_copy", modT[:], mtp[:, 0:32], reads=[mtp], writes=[modT])
    P.op("dve", "tensor_scalar_add", modT[:, 16:32], modT[:, 16:32], 1.0, reads=[modT], writes=[modT])
    P.release(mkm)
    mk0 = P.mark()
    stg = Rot([P.sb("stg%d" % i, [128, 2048], F32) for i in range(2)])
    stgb = Rot([P.sb("stgb%d" % i, [128, 2048], BF16) for i in range(2)])
    cast_to_bf16_dram(P, wg_d, wg_bf, 4 * D, D, stg, stgb)
    cast_to_bf16_dram(P, wbr_d, wbr_bf, 4 * 512, D, stg, stgb)
    cast_to_bf16_dram(P, wup_d, wup_bf, D, 4 * D, stg, stgb)
    cast_to_bf16_dram(P, wdn_d, wdn_bf, 4 * D, D, stg, stgb)
    wo_bf = P.dram("wo_bf", [D, D], BF16)
    cast_to_bf16_dram(P, wo_d, wo_bf, D, D, stg, stgb)
    P.release(mk0)

    blocks = [(i * 512, 512, 0) for i in range(n_lat // 512)]
    if n_ctx:
        blocks.append((n_lat, n_ctx, 1))
    if dbg == 1:
        P.emit()
        return nc

    mk1 = P.mark()
    wob = P.sb("wob", [128, 8, D], BF16)
    P.dma("sp", wob[:], wo_bf[:, :].rearrange("(c p) n -> p c n", p=128), reads=[wo_bf], writes=[wob])
    gat = [bcast("ga%d" % r, mod_d[r:r + 1, 2 * D:3 * D], D) for r in range(2)]
    bc = {nm: bcast(nm, bcsrc[nm][:, :], w) for nm, w in (("ln1g", D), ("ln1b", D), ("mln", 512), ("ssn", 512), ("dsk", 512))}
    wgs = Rot([P.sb("wg%d" % i, [128, 8, D], BF16) for i in range(2)])
    wbs = Rot([P.sb("wb%d" % i, [128, 4, D], BF16) for i in range(2)])
    hTb = P.sb("hTb", [128, 8, 512], BF16)
    brT = Rot([P.sb("brT%d" % i, [128, 4, 512], BF16) for i in range(2)])
    bTs = P.sb("bTs", [128, 4, 512], BF16)
    sTs = P.sb("sTs", [128, 4, 512], BF16)
    yT = P.sb("yT", [128, 8, 512], F32)
    yTb = P.sb("yTb", [128, 8, 512], BF16)
    Gs = Rot([P.sb("G%d" % i, [128, 512], F32) for i in range(2)])
    prs = Rot([P.sb("pr%d" % i, [128, 512], F32) for i in range(2)])
    tA = Rot([P.sb("tA%d" % i, [128, 512], F32) for i in range(2)])
    tB = Rot([P.sb("tB%d" % i, [128, 512], F32) for i in range(2)])
    tC = Rot([P.sb("tC%d" % i, [128, 512], F32) for i in range(2)])
    tD = Rot([P.sb("tD%d" % i, [128, 512], F32) for i in range(2)])
    tbf = Rot([P.sb("tbf%d" % i, [128, 512], BF16) for i in range(2)])
    cols = Rot([P.sb("cl%d" % i, [128, 4], F32) for i in range(4)])
    xts = Rot([P.sb("xt%d" % i, [128, D], F32) for i in range(2)])
    rts = Rot([P.sb("rt%d" % i, [128, D], F32) for i in range(2)])
    pg = Rot([P.ps("pg%d" % i, [128, 512], F32) for i in range(2)])
    pb = Rot([P.ps("pb%d" % i, [128, 512], F32) for i in range(2)])
    po = Rot([P.ps("po%d" % i, [128, 512], F32) for i in range(2)])
    ptp = P.ps("ptp", [128, 1024], BF16)

    def rms_groups(src, ngrp, gw, wtile, dst, sq):
        cl = cols.next()
        P.op("pool", "memset", cl[:], 0.0, writes=[cl])
        for g in range(ngrp):
            P.op("act", "activation", sq[:, g * gw:(g + 1) * gw], src[:, g * gw:(g + 1) * gw], AF.Square, accum_out=cl[:, g:g + 1],
                 reads=[src, cl], writes=[sq, cl])
        P.op("act", "activation", cl[:, 0:ngrp], cl[:, 0:ngrp], AF.Sqrt, bias=epst[:], scale=1.0 / gw, reads=[cl, epst], writes=[cl])
        P.op("dve", "reciprocal", cl[:, 0:ngrp], cl[:, 0:ngrp], reads=[cl], writes=[cl])
        for g in range(ngrp):
            P.op("dve", "scalar_tensor_tensor", dst[:, g * gw:(g + 1) * gw], src[:, g * gw:(g + 1) * gw], cl[:, g:g + 1],
                 wtile[:, g * gw:(g + 1) * gw], ALU.mult, ALU.mult, reads=[src, cl, wtile], writes=[dst])

    for (t0, bw, r_) in blocks:
        nj = bw // 128
        P.dma("sp", hTb[:, :, 0:bw], hT_d[:, t0:t0 + bw].rearrange("(k p) t -> p k t", p=128), reads=[hT_d], writes=[hTb])
        for j in range(nj):
            tok = slice(t0 + j * 128, t0 + (j + 1) * 128)
            a_, b_, c_, d_ = tA.next(), tB.next(), tC.next(), tD.next()
            P.dma("sp", a_[:], hf_d[tok, :], reads=[hf_d], writes=[a_])
            P.dma("sp", b_[:], hb_d[tok, :], reads=[hb_d], writes=[b_])
            P.dma("sp", c_[:], og_d[tok, :], reads=[og_d], writes=[c_])
            P.op("pool", "tensor_tensor", a_[:], a_[:], b_[:], ALU.add, reads=[a_, b_], writes=[a_])
            rms_groups(a_, 4, 128, bc["mln"], b_, d_)
            P.op("act", "activation", c_[:], c_[:], AF.Sigmoid, reads=[c_], writes=[c_])
            tb_ = tbf.next()
            P.op("dve", "tensor_tensor", tb_[:], b_[:], c_[:], ALU.mult, reads=[b_, c_], writes=[tb_])
            for k in range(4):
                P.op("pe", "transpose", ptp[:, k * 128:(k + 1) * 128], tb_[:, k * 128:(k + 1) * 128], ident[:], reads=[tb_, ident], writes=[ptp])
            P.op("act", "copy", bTs[:, :, j * 128:(j + 1) * 128], ptp[:, 0:512].rearrange("p (k t) -> p k t", k=4), reads=[ptp], writes=[bTs])
            a_, b_, c_, d_ = tA.next(), tB.next(), tC.next(), tD.next()
            P.dma("sp", a_[:], yf_d[tok, :], reads=[yf_d], writes=[a_])
            P.dma("sp", b_[:], yb_d[tok, :], reads=[yb_d], writes=[b_])
            P.dma("sp", c_[:], xs_d[tok, :], reads=[xs_d], writes=[c_])
            P.dma("sp", d_[:], z_d[tok, :], reads=[z_d], writes=[d_])
            P.op("pool", "tensor_tensor", a_[:], a_[:], b_[:], ALU.add, reads=[a_, b_], writes=[a_])
            P.op("pool", "tensor_tensor", c_[:], c_[:], bc["dsk"][:], ALU.mult, reads=[c_, bc["dsk"]], writes=[c_])
            P.op("pool", "tensor_tensor", a_[:], a_[:], c_[:], ALU.add, reads=[a_, c_], writes=[a_])
            P.op("act", "activation", d_[:], d_[:], AF.Silu, reads=[d_], writes=[d_])
            P.op("dve", "tensor_tensor", a_[:], a_[:], d_[:], ALU.mult, reads=[a_, d_], writes=[a_])
            tb_ = tbf.next()
            rms_groups(a_, 2, 256, bc["ssn"], b_, c_)
            P.op("pool", "tensor_copy", tb_[:], b_[:], reads=[b_], writes=[tb_])
            for k in range(4):
                P.op("pe", "transpose", ptp[:, k * 128:(k + 1) * 128], tb_[:, k * 128:(k + 1) * 128], ident[:], reads=[tb_, ident], writes=[ptp])
            P.op("act", "copy", sTs[:, :, j * 128:(j + 1) * 128], ptp[:, 0:512].rearrange("p (k t) -> p k t", k=4), reads=[ptp], writes=[sTs])
        for k in range(4):
            wg, wb = wgs.next(), wbs.next()
            P.dma("sp", wg[:], wg_bf[k * D:(k + 1) * D, :].rearrange("(c p) n -> p c n", p=128), reads=[wg_bf], writes=[wg])
            P.dma("sp", wb[:], wbr_bf[k * 512:(k + 1) * 512, :].rearrange("(c p) n -> p c n", p=128), reads=[wbr_bf], writes=[wb])
            if k == 0 or k == 2:
                br = brT.next()
                src_ = aT_d if k == 0 else mT_d
                P.dma("sp", br[:, :, 0:bw], src_[:, t0:t0 + bw].rearrange("(c p) t -> p c t", p=128), reads=[src_], writes=[br])
            else:
                br = bTs if k == 1 else sTs
            for fc in range(8):
                g_ps = pg.next()
                for kc in range(8):
                    P.op("pe", "matmul", g_ps[:, 0:bw], wg[:, kc, fc * 128:(fc + 1) * 128], hTb[:, kc, 0:bw], start=(kc == 0), stop=(kc == 7),
                         reads=[wg, hTb], writes=[g_ps])
                G = Gs.next()
                P.op("act", "activation", G[:, 0:bw], g_ps[:, 0:bw], AF.Sigmoid, bias=bgT[:, k * 8 + fc:k * 8 + fc + 1], scale=1.0,
                     reads=[g_ps, bgT], writes=[G])
                b_ps = pb.next()
                for kc in range(4):
                    P.op("pe", "matmul", b_ps[:, 0:bw], wb[:, kc, fc * 128:(fc + 1) * 128], br[:, kc, 0:bw], start=(kc == 0), stop=(kc == 3),
                         reads=[wb, br], writes=[b_ps])
                if k == 0:
                    P.op("dve", "tensor_tensor", yT[:, fc, 0:bw], b_ps[:, 0:bw], G[:, 0:bw], ALU.mult, reads=[b_ps, G], writes=[yT])
                else:
                    pr = prs.next()
                    P.op("dve", "tensor_tensor", pr[:, 0:bw], b_ps[:, 0:bw], G[:, 0:bw], ALU.mult, reads=[b_ps, G], writes=[pr])
                    if k < 3:
                        P.op("pool", "tensor_tensor", yT[:, fc, 0:bw], yT[:, fc, 0:bw], pr[:, 0:bw], ALU.add, reads=[yT, pr], writes=[yT])
                    else:
                        P.op("pool", "tensor_tensor", yTb[:, fc, 0:bw], yT[:, fc, 0:bw], pr[:, 0:bw], ALU.add, reads=[yT, pr], writes=[yTb])
        for j in range(nj):
            tok = slice(t0 + j * 128, t0 + (j + 1) * 128)
            xt = xts.next()
            P.dma("sp", xt[:], x_d[tok, :], reads=[x_d], writes=[xt])
            rt = rts.next()
            for nb_ in range(2):
                o_ps = po.next()
                for fc in range(8):
                    P.op("pe", "matmul", o_ps[:], yTb[:, fc, j * 128:(j + 1) * 128], wob[:, fc, nb_ * 512:(nb_ + 1) * 512], start=(fc == 0), stop=(fc == 7),
                         reads=[yTb, wob], writes=[o_ps])
                P.op("dve", "tensor_tensor", rt[:, nb_ * 512:(nb_ + 1) * 512], o_ps[:], gat[r_][:, nb_ * 512:(nb_ + 1) * 512], ALU.mult,
                     reads=[o_ps, gat[r_]], writes=[rt])
            P.op("dve", "scalar_tensor_tensor", rt[:], xt[:], ALPHA, rt[:], ALU.mult, ALU.add, reads=[xt, rt], writes=[rt])
            rstd, nb = ln_stats(P, rt, scrs.next(), epst)
            P.op("act", "activation", xt[:], rt[:], AF.Identity, bias=nb[:], scale=rstd[:], reads=[rt, nb, rstd], writes=[xt])
            P.op("pool", "tensor_tensor", xt[:], xt[:], bc["ln1g"][:], ALU.mult, reads=[xt, bc["ln1g"]], writes=[xt])
            P.op("pool", "tensor_tensor", xt[:], xt[:], bc["ln1b"][:], ALU.add, reads=[xt, bc["ln1b"]], writes=[xt])
            P.dma("pool", x1_d[tok, :], xt[:], reads=[xt], writes=[x1_d])
    P.release(mk1)
    if dbg == 2:
        P.emit()
        return nc
    gmt = [bcast("gm%d" % r, mod_d[r:r + 1, 5 * D:6 * D], D) for r in range(2)]
    bc = {nm: bcast(nm, bcsrc[nm][:, :], w) for nm, w in (("ln2g", D), ("ln2b", D))}
    gmb = [bcast("gmb%d" % r, bdn_d[:, :], D) for r in range(2)]
    for r in range(2):
        P.op("dve", "tensor_tensor", gmb[r][:], gmb[r][:], gmt[r][:], ALU.mult, reads=[gmt[r], gmb[r]], writes=[gmb[r]])
    wus = Rot([P.sb("wu%d" % i, [128, 8, D], BF16) for i in range(2)])
    wds = Rot([P.sb("wd%d" % i, [128, 8, D], BF16) for i in range(2)])
    h2T = P.sb("h2T", [128, 8, 512], BF16)
    u2T = P.sb("u2T", [128, 32, 512], BF16)
    us = Rot([P.sb("u%d" % i, [128, 512], F32) for i in range(2)])
    x1s = [P.sb("x1_%d" % i, [128, D], F32) for i in range(4)]
    xns = Rot([P.sb("xn%d" % i, [128, D], BF16) for i in range(2)])
    rts = Rot([P.sb("rt2_%d" % i, [128, D], F32) for i in range(2)])
    pu = Rot([P.ps("pu%d" % i, [128, 512], F32) for i in range(2)])
    pd = [P.ps("pd%d" % i, [128, 512], F32) for i in range(4)]
    ptp2 = P.ps("ptp2", [128, 1024], BF16)
    for (t0, bw, r_) in blocks:
        nj = bw // 128
        for j in range(nj):
            tok = slice(t0 + j * 128, t0 + (j + 1) * 128)
            xt = x1s[j]
            P.dma("sp", xt[:], x1_d[tok, :], reads=[x1_d], writes=[xt])
            rstd, nb = ln_stats(P, xt, scrs.next(), epst)
            xn = xns.next()
            P.op("act", "activation", xn[:], xt[:], AF.Identity, bias=nb[:], scale=rstd[:], reads=[xt, nb, rstd], writes=[xn])
            for k in range(8):
                P.op("pe", "transpose", ptp2[:, k * 128:(k + 1) * 128], xn[:, k * 128:(k + 1) * 128], ident[:], reads=[xn, ident], writes=[ptp2])
            for k in range(8):
                P.op("act", "activation", h2T[:, k, j * 128:(j + 1) * 128], ptp2[:, k * 128:(k + 1) * 128], AF.Identity,
                     bias=modT[:, 2 * k + r_: 2 * k + r_ + 1], scale=modT[:, 16 + 2 * k + r_: 16 + 2 * k + r_ + 1], reads=[ptp2, modT], writes=[h2T])
        for q in range(4):
            wu = wus.next()
            P.dma("sp", wu[:], wup_bf[:, q * D:(q + 1) * D].rearrange("(c p) n -> p c n", p=128), reads=[wup_bf], writes=[wu])
            for fl in range(8):
                ffc = q * 8 + fl
                u_ps = pu.next()
                for kc in range(8):
                    P.op("pe", "matmul", u_ps[:, 0:bw], wu[:, kc, fl * 128:(fl + 1) * 128], h2T[:, kc, 0:bw], start=(kc == 0), stop=(kc == 7),
                         reads=[wu, h2T], writes=[u_ps])
                u = us.next()
                P.op("act", "activation", u[:, 0:bw], u_ps[:, 0:bw], AF.Relu, bias=bupT[:, ffc:ffc + 1], scale=1.0, reads=[u_ps, bupT], writes=[u])
                P.op("dve" if fl % 2 else "pool", "tensor_tensor", u2T[:, ffc, 0:bw], u[:, 0:bw], u[:, 0:bw], ALU.mult, reads=[u], writes=[u2T])
        for half in range((nj + 1) // 2):
            js = [j for j in (2 * half, 2 * half + 1) if j < nj]
            for q in range(4):
                wd = wds.next()
                P.dma("sp", wd[:], wdn_bf[q * D:(q + 1) * D, :].rearrange("(c p) n -> p c n", p=128), reads=[wdn_bf], writes=[wd])
                for ji, j in enumerate(js):
                    for nb_ in range(2):
                        d_ps = pd[ji * 2 + nb_]
                        for fl in range(8):
                            ffc = q * 8 + fl
                            P.op("pe", "matmul", d_ps[:], u2T[:, ffc, j * 128:(j + 1) * 128], wd[:, fl, nb_ * 512:(nb_ + 1) * 512],
                                 start=(ffc == 0), stop=(ffc == 31), reads=[u2T, wd], writes=[d_ps])
            for ji, j in enumerate(js):
                tok = slice(t0 + j * 128, t0 + (j + 1) * 128)
                xt = x1s[j]
                rt = rts.next()
                for nb_ in range(2):
                    cs_ = slice(nb_ * 512, (nb_ + 1) * 512)
                    P.op("dve", "tensor_tensor", rt[:, cs_], pd[ji * 2 + nb_][:], gmt[r_][:, cs_], ALU.mult, reads=[pd[ji * 2 + nb_], gmt[r_]], writes=[rt])
                P.op("pool", "tensor_tensor", rt[:], rt[:], gmb[r_][:], ALU.add, reads=[rt, gmb[r_]], writes=[rt])
                P.op("dve", "scalar_tensor_tensor", rt[:], xt[:], ALPHA, rt[:], ALU.mult, ALU.add, reads=[xt, rt], writes=[rt])
                rstd, nb = ln_stats(P, rt, scrs.next(), epst)
                P.op("act", "activation", rt[:], rt[:], AF.Identity, bias=nb[:], scale=rstd[:], reads=[rt, nb, rstd], writes=[rt])
                P.op("pool", "tensor_tensor", rt[:], rt[:], bc["ln2g"][:], ALU.mult, reads=[rt, bc["ln2g"]], writes=[rt])
                P.op("pool", "tensor_tensor", rt[:], rt[:], bc["ln2b"][:], ALU.add, reads=[rt, bc["ln2b"]], writes=[rt])
                P.dma("pool", xo_d[tok, :], rt[:], reads=[rt], writes=[xo_d])
    P.emit()
    return nc


_PROGS = {}
_CONSTS = {}
VERBOSE = False


def _prog(name, builder):
    if name not in _PROGS:
        _PROGS[name] = builder()
    return _PROGS[name]


def _run(name, builder, in_maps):
    import time
    t0 = time.time()
    nc = _prog(name, builder)
    res = run_bass_kernel_spmd(nc, in_maps, core_ids=list(range(NCORE)))
    if VERBOSE:
        print("launch", name, "%.1fs" % (time.time() - t0), flush=True)
    return res.results


def _c(a):
    return np.ascontiguousarray(a)


def _pad_streams(a):
    z = np.zeros((a.shape[0], 2), a.dtype)
    return _c(np.concatenate([z, a[:, :CTX], z, z, a[:, CTX:], z], 1))


def _flip_streams(a):
    return np.concatenate([a[:CTX][::-1], a[CTX:][::-1]], 0)


def kernel(x, c, ctx, c_ctx, w_mod, b_mod, w_in, diff_lambda, diff_subln, ml_conv_w, ml_conv_b,
           ml_gate_b, ml_norm, mla_q_norm, mla_kv_norm, mla_w_uq, mla_w_ukv, ssd_conv_w, ssd_conv_b,
           ssd_dt_bias, ssd_a_log, ssd_d, ssd_norm, w_gate, b_gate, w_branch, w_o, ln1_g, ln1_b,
           w_up, b_up, w_down, b_down, ln2_g, ln2_b):
    f32 = np.float32
    A_ = lambda v: np.asarray(v, dtype=f32)
    x = A_(x)[0]
    xc = A_(ctx)[0]
    ccT = _c(np.stack([A_(c)[0], A_(c_ctx)], 1))
    ident = np.eye(128, dtype=f32)
    k_ = np.arange(128)
    consts = _c(np.stack([np.eye(128), (k_[:, None] <= k_[None, :]), (k_[:, None] > k_[None, :]), np.ones((128, 128))], 1).astype(f32))
    ropes = []
    for cid in range(NCORE):
        pos = np.arange(cid * TL, (cid + 1) * TL)
        prow = np.concatenate([pos // 64, np.zeros(CTX, np.int64)])
        pcol = np.concatenate([pos % 64, np.zeros(CTX, np.int64)])
        isctx = np.concatenate([np.zeros(TL, bool), np.ones(CTX, bool)])
        ropes.append(rope_tables(prow, pcol, isctx))
    depth = A_(w_mod).shape[0]
    for l in range(depth):
        W = lambda v: A_(v)[l]
        lam_init = 0.8 - 0.6 * math.exp(-0.3 * l)
        ims = []
        for cid in range(NCORE):
            ims.append(dict(xt=_c(np.concatenate([x[cid * TL:(cid + 1) * TL], xc], 0)), ccT=ccT, w_mod=W(w_mod), b_mod=W(b_mod)[None],
                            w_in=W(w_in), rope=ropes[cid], mla_q_norm=W(mla_q_norm)[None], mla_kv_norm=W(mla_kv_norm)[None],
                            mla_w_uq=W(mla_w_uq), mla_w_ukv=W(mla_w_ukv), ident=ident))
        ra = _run("A", build_A, ims)

        def gather(name):
            return np.concatenate([ra[cid][name][:TL] for cid in range(NCORE)] + [ra[0][name][TL:]], 0)
        kd_all, vd_all = gather("kd"), gather("vd")
        kvm_all, krm_all = gather("kvm").reshape(NK, 4, 192), gather("krm")
        KdT = _c(kd_all.reshape(NK, 4, 128).transpose(1, 2, 0))
        Vd = _c(vd_all.reshape(NKT, 128, 4, 128).transpose(2, 1, 0, 3))
        km = np.concatenate([kvm_all[:, :, :64], np.broadcast_to(krm_all[:, None, :], (NK, 4, 32))], 2)
        KmT = _c(km.transpose(1, 2, 0))
        Vm = _c(kvm_all[:, :, 64:].reshape(NKT, 128, 4, 128).transpose(2, 1, 0, 3))
        lam_in = _c(W(diff_lambda).reshape(1, 256))
        common = dict(diff_lambda=lam_in, diff_subln=W(diff_subln)[None], lamc=np.array([[lam_init, 1.0 - lam_init]], f32))
        ims = [dict(QdT=_c(ra[cid]["qd"].reshape(NT, 4, 128).transpose(1, 2, 0)), KdT=KdT, Vd=Vd,
                    QmT=_c(ra[cid]["qm"].reshape(NT, 4, 96).transpose(1, 2, 0)), KmT=KmT, Vm=Vm, **common) for cid in range(NCORE)]
        rb1 = _run("B1", build_B1, ims)
        del KdT, Vd, KmT, Vm, km, kd_all, vd_all, kvm_all
        p_lat = np.concatenate([ra[cid]["p"][:TL] for cid in range(NCORE)], 0)
        p_seq = np.concatenate([ra[0]["p"][TL:], p_lat], 0)
        del p_lat
        p_dir = [p_seq, _flip_streams(p_seq)]
        mcw, mcb = W(ml_conv_w), W(ml_conv_b)
        scw, scb = W(ssd_conv_w), W(ssd_conv_b)
        ims = []
        for cid in range(NCORE):
            hm, d_ = cid % 4, cid // 4
            ps = p_dir[d_]
            taps = (lambda w: w[::-1]) if d_ else (lambda w: w)
            qf = slice(hm * 64, hm * 64 + 64)
            kf = slice(256 + hm * 64, 256 + hm * 64 + 64)
            ml_cw = np.concatenate([np.concatenate([taps(mcw[:, qf]).T, mcb[qf, None]], 1),
                                    np.concatenate([taps(mcw[:, kf]).T, mcb[kf, None]], 1)], 0).astype(f32)
            gates = ps[:, O_MLG:O_MLG + 16]
            hA = 2 * hm
            g_ = hA // 4
            xf = slice(hA * 64, hA * 64 + 128)
            bf_ = slice(512 + g_ * 64, 512 + g_ * 64 + 64)
            cf = slice(640 + g_ * 64, 640 + g_ * 64 + 64)
            sd_cw = np.zeros((128, 3, 6), f32)
            sd_cw[:, 0, :5], sd_cw[:, 0, 5] = taps(scw[:, xf]).T, scb[xf]
            sd_cw[:64, 1, :5], sd_cw[:64, 1, 5] = taps(scw[:, bf_]).T, scb[bf_]
            sd_cw[:64, 2, :5], sd_cw[:64, 2, 5] = taps(scw[:, cf]).T, scb[cf]
            dtb, alg = W(ssd_dt_bias), W(ssd_a_log)
            ims.append(dict(
                ml_qpre=_pad_streams(ps[:, O_MLQK + hm * 64: O_MLQK + hm * 64 + 64].T),
                ml_kpre=_pad_streams(ps[:, O_MLQK + 256 + hm * 64: O_MLQK + 256 + hm * 64 + 64].T),
                ml_cw=_c(ml_cw), ml_v=_c(ps[:, O_MLV + hm * 128: O_MLV + hm * 128 + 128]),
                ml_g=_c(np.stack([gates[:, (2 * d_) * 4 + hm], gates[:, (2 * d_ + 1) * 4 + hm]], 1)),
                ml_gb=np.array([[W(ml_gate_b)[2 * d_, hm], W(ml_gate_b)[2 * d_ + 1, hm]]], f32),
                sd_xpre=_pad_streams(ps[:, O_SXBC + hA * 64: O_SXBC + hA * 64 + 128].T),
                sd_bpre=_pad_streams(ps[:, O_SXBC + 512 + g_ * 64: O_SXBC + 512 + g_ * 64 + 64].T),
                sd_cpre=_pad_streams(ps[:, O_SXBC + 640 + g_ * 64: O_SXBC + 640 + g_ * 64 + 64].T),
                sd_cw=sd_cw, sd_dt=_c(ps[:, [O_SDT + d_ * 8 + hA, O_SDT + d_ * 8 + hA + 1]]),
                sd_par=np.array([[dtb[d_, hA], dtb[d_, hA + 1], alg[d_, hA], alg[d_, hA + 1]]], f32),
                consts=consts))
        rb2 = _run("B2", build_B2, ims)
        del p_dir, ims
        unf = lambda cid, a: (_flip_streams(a) if cid >= 4 else a)
        hf_all = np.concatenate([rb2[cid]["ml_h"] for cid in range(4)], 1)
        hb_all = np.concatenate([unf(cid, rb2[cid]["ml_h"]) for cid in range(4, 8)], 1)
        yf_all = np.concatenate([rb2[cid]["sd_y"] for cid in range(4)], 1)
        yb_all = np.concatenate([unf(cid, rb2[cid]["sd_y"]) for cid in range(4, 8)], 1)
        xs_all = np.concatenate([rb2[cid]["sd_xs"] for cid in range(4)], 1)
        dsk = _c(np.repeat(W(ssd_d), 64)[None])
        bgT = _c(W(b_gate).reshape(4, 8, 128).transpose(2, 0, 1).reshape(128, 32))
        bupT = _c(W(b_up).reshape(32, 128).T)
        ims = []
        for cid in range(NCORE):
            def tok(a):
                return _c(np.concatenate([a[CTX + cid * TL: CTX + (cid + 1) * TL], a[:CTX]], 0))
            ims.append(dict(
                x=_c(np.concatenate([x[cid * TL:(cid + 1) * TL], xc], 0)), hT=ra[cid]["hT"], mod=ra[cid]["mod"],
                aT=_c(rb1[cid]["a_out"].T), mT=_c(rb1[cid]["m_out"].T),
                hf=tok(hf_all), hb=tok(hb_all), og=_c(ra[cid]["p"][:, O_MLO:O_MLO + 512]),
                yf=tok(yf_all), yb=tok(yb_all), xs=tok(xs_all), z=_c(ra[cid]["p"][:, O_SZ:O_SZ + 512]),
                ml_norm=W(ml_norm)[None], ssd_norm=W(ssd_norm)[None], dskip=dsk,
                w_gate=_c(W(w_gate).reshape(4 * D, D)), b_gateT=bgT, w_branch=_c(W(w_branch).reshape(4 * 512, D)), w_o=W(w_o),
                ln1_g=W(ln1_g)[None], ln1_b=W(ln1_b)[None], ln2_g=W(ln2_g)[None], ln2_b=W(ln2_b)[None],
                w_up=W(w_up), b_upT=bupT, w_down=W(w_down), b_down=W(b_down)[None], ident=ident))
        rc = _run("C", build_C, ims)
        x = np.concatenate([rc[cid]["x_out"][:TL] for cid in range(NCORE)], 0)
        xc = rc[0]["x_out"][TL:]
        del ra, rb1, rb2, rc, ims
    return x[None].astype(f32)
```
